# Optimizing a Trainium2 kernel written in Bass

```python
import math
import jax, jax.numpy as jnp
from jax import lax
import numpy as np

D_MODEL = 1024
BATCH = 2
SEQ = 8192
DEPTH = 2

HEAD_DIM = 64
N_A_LAYERS = DEPTH // 2
N_B_LAYERS = DEPTH - N_A_LAYERS
MAIN_W = (3 * D_MODEL) // 4
DIFF_HEADS = MAIN_W // (2 * HEAD_DIM)
DIFF_W = DIFF_HEADS * 2 * HEAD_DIM
SB_HEADS = MAIN_W // HEAD_DIM
SB_W = SB_HEADS * HEAD_DIM
MEM_HEADS = 4
MEM_LEN = 256
MEM_W = D_MODEL - MAIN_W
MEM_HEAD_DIM = MEM_W // MEM_HEADS
MIX_W = DIFF_W + MEM_W
A_IN = 3 * DIFF_W + MEM_W + MIX_W
B_IN = SB_W + MEM_W + MIX_W
ROPE_DIM = HEAD_DIM // 4
ROPE_THETA = 500000.0
BLOCK = 128
EPS = 1e-6

kernel_name = "yoco_diffattn_stickbreak_memory_hybrid"


def rms_norm(x, g):
    xf = x.astype(jnp.float32)
    y = xf * lax.rsqrt(jnp.mean(xf * xf, axis=-1, keepdims=True) + EPS)
    return (y * g.astype(jnp.float32)).astype(x.dtype)


def rope_cos_sin(positions):
    inv = ROPE_THETA ** (-jnp.arange(0, ROPE_DIM, 2, dtype=jnp.float32) / ROPE_DIM)
    ang = positions.astype(jnp.float32)[..., None] * inv
    return jnp.cos(ang), jnp.sin(ang)


def apply_partial_rope(x, cos, sin):
    half = ROPE_DIM // 2
    c = cos[:, :, None, None, :].astype(x.dtype)
    s = sin[:, :, None, None, :].astype(x.dtype)
    x1 = x[..., :half]
    x2 = x[..., half:ROPE_DIM]
    return jnp.concatenate([x1 * c - x2 * s, x2 * c + x1 * s, x[..., ROPE_DIM:]], axis=-1)


def diff_attention(q, k, v, lam):
    S = q.shape[3]
    scale = HEAD_DIM ** -0.5
    outs = []
    for i in range(S // BLOCK):
        s0, s1 = i * BLOCK, (i + 1) * BLOCK
        sc = jnp.einsum('bchqd,bchkd->bchqk', q[:, :, :, s0:s1], k[:, :, :, :s1]).astype(jnp.float32) * scale
        mask = (s0 + jnp.arange(BLOCK))[:, None] >= jnp.arange(s1)[None, :]
        p = jax.nn.softmax(jnp.where(mask, sc, -jnp.inf), axis=-1)
        w = p[:, 0] - lam * p[:, 1]
        outs.append(jnp.einsum('bhqk,bhkd->bhqd', w.astype(v.dtype), v[:, :, :s1]))
    return jnp.concatenate(outs, axis=2)


def stick_breaking_attention(q, k, v):
    S = q.shape[2]
    scale = HEAD_DIM ** -0.5
    outs = []
    for i in range(S // BLOCK):
        s0, s1 = i * BLOCK, (i + 1) * BLOCK
        z = jnp.einsum('bhqd,bhkd->bhqk', q[:, :, s0:s1], k[:, :, :s1]).astype(jnp.float32) * scale
        mask = (s0 + jnp.arange(BLOCK))[:, None] > jnp.arange(s1)[None, :]
        log_beta = jax.nn.log_sigmoid(z)
        log_1m = jnp.where(mask, jax.nn.log_sigmoid(-z), 0.0)
        suffix = lax.cumsum(log_1m, axis=3, reverse=True) - log_1m
        a = jnp.exp(jnp.where(mask, log_beta + suffix, -jnp.inf))
        outs.append(jnp.einsum('bhqk,bhkd->bhqd', a.astype(v.dtype), v[:, :, :s1]))
    return jnp.concatenate(outs, axis=2)


def memory_attention(qm, mem, l, mem_norm, mem_w_kv, mem_q_norm, mem_k_norm):
    B, S, _ = qm.shape
    M = mem.shape[1]
    kvm = rms_norm(mem, mem_norm[l]) @ mem_w_kv[l]
    km, vm = jnp.split(kvm, 2, axis=-1)
    km = rms_norm(km.reshape(B, M, MEM_HEADS, MEM_HEAD_DIM), mem_k_norm[l])
    vm = vm.reshape(B, M, MEM_HEADS, MEM_HEAD_DIM)
    qh = rms_norm(qm.reshape(B, S, MEM_HEADS, MEM_HEAD_DIM), mem_q_norm[l])
    sc = jnp.einsum('bshd,bmhd->bhsm', qh, km).astype(jnp.float32) * (MEM_HEAD_DIM ** -0.5)
    p = jax.nn.softmax(sc, axis=-1)
    o = jnp.einsum('bhsm,bmhd->bshd', p.astype(vm.dtype), vm)
    return o.reshape(B, S, MEM_W)


def setup_inputs(seed: int = 0) -> dict:
    key = jax.random.key(seed)
    ks = jax.random.split(key, 24)
    nrm = lambda k, shape: jax.random.normal(k, shape, jnp.float32)
    gain = lambda k, shape: 1.0 + 0.02 * nrm(k, shape)
    s_in = D_MODEL ** -0.5
    return {
        "x": nrm(ks[0], (BATCH, SEQ, D_MODEL)),
        "mem": nrm(ks[1], (BATCH, MEM_LEN, D_MODEL)),
        "positions": jnp.broadcast_to(jnp.arange(SEQ, dtype=jnp.int32), (BATCH, SEQ)),
        "a_norm": gain(ks[2], (N_A_LAYERS, D_MODEL)),
        "a_w_in": nrm(ks[3], (N_A_LAYERS, D_MODEL, A_IN)) * s_in,
        "a_q_norm": gain(ks[4], (N_A_LAYERS, HEAD_DIM)),
        "a_k_norm": gain(ks[5], (N_A_LAYERS, HEAD_DIM)),
        "a_lambda_q1": 0.1 * nrm(ks[6], (N_A_LAYERS, HEAD_DIM)),
        "a_lambda_k1": 0.1 * nrm(ks[7], (N_A_LAYERS, HEAD_DIM)),
        "a_lambda_q2": 0.1 * nrm(ks[8], (N_A_LAYERS, HEAD_DIM)),
        "a_lambda_k2": 0.1 * nrm(ks[9], (N_A_LAYERS, HEAD_DIM)),
        "a_subln": gain(ks[10], (N_A_LAYERS, 2 * HEAD_DIM)),
        "a_w_out": nrm(ks[11], (N_A_LAYERS, MIX_W, D_MODEL)) * (MIX_W ** -0.5),
        "kv_norm": gain(ks[12], (D_MODEL,)),
        "w_kv_shared": nrm(ks[13], (D_MODEL, 2 * SB_W)) * s_in,
        "b_norm": gain(ks[14], (N_B_LAYERS, D_MODEL)),
        "b_w_in": nrm(ks[15], (N_B_LAYERS, D_MODEL, B_IN)) * s_in,
        "b_w_out": nrm(ks[16], (N_B_LAYERS, MIX_W, D_MODEL)) * (MIX_W ** -0.5),
        "mem_norm": gain(ks[17], (DEPTH, D_MODEL)),
        "mem_w_kv": nrm(ks[18], (DEPTH, D_MODEL, 2 * MEM_W)) * s_in,
        "mem_q_norm": gain(ks[19], (DEPTH, MEM_HEAD_DIM)),
        "mem_k_norm": gain(ks[20], (DEPTH, MEM_HEAD_DIM)),
    }


def reference(x, mem, positions, a_norm, a_w_in, a_q_norm, a_k_norm, a_lambda_q1, a_lambda_k1,
              a_lambda_q2, a_lambda_k2, a_subln, a_w_out, kv_norm, w_kv_shared, b_norm, b_w_in,
              b_w_out, mem_norm, mem_w_kv, mem_q_norm, mem_k_norm):
    B, S, D = x.shape
    cos, sin = rope_cos_sin(positions)
    k_sb = v_sb = None
    for i in range(DEPTH):
        if i < N_A_LAYERS:
            l = i
            h = rms_norm(x, a_norm[l])
            proj = h @ a_w_in[l]
            q, k, v, qm, gate = jnp.split(
                proj, [DIFF_W, 2 * DIFF_W, 3 * DIFF_W, 3 * DIFF_W + MEM_W], axis=-1)
            q = apply_partial_rope(rms_norm(q.reshape(B, S, 2, DIFF_HEADS, HEAD_DIM), a_q_norm[l]), cos, sin)
            k = apply_partial_rope(rms_norm(k.reshape(B, S, 2, DIFF_HEADS, HEAD_DIM), a_k_norm[l]), cos, sin)
            q = q.transpose(0, 2, 3, 1, 4)
            k = k.transpose(0, 2, 3, 1, 4)
            v = v.reshape(B, S, DIFF_HEADS, 2 * HEAD_DIM).transpose(0, 2, 1, 3)
            lambda_init = 0.8 - 0.6 * math.exp(-0.3 * i)
            lam = (jnp.exp(jnp.sum(a_lambda_q1[l] * a_lambda_k1[l]).astype(jnp.float32))
                   - jnp.exp(jnp.sum(a_lambda_q2[l] * a_lambda_k2[l]).astype(jnp.float32))
                   + lambda_init)
            o = diff_attention(q, k, v, lam).transpose(0, 2, 1, 3)
            o = (rms_norm(o, a_subln[l]) * (1.0 - lambda_init)).reshape(B, S, DIFF_W)
            om = memory_attention(qm, mem, i, mem_norm, mem_w_kv, mem_q_norm, mem_k_norm)
            y = jnp.concatenate([o, om], axis=-1) * jax.nn.silu(gate)
            x = x + y @ a_w_out[l]
        else:
            j = i - N_A_LAYERS
            if i == N_A_LAYERS:
                kv = rms_norm(x, kv_norm) @ w_kv_shared
                k_s, v_s = jnp.split(kv, 2, axis=-1)
                k_sb = k_s.reshape(B, S, SB_HEADS, HEAD_DIM).transpose(0, 2, 1, 3)
                v_sb = v_s.reshape(B, S, SB_HEADS, HEAD_DIM).transpose(0, 2, 1, 3)
            h = rms_norm(x, b_norm[j])
            proj = h @ b_w_in[j]
            q, qm, gate = jnp.split(proj, [SB_W, SB_W + MEM_W], axis=-1)
            q = q.reshape(B, S, SB_HEADS, HEAD_DIM).transpose(0, 2, 1, 3)
            o = stick_breaking_attention(q, k_sb, v_sb).transpose(0, 2, 1, 3).reshape(B, S, SB_W)
            om = memory_attention(qm, mem, i, mem_norm, mem_w_kv, mem_q_norm, mem_k_norm)
            y = jnp.concatenate([o, om], axis=-1) * jax.nn.silu(gate)
            x = x + y @ b_w_out[j]
    return x
```

```python
import math
import numpy as np
import ml_dtypes
import concourse.bass as bass
import concourse.mybir as mybir
from concourse.bass_utils import run_bass_kernel_spmd

F32 = mybir.dt.float32
BF16 = mybir.dt.bfloat16
I32 = mybir.dt.int32
AF = mybir.ActivationFunctionType
ALU = mybir.AluOpType
AX = mybir.AxisListType

NT = 2048
EPS = 1e-6
LAMBDA_INIT0 = 0.8 - 0.6 * math.exp(-0.3 * 0)
SEM_CAP = 20000


class Prog:
    def __init__(self, nc):
        self.nc = nc
        self.ops = []
        self.lw = {}
        self.rd = {}
        self.h = {"pe": nc.tensor, "act": nc.scalar, "dve": nc.vector, "pool": nc.gpsimd, "sp": nc.sync}

    def add(self, eng, fn, r=(), w=(), dma=None):
        idx = len(self.ops)
        deps = set()
        for k in r:
            if k in self.lw:
                deps.add(self.lw[k])
            if k[0] == "ps":
                for x in self.rd.get(k, ()):
                    if self.ops[x][0] != eng:
                        deps.add(x)
        for k in w:
            if k in self.lw:
                deps.add(self.lw[k])
            for x in self.rd.get(k, ()):
                deps.add(x)
        for k in w:
            self.lw[k] = idx
            self.rd[k] = []
        for k in r:
            self.rd.setdefault(k, []).append(idx)
        deps.discard(idx)
        self.ops.append((eng, fn, deps, dma))
        return idx

    def op(self, eng, meth, *args, r=(), w=(), dma=None, **kw):
        return self.add(eng, (meth, args, kw), r, w, dma)

    def emit(self, final_groups=()):
        nc = self.nc
        ops = self.ops
        n = len(ops)
        sig = [False] * n
        for (eng, fn, deps, dma) in ops:
            for d in deps:
                p = ops[d]
                if p[3] is not None:
                    continue
                if p[0] == "pe" and eng == "pe" and dma is None:
                    continue
                sig[d] = True
        import os as _os
        if int(_os.environ.get("KLIMIT", "0")):
            sig = [True] * n
        sems = {}

        def getsem(name):
            if name not in sems:
                sems[name] = nc.semaphore(name).__enter__()
            return sems[name]

        ecount = {}
        sigval = [None] * n
        gcount = {}
        waited = {}
        import os
        limit = int(os.environ.get("KLIMIT", "0")) or n
        for i, (eng, fn, deps, dma) in enumerate(ops):
            if i >= limit:
                break
            E = self.h[eng]
            needs = {}
            for d in deps:
                p = ops[d]
                if p[3] is not None:
                    key = "g_" + p[3]
                    val = gcount[p[3]]
                else:
                    if p[0] == "pe" and eng == "pe" and dma is None:
                        continue
                    key, val = sigval[d]
                if needs.get(key, 0) < val:
                    needs[key] = val
            for key, val in needs.items():
                if waited.get((eng, key), 0) < val:
                    E.wait_ge(getsem(key), val)
                    waited[(eng, key)] = val
            meth, args, kw = fn
            ins = getattr(E, meth)(*args, **kw)
            if dma is not None:
                gcount[dma] = gcount.get(dma, 0) + 16
                ins.then_inc(getsem("g_" + dma), 16)
            elif sig[i]:
                c = ecount.get(eng, 0)
                sname = "e_%s_%d" % (eng, c // SEM_CAP)
                v = c % SEM_CAP + 1
                ecount[eng] = c + 1
                ins.then_inc(getsem(sname), 1)
                sigval[i] = (sname, v)
        sp = self.h["sp"]
        if limit < n:
            for g, v in gcount.items():
                sp.wait_ge(getsem("g_" + g), v)
            for eng_, c_ in ecount.items():
                if c_ > 0:
                    sp.wait_ge(getsem("e_%s_%d" % (eng_, (c_ - 1) // SEM_CAP)), (c_ - 1) % SEM_CAP + 1)
            return
        for g in final_groups:
            sp.wait_ge(getsem("g_" + g), gcount[g])


def _blocks(c):
    return [16 * j + 4 * s + c for j in range(4) for s in range(4)]


def build(phases, fused):
    nc = bass.Bass("TRN2", target_bir_lowering=False, dynamic_dma_scratch_size=2048)
    P = Prog(nc)
    ins_names = []
    outs_names = []

    def din(name, shape, dt):
        ins_names.append(name)
        return nc.dram_tensor("d_" + name, list(shape), dt, kind="ExternalInput").ap()

    def dout(name, shape, dt):
        outs_names.append(name)
        return nc.dram_tensor("d_" + name, list(shape), dt, kind="ExternalOutput").ap()

    def dint(name, shape, dt):
        return nc.dram_tensor("d_" + name, list(shape), dt, kind="Internal").ap()

    def sb(name, shape, dt):
        return nc.sbuf_tensor(name, list(shape), dt).__enter__()

    ident = sb("ident", [128, 128], BF16)
    tri = sb("tri", [128, 128], BF16)
    ones = sb("ones", [128, 128], BF16)
    zer = sb("zer", [128, 512], BF16)
    cst32 = sb("cst32", [128, 2, 128], F32)
    mk = sb("mk", [128, 2, 4, 128], F32)
    gains = sb("gains", [128, 5, 8], F32)
    hg = sb("hg", [128, 6, 64], F32)
    lamv = sb("lamv", [128, 4, 64], F32)
    sml = sb("sml", [128, 16], F32)
    invf = sb("invf", [128, 8], F32)
    cosT = sb("cosT", [128, 16, 8], F32)
    sinT = sb("sinT", [128, 16, 8], F32)
    hy = sb("hy", [128, 8, 2048], BF16)
    QT = sb("QT", [128, 6, 2048], BF16)
    gateT = sb("gateT", [128, 8, 2048], BF16)
    qmT = sb("qmT", [128, 2, 2048], BF16)
    BIG = sb("BIG", [128, 24576], BF16)
    xt = [sb("xt%d" % i, [128, 1024], F32) for i in range(2)]
    xnb = [sb("xnb%d" % i, [128, 1024], BF16) for i in range(2)]
    w32all = sb("w32all", [128, 6, 512], F32)
    w32 = [w32all[:, i, :] for i in range(6)]
    sq = w32all[:, 0:2, :].rearrange("p a b -> p (a b)")
    SQK = [("w32", a_, hb_) for a_ in range(2) for hb_ in range(8)]
    stg = [sb("stg%d" % i, [128, 512], BF16) for i in range(2)]
    tst = [sb("tst%d" % i, [128, 4, 128], BF16) for i in range(2)]
    vst = [sb("vst%d" % i, [128, 768], BF16) for i in range(2)]
    s8 = [sb("s8_%d" % i, [128, 4, 8], F32) for i in range(2)]
    kmpad = sb("kmpad", [128, 2, 4, 256], BF16)
    vmpad = sb("vmpad", [128, 2, 4, 2, 128], BF16)
    if 1 in phases and not fused:
        rp = [sb("rp%d" % i, [128, 3, 8, 16], F32) for i in range(1)]
        memT = sb("memT", [128, 8, 256], BF16)
        posi = sb("posi", [128, 16], I32)
        rt = [sb("rt%d" % i, [128, 16, 8], F32) for i in range(3)]
        rti = sb("rti", [128, 16, 8], I32)
        pb = ab = acc = qpad = None
    else:
        pb = [sb("pb%d" % i, [128, 2, 512], BF16) for i in range(2)]
        ab = [sb("ab%d" % i, [128, 2, 512], BF16) for i in range(2)]
        acc = [sb("acc%d" % i, [128, 2, 512], BF16) for i in range(3)]
        qpad = [sb("qpad%d" % i, [128, 2, 512], BF16) for i in range(2)]
        accl = sb("accl", [128, 2, 512], F32)
        ones32 = sb("ones32", [128, 128], F32)
        rp = None

    KTs = [BIG[:, i * 8192:(i + 1) * 8192] for i in range(2)]
    Vs = [BIG[:, 16384:24576].rearrange("p (t d) -> p t d", d=128) for i in range(2)]
    Wst = [BIG[:, i * 8192:(i + 1) * 8192].bitcast(F32).rearrange("p (k n) -> p k n", k=8) for i in range(2)]
    Wb = [BIG[:, 16384 + i * 4096:16384 + (i + 1) * 4096].rearrange("p (k n) -> p k n", k=8) for i in range(2)]

    PSA = nc.psum_tensor("psall", [128, 8, 512], F32).__enter__()
    PS = [PSA[:, i, :] for i in range(8)]

    def psbf(i):
        return PS[i].bitcast(BF16)

    x_d = din("x", [NT, 1024], F32) if (1 in phases or 2 in phases) else None
    cst_d = din("cst", [128, 2, 128], F32)
    if 1 in phases:
        pos_d = din("pos", [128, 16], I32)
        invf_d = din("invf", [128, 8], F32)
        mem_d = din("mem", [256, 1024], F32)
        w_ain = din("w_ain", [1024, 3584], F32)
        w_memkv = din("w_memkv", [2, 1024, 512], F32)
    gains_d = din("gains", [128, 5, 8], F32)
    hg_d = din("hg", [128, 6, 64], F32)
    mk_d = din("mk", [128, 2, 4, 128], F32)
    if 2 in phases:
        lamv_d = din("lamv", [128, 4, 64], F32)
        subln_d = din("subln", [128, 1], F32)
        w_aout = din("w_aout", [1024, 1024], F32)
        w_kv = din("w_kv", [1024, 1536], F32)
        w_bin = din("w_bin", [1024, 2048], F32)
    if 3 in phases:
        w_bout = din("w_bout", [1024, 1024], F32)

    def exch(layer):
        if fused:
            own_k = dint("kown%d" % layer, [6, 128, 2048], BF16)
            own_v = dint("vown%d" % layer, [6, 128, 16, 128], BF16)
            all_k = dint("kall%d" % layer, [4, 6, 128, 2048], BF16)
            all_v = dint("vall%d" % layer, [4, 6, 128, 16, 128], BF16)
            return own_k, own_v, all_k, all_v
        own_k = own_v = all_k = all_v = None
        prod = 1 if layer == 0 else 2
        cons = 2 if layer == 0 else 3
        if prod in phases:
            own_k = dout("kown%d" % layer, [6, 128, 2048], BF16)
            own_v = dout("vown%d" % layer, [6, 128, 16, 128], BF16)
        if cons in phases:
            all_k = din("kall%d" % layer, [4, 6, 128, 2048], BF16)
            all_v = din("vall%d" % layer, [4, 6, 128, 16, 128], BF16)
        return own_k, own_v, all_k, all_v

    ex0 = exch(0)
    ex1 = exch(1)
    if fused:
        x1_d = dint("x1", [NT, 1024], F32)
    else:
        x1_d = None
        if 2 in phases:
            x1_d = dout("x1", [NT, 1024], F32)
        if 3 in phases:
            x1_d = din("x1", [NT, 1024], F32)
    out_d = dout("out", [NT, 1024], F32) if 3 in phases else None

    states = {
        "QT": (QT, [128, 6, 2048], BF16, [("QT", p_, t_) for p_ in range(6) for t_ in range(16)]),
        "gateT": (gateT, [128, 8, 2048], BF16, [("gateT", p_, j_) for p_ in range(8) for j_ in range(4)]),
        "qmT": (qmT, [128, 2, 2048], BF16, [("qmT", p_, t_) for p_ in range(2) for t_ in range(16)]),
        "kmpad": (kmpad, [128, 2, 4, 256], BF16, [("kmpad",)]),
        "vmpad": (vmpad, [128, 2, 4, 2, 128], BF16, [("vmpad",)]),
    }

    def load_state(name):
        t, shape, dt, keys = states[name]
        d = din("st_" + name, shape, dt)
        P.op("sp", "dma_start", out=t[:], in_=d, w=keys, dma="st_" + name)

    def store_state(name):
        t, shape, dt, keys = states[name]
        d = dout("so_" + name, shape, dt)
        P.op("sp", "dma_start", out=d, in_=t[:], r=keys, dma="so")

    P.op("sp", "dma_start", out=cst32[:], in_=cst_d, w=[("cst32",)], dma="c0")
    P.op("sp", "dma_start", out=gains[:], in_=gains_d, w=[("gains",)], dma="c0")
    P.op("sp", "dma_start", out=hg[:], in_=hg_d, w=[("hg",)], dma="c0")
    P.op("sp", "dma_start", out=mk[:], in_=mk_d, w=[("mk",)], dma="c0")
    P.op("dve", "tensor_copy", ident[:], cst32[:, 0, :], r=[("cst32",)], w=[("ident",)])
    P.op("dve", "tensor_copy", tri[:], cst32[:, 1, :], r=[("cst32",)], w=[("tri",)])
    P.op("dve", "memset", ones[:], 1.0, w=[("ones",)])
    P.op("dve", "memset", zer[:], 0.0, w=[("zer",)])
    if not (1 in phases and not fused):
        P.op("dve", "memset", ones32[:], 1.0, w=[("ones32",)])

    cnt = [0]
    tc_cnt = [0]

    def uid():
        cnt[0] += 1
        return cnt[0]

    def load_weights(w_ap, col0, ncols, gidx, slot):
        src = w_ap[:, col0:col0 + ncols].rearrange("(k p) n -> p k n", p=128)
        P.op("pool", "dma_start", out=Wst[slot][:, :, 0:ncols], in_=src,
              w=[("KTs", slot)], dma="w%d" % slot)
        for kc in range(8):
            if kc % 2 == 0:
                P.op("act", "activation", out=Wb[slot][:, kc, 0:ncols], in_=Wst[slot][:, kc, 0:ncols],
                                                           func=AF.Copy, scale=gains[:, gidx, kc:kc + 1],
                      r=[("KTs", slot), ("gains",)], w=[("Wb", slot, kc), ("VsW", slot)])
            else:
                P.op("dve", "tensor_scalar", Wb[slot][:, kc, 0:ncols], Wst[slot][:, kc, 0:ncols],
                                                              gains[:, gidx, kc:kc + 1], None, ALU.mult,
                      r=[("KTs", slot), ("gains",)], w=[("Wb", slot, kc), ("VsW", slot)])

    def wb_keys(slot):
        return [("Wb", slot, kc) for kc in range(8)] + [("Vs", slot)]

    def wb_wkeys(slot):
        return [("Vs", slot)]

    def rstd_from_ss(out_ap, ss_ap, n, rkeys, wkeys):
        t = uid()
        P.op("act", "activation", out=out_ap, in_=ss_ap, func=AF.Ln, scale=1.0 / n, bias=EPS,
              r=rkeys, w=wkeys)
        P.op("act", "activation", out=out_ap, in_=out_ap, func=AF.Exp, scale=-0.5,
              r=wkeys, w=wkeys)

    def norm_tile_to_T(src_ap_fn, src_keys, slot, dstT, dkey, tt, ncols_tok=128, width=2048):
        xs = xt[slot]
        P.op("act", "activation", out=sq[:], in_=xs[:], func=AF.Square, r=[("xt", slot)], w=SQK)
        ssk = ("ssx", slot)
        P.op("dve", "tensor_reduce", s8[slot][:, 0, 0:1], sq[:], AX.X, ALU.add, r=SQK, w=[ssk])
        rstd_from_ss(s8[slot][:, 0, 1:2], s8[slot][:, 0, 0:1], 1024.0, [ssk], [("rsx", slot)])
        P.op("dve", "tensor_scalar", xnb[slot][:], xs[:], s8[slot][:, 0, 1:2], None, ALU.mult,
              r=[("xt", slot), ("rsx", slot)], w=[("xnb", slot)])
        for half in range(2):
            bank = 6 + half
            for q4 in range(4):
                kc = half * 4 + q4
                P.op("pe", "transpose", psbf(bank)[:, q4 * 128:(q4 + 1) * 128], xnb[slot][:, kc * 128:(kc + 1) * 128], ident[:],
                    r=[("xnb", slot), ("ident",)], w=[("ps", bank)])
            eng = "act" if half == 0 else "dve"
            dst = dstT[:, half * 4:half * 4 + 4, tt * ncols_tok:(tt + 1) * ncols_tok]
            srcp = psbf(bank)[:, 0:512].rearrange("p (a b) -> p a b", b=128)
            if eng == "act":
                P.op("act", "activation", out=dst, in_=srcp, func=AF.Copy,
                      r=[("ps", bank)], w=[(dkey, kc_, tt) for kc_ in range(half * 4, half * 4 + 4)])
            else:
                P.op("dve", "tensor_copy", dst, srcp,
                      r=[("ps", bank)], w=[(dkey, kc_, tt) for kc_ in range(half * 4, half * 4 + 4)])

    def headnorm(bank, slot, hb0, nhb, gain_idx, out_stg, rope, tt):
        c0, c1 = hb0 * 64, (hb0 + nhb) * 64
        ps3 = PS[bank][:, c0:c1].rearrange("p (h d) -> p h d", d=64)
        sq3 = w32[0][:, c0:c1]
        hbs = range(hb0, hb0 + nhb)
        kq = [("w32", 0, hb) for hb in hbs]
        P.op("act", "activation", out=sq3, in_=PS[bank][:, c0:c1], func=AF.Square, r=[("ps", bank)], w=kq)
        kss = [("s8", slot, hb) for hb in hbs]
        ssap = s8[slot][:, 1, hb0:hb0 + nhb]
        rsap = s8[slot][:, 2, hb0:hb0 + nhb]
        P.op("dve", "tensor_reduce", ssap, sq3.rearrange("p (h d) -> p h d", d=64), AX.X, ALU.add,
              r=kq, w=kss)
        krs = [("s8r", slot, hb) for hb in hbs]
        rstd_from_ss(rsap, ssap, 64.0, kss, krs)
        xn3 = w32[1][:, c0:c1].rearrange("p (h d) -> p h d", d=64)
        kxn = [("w32", 1, hb) for hb in hbs]
        P.op("dve", "tensor_tensor", xn3, ps3, rsap.unsqueeze(2).broadcast_to([128, nhb, 64]), ALU.mult,
              r=[("ps", bank)] + krs, w=kxn)
        o3 = out_stg[:, c0:c1].rearrange("p (h d) -> p h d", d=64)
        g3 = hg[:, gain_idx, :].unsqueeze(1).broadcast_to([128, nhb, 64])
        kst = [("stg", slot, hb) for hb in hbs]
        P.op("dve", "tensor_tensor", o3, xn3, g3, ALU.mult, r=kxn + [("hg",)], w=kst)
        if rope:
            R = rp[0]
            kr = ("rp",)
            xg = R[:, 0, hb0:hb0 + nhb, :]
            g16 = hg[:, gain_idx, 0:16].unsqueeze(1).broadcast_to([128, nhb, 16])
            P.op("dve", "tensor_tensor", xg, xn3[:, :, 0:16], g16, ALU.mult, r=kxn + [("hg",)], w=[kr])
            cs = cosT[:, tt, :].unsqueeze(1).broadcast_to([128, nhb, 8])
            sn = sinT[:, tt, :].unsqueeze(1).broadcast_to([128, nhb, 8])
            x1 = xg[:, :, 0:8]
            x2 = xg[:, :, 8:16]
            t1 = R[:, 1, hb0:hb0 + nhb, 0:8]
            t2 = R[:, 1, hb0:hb0 + nhb, 8:16]
            t3 = R[:, 2, hb0:hb0 + nhb, 0:8]
            t4 = R[:, 2, hb0:hb0 + nhb, 8:16]
            P.op("dve", "tensor_tensor", t1, x1, cs, ALU.mult, r=[kr, ("rope",)], w=[("rp1",)])
            P.op("dve", "tensor_tensor", t2, x2, sn, ALU.mult, r=[kr, ("rope",)], w=[("rp2",)])
            P.op("dve", "tensor_tensor", t3, x2, cs, ALU.mult, r=[kr, ("rope",)], w=[("rp3",)])
            P.op("dve", "tensor_tensor", t4, x1, sn, ALU.mult, r=[kr, ("rope",)], w=[("rp4",)])
            P.op("dve", "tensor_tensor", o3[:, :, 0:8], t1, t2, ALU.subtract,
                  r=[("rp1",), ("rp2",)], w=kst)
            P.op("dve", "tensor_tensor", o3[:, :, 8:16], t3, t4, ALU.add,
                  r=[("rp3",), ("rp4",)], w=kst)
        return kst

    def transpose_chunks(slot, src_stg, chunk_list, src_keys, tbank):
        for i, (ci, dst, dk) in enumerate(chunk_list):
            P.op("pe", "transpose", psbf(tbank)[:, i * 128:(i + 1) * 128],
                                                          src_stg[:, ci * 128:(ci + 1) * 128], ident[:],
                  r=list(src_keys) + [("ident",)], w=[("ps", tbank)])
        tc_cnt[0] += 1
        for i, (ci, dst, dk) in enumerate(chunk_list):
            eng = "act" if tc_cnt[0] % 2 == 0 else "dve"
            if eng == "act":
                P.op("act", "activation", out=dst, in_=psbf(tbank)[:, i * 128:(i + 1) * 128], func=AF.Copy,
                      r=[("ps", tbank)], w=dk)
            else:
                P.op("dve", "tensor_copy", dst, psbf(tbank)[:, i * 128:(i + 1) * 128],
                      r=[("ps", tbank)], w=dk)

    def proj_tok(slot_w, bank, tt, ncols, srcT, skey):
        for kc in range(8):
            P.op("pe", "matmul", PS[bank][:, 0:ncols], srcT[:, kc, tt * 128:(tt + 1) * 128],
                                                  Wb[slot_w][:, kc, 0:ncols], start=(kc == 0), stop=(kc == 7),
                  r=[(skey, kc, tt), ("Wb", slot_w, kc), ("VsW", slot_w)], w=[("ps", bank)])

    def proj_feat(slot_w, bank, chunk, j, srcT, skey):
        for kc in range(8):
            P.op("pe", "matmul", PS[bank][:, :], Wb[slot_w][:, kc, chunk * 128:(chunk + 1) * 128],
                                                  srcT[:, kc, j * 512:(j + 1) * 512], start=(kc == 0), stop=(kc == 7),
                  r=[(skey, kc, t_) for t_ in range(4 * j, 4 * j + 4)] + [("Wb", slot_w, kc), ("VsW", slot_w)], w=[("ps", bank)])

    def silu_evac(bank, dst, dkeys, wi):
        e = w32[wi][:, :]
        ke = ("w32", wi)
        P.op("act", "activation", out=e, in_=PS[bank][:, :], func=AF.Exp, scale=-1.0, r=[("ps", bank)], w=[ke])
        P.op("dve", "tensor_scalar", e, e, 1.0, None, ALU.add, r=[ke], w=[ke])
        P.op("dve", "reciprocal", e, e, r=[ke], w=[ke])
        P.op("dve", "tensor_tensor", dst, PS[bank][:, :], e, ALU.mult, r=[ke, ("ps", bank)], w=dkeys)

    def phase1():
        kown, vown = ex0[0], ex0[1]
        P.op("sp", "dma_start", out=posi[:], in_=pos_d, w=[("posi",)], dma="c1")
        P.op("sp", "dma_start", out=invf[:], in_=invf_d, w=[("invf",)], dma="c1")
        pf = sml
        posf = rt[2][:, :, 0]
        P.op("dve", "tensor_copy", posf, posi[:], r=[("posi",)], w=[("posf",)])
        ang = rt[0]
        P.op("dve", "tensor_tensor", ang[:], posf.unsqueeze(2).broadcast_to([128, 16, 8]),
                                               invf[:].unsqueeze(1).broadcast_to([128, 16, 8]), ALU.mult,
              r=[("posf",), ("invf",)], w=[("ang",)])
        for which, shift, dst in (("s", 0.0, sinT), ("c", math.pi / 2, cosT)):
            a2 = rt[1]
            k2 = ("a2",)
            P.op("dve", "tensor_scalar", a2[:], ang[:], shift, None, ALU.add, r=[("ang",)], w=[k2])
            kfl = rt[2]
            P.op("dve", "tensor_scalar", kfl[:], a2[:], 1.0 / (2 * math.pi), None, ALU.mult, r=[k2], w=[("kfl",), ("posf",)])
            P.op("dve", "tensor_copy", rti[:], kfl[:], r=[("kfl",)], w=[("rti",)])
            P.op("dve", "tensor_copy", kfl[:], rti[:], r=[("rti",)], w=[("kfl",)])
            P.op("dve", "scalar_tensor_tensor", a2[:], kfl[:], -2 * math.pi, a2[:], ALU.mult, ALU.add,
                  r=[("kfl",), k2], w=[k2])
            P.op("dve", "tensor_scalar", a2[:], a2[:], 3.1415925, -3.1415925, ALU.min, ALU.max, r=[k2], w=[k2])
            P.op("act", "activation", out=dst[:], in_=a2[:], func=AF.Sin, r=[k2], w=[("rope",)])
            if which == "s":
                pass

        for mt in range(2):
            slot = mt
            P.op("sp", "dma_start", out=xt[slot][:], in_=mem_d[mt * 128:(mt + 1) * 128, :],
                  w=[("xt", slot)], dma="xt%d" % slot)
            norm_tile_to_T(None, None, slot, memT, "memT", mt, ncols_tok=128)
        for l in range(2):
            slotw = l % 2
            src = w_memkv[l].rearrange("(k p) n -> p k n", p=128)
            P.op("pool", "dma_start", out=Wst[slotw][:, :, :], in_=src,
                  w=[("KTs", slotw)], dma="w%d" % slotw)
            for kc in range(8):
                P.op("dve", "tensor_scalar", Wb[slotw][:, kc, :], Wst[slotw][:, kc, :], gains[:, 3 + l, kc:kc + 1], None, ALU.mult,
                    r=[("KTs", slotw), ("gains",)], w=[("Wb", slotw, kc), ("VsW", slotw)])
            P.op("dve", "memset", kmpad[:, l], 0.0, w=[("kmpad",)])
            P.op("dve", "memset", vmpad[:, l], 0.0, w=[("vmpad",)])
            for mt in range(2):
                bank = mt
                proj_tok(slotw, bank, mt, 512, memT, "memT")
                sl = mt
                kst = headnorm(bank, sl, 0, 4, 3 + 2 * l, stg[sl], False, 0)
                for h in range(4):
                    dst = vmpad[:, l, h, mt, (h % 2) * 64:(h % 2) * 64 + 64]
                    P.op("dve", "tensor_copy", dst, PS[bank][:, 256 + h * 64:256 + (h + 1) * 64],
                        r=[("ps", bank)], w=[("vmpad",)])
                chunk_list = []
                for ci in range(2):
                    chunk_list.append((ci, tst[sl][:, ci, :], [("tst", sl, ci)]))
                transpose_chunks(sl, stg[sl], chunk_list, kst, 6 + mt)
                for h in (3, 2, 1, 0):
                    r0 = (h % 2) * 64
                    P.op("dve", "tensor_copy", kmpad[r0:r0 + 64, l, h, mt * 128:(mt + 1) * 128], tst[sl][r0:r0 + 64, h // 2, :],
                        r=[("tst", sl, h // 2)], w=[("kmpad",)])

        for tt in range(16):
            slot = tt % 2
            P.op("sp", "dma_start", out=xt[slot][:], in_=x_d[tt * 128:(tt + 1) * 128, :],
                  w=[("xt", slot)], dma="xt%d" % slot)
            norm_tile_to_T(None, None, slot, hy, "hy", tt)

        groups = [
            (0, [("q", 0, 8, 0)]),
            (512, [("q", 0, 4, 4), ("k", 4, 4, 0)]),
            (1024, [("k", 0, 8, 2)]),
            (1536, [("v", 0, 512, 0)]),
            (2048, [("v", 0, 256, 512), ("qm", 4, 4, 0)]),
        ]
        for gi, (col0, parts) in enumerate(groups):
            slotw = gi % 2
            load_weights(w_ain, col0, 512, 0, slotw)
            for tt in range(16):
                bank = tt % 2
                sl = tt % 2
                proj_tok(slotw, bank, tt, 512, hy, "hy")
                chunk_list = []
                skeys = []
                for part in parts:
                    kind = part[0]
                    if kind in ("q", "k"):
                        _, hb0, nhb, pair0 = part
                        kst = headnorm(bank, sl, hb0, nhb, 0 if kind == "q" else 1, stg[sl], True, tt)
                        skeys.extend(kst)
                        for ci in range(nhb // 2):
                            pair = pair0 + ci
                            if kind == "q":
                                chunk_list.append((hb0 // 2 + ci, QT[:, pair, tt * 128:(tt + 1) * 128], [("QT", pair, tt)]))
                            else:
                                chunk_list.append((hb0 // 2 + ci, tst[sl][:, hb0 // 2 + ci, :], [("tst", sl, hb0 // 2 + ci)]))
                    elif kind == "qm":
                        _, hb0, nhb, _ = part
                        kst = headnorm(bank, sl, hb0, nhb, 2, stg[sl], False, tt)
                        skeys.extend(kst)
                        for ci in range(2):
                            chunk_list.append((hb0 // 2 + ci, qmT[:, ci, tt * 128:(tt + 1) * 128], [("qmT", ci, tt)]))
                    else:
                        _, c0, nc_, vc0 = part
                        P.op("act", "activation", out=vst[sl][:, vc0:vc0 + nc_], in_=PS[bank][:, c0:c0 + nc_], func=AF.Copy,
                            r=[("ps", bank)], w=[("vst", sl, vc0)])
                        dst = vown[vc0 // 128:(vc0 + nc_) // 128, :, tt, :].rearrange("h p d -> p h d")
                        P.op("sp", "dma_start", out=dst, in_=vst[sl][:, vc0:vc0 + nc_].rearrange("p (h d) -> p h d", d=128),
                            r=[("vst", sl, vc0)], w=[("vown", tt, vc0)], dma="vo")
                if chunk_list:
                    transpose_chunks(sl, stg[sl], chunk_list, skeys, 6 + sl)
                    for (ci, dst, dk) in chunk_list:
                        if dk[0][0] == "tst":
                            kpair = None
                            for part in parts:
                                if part[0] == "k":
                                    kpair = part[3] + (ci - part[1] // 2)
                            P.op("sp", "dma_start", out=kown[kpair, :, tt * 128:(tt + 1) * 128], in_=tst[sl][:, ci, :],
                                r=dk, w=[("kown", kpair, tt)], dma="ko")
        for gg in range(2):
            slotw = (5 + gg) % 2
            load_weights(w_ain, 2560 + gg * 512, 512, 0, slotw)
            for ch in range(4):
                for j in range(4):
                    bank = 2 + (ch * 4 + j) % 4
                    proj_feat(slotw, bank, ch, j, hy, "hy")
                    chunk = gg * 4 + ch
                    silu_evac(bank, gateT[:, chunk, j * 512:(j + 1) * 512], [("gateT", chunk, j)], 2 + (ch * 4 + j) % 4)

    def load_kv(all_k, all_v, pair, slot):
        for rnk in range(4):
            P.op("sp", "dma_start", out=KTs[slot][:, rnk * 2048:(rnk + 1) * 2048], in_=all_k[rnk, pair],
                  w=[("KTs", slot)], dma="kt%d" % slot)
            P.op("sp", "dma_start", out=Vs[slot][:, rnk * 16:(rnk + 1) * 16, :], in_=all_v[rnk, pair],
                  w=[("Vs", 0), ("VsW", 0), ("VsW", 1)], dma="vs0")

    def key_steps(j):
        steps = []
        for m in (3, 2, 1, 0):
            for r in (3, 2, 1, 0):
                steps.append((r, 4 * j + m, m, r))
        for g in range(16 * j - 1, -1, -1):
            jj, rem = divmod(g, 16)
            s_, c_ = divmod(rem, 4)
            steps.append((c_, 4 * jj + s_, None, None))
        return steps

    def make_qpad(pair, j, qs, scale):
        P.op("pool", "memset", qpad[qs][:], 0.0, w=[("qpad", qs)])
        for hh in range(2):
            r0 = hh * 64
            if scale == 1.0:
                P.op("pool", "tensor_copy", qpad[qs][r0:r0 + 64, hh, :], QT[r0:r0 + 64, pair, j * 512:(j + 1) * 512],
                      r=[("QT", pair, t_) for t_ in range(4 * j, 4 * j + 4)], w=[("qpad", qs)])
            else:
                P.op("dve", "tensor_scalar", qpad[qs][r0:r0 + 64, hh, :], QT[r0:r0 + 64, pair, j * 512:(j + 1) * 512],
                                                                  scale, None, ALU.mult,
                      r=[("QT", pair, t_) for t_ in range(4 * j, 4 * j + 4)], w=[("qpad", qs)])

    def attention0(all_k, all_v, finalize):
        it = 0
        for pair in range(6):
            slot = pair % 2
            load_kv(all_k, all_v, pair, slot)
            for j in range(4):
                qs = (pair * 4 + j) % 2
                make_qpad(pair, j, qs, 1.0)
                for b in (4, 5):
                    P.op("pe", "matmul", PS[b][:, :], zer[:, 0:128], zer[:, :], start=True, stop=False,
                          r=[("zer",)], w=[("ps", b)])
                P.op("pool", "memset", accl[:], 0.0, w=[("accl",)])
                steps = key_steps(j)
                prev = None
                for si, (rnk, lt, m, r) in enumerate(steps):
                    buf = it % 2
                    it += 1
                    c0 = 0 if m is None else 128 * m
                    kcol = rnk * 2048 + lt * 128
                    vt = rnk * 16 + lt
                    last = (si == len(steps) - 1)
                    for c in range(2):
                        P.op("pe", "matmul", PS[buf * 2 + c][:, c0:512], KTs[slot][:, kcol:kcol + 128], qpad[qs][:, c, c0:512],
                            start=True, stop=True,
                            r=[("KTs", slot), ("qpad", qs)], w=[("ps", buf * 2 + c)])
                    P.op("act", "activation", out=pb[buf][:, :, c0:512], in_=PSA[:, buf * 2:buf * 2 + 2, c0:512], func=AF.Exp, scale=0.125,
                        r=[("ps", buf * 2), ("ps", buf * 2 + 1)], w=[("pb", buf, 0), ("pb", buf, 1)])
                    if m is not None:
                        P.op("dve", "tensor_tensor", pb[buf][:, :, c0:c0 + 128], pb[buf][:, :, c0:c0 + 128],
                            mk[:, 0, r, :].unsqueeze(1).broadcast_to([128, 2, 128]), ALU.mult,
                            r=[("pb", buf, 0), ("pb", buf, 1), ("mk",)], w=[("pb", buf, 0), ("pb", buf, 1)])
                    P.op("dve", "tensor_tensor", accl[:, :, c0:512], accl[:, :, c0:512], pb[buf][:, :, c0:512], ALU.add,
                          r=[("accl",), ("pb", buf, 0), ("pb", buf, 1)], w=[("accl",)])
                    if prev is not None:
                        emit_pv0(prev, slot, False)
                    prev = (buf, c0, vt)
                emit_pv0(prev, slot, True)
                for c in range(2):
                    P.op("pe", "matmul", PS[6 + c][:, :], ones32[:], accl[:, c, :], start=True, stop=True,
                          r=[("ones32",), ("accl",)], w=[("ps", 6 + c)])
                finalize(pair, j)

    def emit_pv0(prev, slot, last):
        buf, c0, vt = prev
        for c in range(2):
            P.op("pe", "matmul", PS[4 + c][:, c0:512], Vs[slot][:, vt, :], pb[buf][:, c, c0:512],
                                                start=False, stop=last,
                  r=[("Vs", 0), ("VsW", 0), ("VsW", 1), ("pb", buf, c)], w=[("ps", 4 + c)])

    def finalize0(pair, j):
        r0, r1, t0, t1 = w32[2], w32[3], w32[4], w32[5]
        P.op("dve", "reciprocal", r0[:], PS[6][:, :], r=[("ps", 6)], w=[("w32", 2)])
        P.op("dve", "reciprocal", r1[:], PS[7][:, :], r=[("ps", 7)], w=[("w32", 3)])
        P.op("dve", "tensor_tensor", t0[:], PS[4][:, :], r0[:], ALU.mult, r=[("ps", 4), ("w32", 2)], w=[("w32", 4)])
        P.op("dve", "tensor_tensor", t1[:], PS[5][:, :], r1[:], ALU.mult, r=[("ps", 5), ("w32", 3)], w=[("w32", 5)])
        P.op("dve", "scalar_tensor_tensor", t0[:], t1[:], sml[:, 2:3], t0[:], ALU.mult, ALU.add,
              r=[("w32", 4), ("w32", 5), ("sml",)], w=[("w32", 4)])
        P.op("act", "activation", out=pb[0][:, 0, :], in_=t0[:], func=AF.Square, r=[("w32", 4)], w=[("pb", 0, 0)])
        P.op("pe", "matmul", PS[6][:, :], ones[:], pb[0][:, 0, :], start=True, stop=True,
              r=[("ones",), ("pb", 0, 0)], w=[("ps", 6)])
        P.op("act", "activation", out=r0[:], in_=PS[6][:, :], func=AF.Ln, scale=1.0 / 128, bias=EPS, r=[("ps", 6)], w=[("w32", 2)])
        P.op("act", "activation", out=r0[:], in_=r0[:], func=AF.Exp, scale=-0.5, r=[("w32", 2)], w=[("w32", 2)])
        P.op("dve", "scalar_tensor_tensor", t0[:], t0[:], sml[:, 1:2], r0[:], ALU.mult, ALU.mult,
              r=[("w32", 4), ("w32", 2), ("sml",)], w=[("w32", 4)])
        P.op("dve", "tensor_tensor", hy[:, pair, j * 512:(j + 1) * 512], t0[:], gateT[:, pair, j * 512:(j + 1) * 512], ALU.mult,
              r=[("w32", 4), ("gateT", pair, j)], w=[("hy", pair, t_) for t_ in range(4 * j, 4 * j + 4)])

    def attention1(all_k, all_v, finalize):
        for pair in range(6):
            slot = pair % 2
            load_kv(all_k, all_v, pair, slot)
            for j in range(4):
                qs = (pair * 4 + j) % 2
                make_qpad(pair, j, qs, 0.125)
                for b in (6, 7):
                    P.op("pe", "matmul", PS[b][:, :], zer[:, 0:128], zer[:, :], start=True, stop=False,
                          r=[("zer",)], w=[("ps", b)])
                steps = key_steps(j)
                n = len(steps)
                P.op("pool", "memset", acc[0][:], 0.0, w=[("acc", 0)])

                def info(si):
                    rnk, lt, m, r = steps[si]
                    c0 = 0 if m is None else 128 * m
                    kcol = rnk * 2048 + lt * 128
                    return c0, KTs[slot][:, kcol:kcol + 128], rnk * 16 + lt, m, r

                def stageA(si):
                    c0, ksl, vt, m, r = info(si)
                    buf = si % 2
                    for hh in range(2):
                        zb = buf * 2 + hh
                        P.op("pe", "matmul", PS[zb][:, c0:512], ksl, qpad[qs][:, hh, c0:512], start=True, stop=True,
                              r=[("KTs", slot), ("qpad", qs)], w=[("ps", zb)])
                    zz = PSA[:, buf * 2:buf * 2 + 2, c0:512]
                    P.op("act", "activation", out=zz, in_=zz, func=AF.Exp,
                          r=[("ps", buf * 2), ("ps", buf * 2 + 1)], w=[("ps", buf * 2), ("ps", buf * 2 + 1)])
                    P.op("act", "activation", out=pb[buf][:, :, c0:512], in_=zz, func=AF.Ln, bias=1.0,
                          r=[("ps", buf * 2), ("ps", buf * 2 + 1)], w=[("pb", buf, 0), ("pb", buf, 1)])
                    if m is not None:
                        P.op("dve", "tensor_tensor", pb[buf][:, :, c0:c0 + 128], pb[buf][:, :, c0:c0 + 128],
                              mk[:, 1, r, :].unsqueeze(1).broadcast_to([128, 2, 128]), ALU.mult,
                              r=[("pb", buf, 0), ("pb", buf, 1), ("mk",)], w=[("pb", buf, 0), ("pb", buf, 1)])
                    if si < n - 1:
                        cur, nxt = si % 3, (si + 1) % 3
                        if c0 > 0:
                            P.op("pool", "memset", acc[nxt][:, :, 0:c0], 0.0, w=[("acc", nxt)])
                        P.op("dve", "tensor_tensor", acc[nxt][:, :, c0:512], acc[cur][:, :, c0:512], pb[buf][:, :, c0:512], ALU.subtract,
                              r=[("acc", cur), ("pb", buf, 0), ("pb", buf, 1)], w=[("acc", nxt)])

                def stageB(si):
                    c0, ksl, vt, m, r = info(si)
                    buf = si % 2
                    cur = si % 3
                    for hh in range(2):
                        cb = 4 + hh
                        P.op("pe", "matmul", PS[cb][:, c0:512], ksl, qpad[qs][:, hh, c0:512], start=True, stop=False,
                              r=[("KTs", slot), ("qpad", qs)], w=[("ps", cb)])
                        P.op("pe", "matmul", PS[cb][:, c0:512], tri[:], pb[buf][:, hh, c0:512], start=False, stop=(si == 0),
                              r=[("tri",), ("pb", buf, hh)], w=[("ps", cb)])
                        if si > 0:
                            P.op("pe", "matmul", PS[cb][:, c0:512], ones[:], acc[cur][:, hh, c0:512], start=False, stop=True,
                                  r=[("ones",), ("acc", cur)], w=[("ps", cb)])
                    P.op("act", "activation", out=ab[buf][:, :, c0:512], in_=PSA[:, 4:6, c0:512], func=AF.Exp,
                          r=[("ps", 4), ("ps", 5)], w=[("ab", buf, 0), ("ab", buf, 1)])
                    if m is not None:
                        P.op("dve", "tensor_tensor", ab[buf][:, :, c0:c0 + 128], ab[buf][:, :, c0:c0 + 128],
                              mk[:, 1, r, :].unsqueeze(1).broadcast_to([128, 2, 128]), ALU.mult,
                              r=[("ab", buf, 0), ("ab", buf, 1), ("mk",)], w=[("ab", buf, 0), ("ab", buf, 1)])

                def stagePV(si, last):
                    c0, ksl, vt, m, r = info(si)
                    buf = si % 2
                    for hh in range(2):
                        P.op("pe", "matmul", PS[6 + hh][:, c0:512], Vs[slot][:, vt, :], ab[buf][:, hh, c0:512],
                              start=False, stop=last,
                              r=[("Vs", 0), ("VsW", 0), ("VsW", 1), ("ab", buf, hh)], w=[("ps", 6 + hh)])

                stageA(0)
                for si in range(n):
                    if si + 1 < n:
                        stageA(si + 1)
                    stageB(si)
                    if si > 0:
                        stagePV(si - 1, False)
                stagePV(n - 1, True)
                finalize(pair, j)

    def emit_pv1(prev, slot, last):
        buf, c0, vt = prev
        for hh in range(2):
            P.op("pe", "matmul", PS[6 + hh][:, c0:512], Vs[slot][:, vt, :], ab[buf][:, hh, c0:512],
                                                  start=False, stop=last,
                  r=[("Vs", 0), ("VsW", 0), ("VsW", 1), ("ab", buf, hh)], w=[("ps", 6 + hh)])

    def finalize1(pair, j):
        for hh in range(2):
            r0 = hh * 64
            P.op("dve", "tensor_tensor", hy[r0:r0 + 64, pair, j * 512:(j + 1) * 512], PS[6 + hh][r0:r0 + 64, :],
                gateT[r0:r0 + 64, pair, j * 512:(j + 1) * 512], ALU.mult,
                r=[("ps", 6 + hh), ("gateT", pair, j)], w=[("hy", pair, t_) for t_ in range(4 * j, 4 * j + 4)])

    def mem_attention(l):
        for ci in range(2):
            for j in range(4):
                P.op("pe", "matmul", PS[3][:, :], zer[:, 0:128], zer[:, :], start=True, stop=False,
                      r=[("zer",)], w=[("ps", 3)])
                for hh in range(2):
                    h = ci * 2 + hh
                    buf = hh
                    for mt in range(2):
                        P.op("pe", "matmul", PS[mt][:, :], kmpad[:, l, h, mt * 128:(mt + 1) * 128],
                                                                   qmT[:, ci, j * 512:(j + 1) * 512], start=True, stop=True,
                              r=[("kmpad",)] + [("qmT", ci, t_) for t_ in range(4 * j, 4 * j + 4)], w=[("ps", mt)])
                        P.op("act", "activation", out=pb[buf][:, mt, :], in_=PS[mt][:, :], func=AF.Exp, scale=0.125,
                              r=[("ps", mt)], w=[("pb", buf, mt)])
                    for mt in range(2):
                        P.op("pe", "matmul", PS[2][:, :], ones[:], pb[buf][:, mt, :], start=(mt == 0), stop=(mt == 1),
                              r=[("ones",), ("pb", buf, mt)], w=[("ps", 2)])
                    rl = w32[2 + hh]
                    P.op("dve", "reciprocal", rl[:], PS[2][:, :], r=[("ps", 2)], w=[("w32", 2 + hh)])
                    P.op("dve", "tensor_tensor", ab[buf][:, :, :], pb[buf][:, :, :],
                                                                           rl[:].unsqueeze(1).broadcast_to([128, 2, 512]), ALU.mult,
                          r=[("w32", 2 + hh), ("pb", buf, 0), ("pb", buf, 1)], w=[("ab", buf, 0), ("ab", buf, 1)])
                    for mt in range(2):
                        P.op("pe", "matmul", PS[3][:, :], vmpad[:, l, h, mt, :], ab[buf][:, mt, :],
                                                                                   start=False, stop=(hh == 1 and mt == 1),
                              r=[("vmpad",), ("ab", buf, mt)], w=[("ps", 3)])
                ch = 6 + ci
                P.op("dve", "tensor_tensor", hy[:, ch, j * 512:(j + 1) * 512], PS[3][:, :],
                                                              gateT[:, ch, j * 512:(j + 1) * 512], ALU.mult,
                      r=[("ps", 3), ("gateT", ch, j)], w=[("hy", ch, t_) for t_ in range(4 * j, 4 * j + 4)])

    def out_proj(w_ap, res_d, dst_d, after_tile, res_key, dst_key):
        for half in range(2):
            src = w_ap[:, half * 512:(half + 1) * 512].rearrange("(k p) n -> p k n", p=128)
            P.op("pool", "dma_start", out=Wst[half][:, :, :], in_=src,
                  w=[("KTs", half)], dma="w%d" % half)
            for kc in range(8):
                eng = "act" if kc % 2 == 0 else "dve"
                if eng == "act":
                    P.op("act", "activation", out=Wb[half][:, kc, :], in_=Wst[half][:, kc, :], func=AF.Copy,
                          r=[("KTs", half)], w=[("Wb", half, kc), ("VsW", half)])
                else:
                    P.op("dve", "tensor_copy", Wb[half][:, kc, :], Wst[half][:, kc, :],
                          r=[("KTs", half)], w=[("Wb", half, kc), ("VsW", half)])
        for tt in range(16):
            slot = tt % 2
            P.op("sp", "dma_start", out=xt[slot][:], in_=res_d[tt * 128:(tt + 1) * 128, :],
                  r=[("dram", res_key, tt)], w=[("xt", slot)], dma="xt%d" % slot)
            for half in range(2):
                bank = half
                for kc in range(8):
                    P.op("pe", "matmul", PS[bank][:, :], hy[:, kc, tt * 128:(tt + 1) * 128], Wb[half][:, kc, :], start=(kc == 0), stop=(kc == 7),
                        r=[("hy", kc, tt), ("Wb", half, kc), ("VsW", half)], w=[("ps", bank)])
                P.op("dve", "tensor_tensor", xt[slot][:, half * 512:(half + 1) * 512], xt[slot][:, half * 512:(half + 1) * 512], PS[bank][:, :], ALU.add,
                    r=[("ps", bank), ("xt", slot)], w=[("xt", slot)])
            P.op("sp", "dma_start", out=dst_d[tt * 128:(tt + 1) * 128, :], in_=xt[slot][:],
                  r=[("xt", slot)], w=[("dram", dst_key, tt)], dma="od")
            if after_tile is not None:
                after_tile(tt, slot)

    def phase2():
        kown1, vown1 = ex1[0], ex1[1]
        P.op("sp", "dma_start", out=lamv[:], in_=lamv_d, w=[("lamv",)], dma="c2")
        P.op("sp", "dma_start", out=sml[:, 0:1], in_=subln_d, w=[("sml",)], dma="c2")
        pr = w32[0][:, 0:128].rearrange("p (a d) -> p a d", d=64)
        P.op("dve", "tensor_tensor", pr[:, 0, :], lamv[:, 0, :], lamv[:, 1, :], ALU.mult, r=[("lamv",)], w=[("w32", 0, 0)])
        P.op("dve", "tensor_tensor", pr[:, 1, :], lamv[:, 2, :], lamv[:, 3, :], ALU.mult, r=[("lamv",)], w=[("w32", 0, 1)])
        P.op("dve", "tensor_reduce", sml[:, 4:6], pr, AX.X, ALU.add, r=[("w32", 0, 0), ("w32", 0, 1)], w=[("sml4",)])
        P.op("act", "activation", out=sml[:, 6:8], in_=sml[:, 4:6], func=AF.Exp, r=[("sml4",)], w=[("sml6",)])
        P.op("dve", "tensor_tensor", sml[:, 2:3], sml[:, 7:8], sml[:, 6:7], ALU.subtract, r=[("sml6",)], w=[("sml2",)])
        P.op("dve", "tensor_scalar", sml[:, 2:3], sml[:, 2:3], -LAMBDA_INIT0, None, ALU.add, r=[("sml2",)], w=[("sml2",)])
        P.op("dve", "tensor_scalar", sml[:, 1:2], sml[:, 0:1], 1.0 - LAMBDA_INIT0, None, ALU.mult,
              r=[("sml",), ("sml2",)], w=[("sml",)])

        attention0(ex0[2], ex0[3], finalize0)
        mem_attention(0)

        def after(tt, slot):
            norm_tile_to_T(None, None, slot, hy, "hy", tt)
        out_proj(w_aout, x_d, x1_d, after, "x", "x1")

        for gi, (col0, ncols) in enumerate(((0, 512), (512, 256))):
            slotw = gi % 2
            load_weights(w_kv, col0, ncols, 1, slotw)
            for ch in range(ncols // 128):
                pair = col0 // 128 + ch
                for j in range(4):
                    bank = 2 + (ch * 4 + j) % 4
                    proj_feat(slotw, bank, ch, j, hy, "hy")
                    sl = (ch * 4 + j) % 2
                    P.op("act", "activation", out=stg[sl][:, :], in_=PS[bank][:, :], func=AF.Copy,
                          r=[("ps", bank)], w=[("stg", sl, hb) for hb in range(8)])
                    P.op("sp", "dma_start", out=kown1[pair, :, j * 512:(j + 1) * 512], in_=stg[sl][:, :],
                          r=[("stg", sl, hb) for hb in range(8)], w=[("kown1", pair, j)], dma="ko")
        for gi, (col0, ncols, vc0) in enumerate(((768, 512, 0), (1280, 256, 512))):
            slotw = gi % 2
            load_weights(w_kv, col0, ncols, 1, slotw)
            for tt in range(16):
                bank = tt % 2
                sl = tt % 2
                proj_tok(slotw, bank, tt, ncols, hy, "hy")
                P.op("act", "activation", out=vst[sl][:, vc0:vc0 + ncols], in_=PS[bank][:, 0:ncols], func=AF.Copy,
                    r=[("ps", bank)], w=[("vst", sl, vc0)])
                dst = vown1[vc0 // 128:(vc0 + ncols) // 128, :, tt, :].rearrange("h p d -> p h d")
                P.op("sp", "dma_start", out=dst, in_=vst[sl][:, vc0:vc0 + ncols].rearrange("p (h d) -> p h d", d=128),
                    r=[("vst", sl, vc0)], w=[("vown1", tt, vc0)], dma="vo")
        for gi, (col0, ncols) in enumerate(((0, 512), (512, 256))):
            slotw = gi % 2
            load_weights(w_bin, col0, ncols, 2, slotw)
            for ch in range(ncols // 128):
                pair = col0 // 128 + ch
                for j in range(4):
                    bank = 2 + (ch * 4 + j) % 4
                    proj_feat(slotw, bank, ch, j, hy, "hy")
                    if (ch * 4 + j) % 2 == 0:
                        P.op("act", "activation", out=QT[:, pair, j * 512:(j + 1) * 512], in_=PS[bank][:, :], func=AF.Copy,
                              r=[("ps", bank)], w=[("QT", pair, t_) for t_ in range(4 * j, 4 * j + 4)])
                    else:
                        P.op("dve", "tensor_copy", QT[:, pair, j * 512:(j + 1) * 512], PS[bank][:, :],
                              r=[("ps", bank)], w=[("QT", pair, t_) for t_ in range(4 * j, 4 * j + 4)])
        load_weights(w_bin, 768, 256, 2, 0)
        for tt in range(16):
            bank = tt % 2
            sl = tt % 2
            proj_tok(0, bank, tt, 256, hy, "hy")
            kst = headnorm(bank, sl, 0, 4, 4, stg[sl], False, tt)
            chunk_list = [(ci, qmT[:, ci, tt * 128:(tt + 1) * 128], [("qmT", ci, tt)]) for ci in range(2)]
            transpose_chunks(sl, stg[sl], chunk_list, kst, 6 + sl)
        for gg in range(2):
            slotw = (1 + gg) % 2
            load_weights(w_bin, 1024 + gg * 512, 512, 2, slotw)
            for ch in range(4):
                for j in range(4):
                    bank = 2 + (ch * 4 + j) % 4
                    proj_feat(slotw, bank, ch, j, hy, "hy")
                    chunk = gg * 4 + ch
                    silu_evac(bank, gateT[:, chunk, j * 512:(j + 1) * 512], [("gateT", chunk, j)], 2 + (ch * 4 + j) % 4)

    def phase3():
        attention1(ex1[2], ex1[3], finalize1)
        mem_attention(1)
        out_proj(w_bout, x1_d, out_d, None, "x1", "out")

    final_groups = []
    if fused:
        raise NotImplementedError
    else:
        if 1 in phases:
            phase1()
            for nm in states:
                store_state(nm)
            final_groups += ["ko", "vo", "so"]
        if 2 in phases:
            for nm in ("QT", "gateT", "qmT", "kmpad", "vmpad"):
                load_state(nm)
            phase2()
            for nm in ("QT", "gateT", "qmT"):
                store_state(nm)
            final_groups += ["ko", "vo", "so", "od"]
        if 3 in phases:
            for nm in ("QT", "gateT", "qmT", "kmpad", "vmpad"):
                load_state(nm)
            phase3()
            final_groups += ["od"]
    P.emit(final_groups)
    return nc, ins_names, outs_names


def _prep_common(inputs):
    f32 = np.float32
    cst = np.zeros((128, 2, 128), f32)
    cst[:, 0, :] = np.eye(128, dtype=f32)
    jj = np.arange(128)[:, None]
    ss = np.arange(128)[None, :]
    cst[:, 1, :] = -(jj >= ss).astype(f32)
    inv = (np.float32(500000.0) ** (-(np.arange(0, 16, 2, dtype=np.float32)) / np.float32(16))).astype(f32)
    invf = np.broadcast_to(inv[None, :], (128, 8)).copy()

    def pk(v):
        return np.ascontiguousarray(np.asarray(v, f32).reshape(8, 128).T)

    gains = np.stack([pk(inputs["a_norm"][0]), pk(inputs["kv_norm"]), pk(inputs["b_norm"][0]),
                      pk(inputs["mem_norm"][0]), pk(inputs["mem_norm"][1])], axis=1)
    hgl = [inputs["a_q_norm"][0], inputs["a_k_norm"][0], inputs["mem_q_norm"][0], inputs["mem_k_norm"][0],
           inputs["mem_q_norm"][1], inputs["mem_k_norm"][1]]
    hg = np.broadcast_to(np.stack([np.asarray(v, f32) for v in hgl], 0)[None], (128, 6, 64)).copy()
    lamv = np.broadcast_to(np.stack([np.asarray(inputs[k][0], f32) for k in
                                     ("a_lambda_q1", "a_lambda_k1", "a_lambda_q2", "a_lambda_k2")], 0)[None], (128, 4, 64)).copy()
    subln = np.asarray(inputs["a_subln"][0], f32).reshape(128, 1).copy()
    w = np.asarray(inputs["a_w_in"][0], f32)
    perm = []
    for base in (0, 768):
        for h in range(6):
            for c in range(2):
                perm.extend(range(base + c * 384 + h * 64, base + c * 384 + h * 64 + 64))
    perm.extend(range(1536, 3584))
    w_ain = np.ascontiguousarray(w[:, perm])
    return dict(cst=cst, invf=invf, gains=gains, hg=hg, lamv=lamv, subln=subln, w_ain=w_ain)


def _masks(c):
    mk = np.zeros((128, 2, 4, 128), np.float32)
    p = np.arange(128)[:, None]
    q = np.arange(128)[None, :]
    for r in range(4):
        if r < c:
            mk[:, :, r, :] = 1.0
        elif r == c:
            mk[:, 0, r, :] = (p <= q)
            mk[:, 1, r, :] = (p < q)
    return mk


_CACHE = {}


def _get(phases, fused):
    key = (tuple(sorted(phases)), fused)
    if key not in _CACHE:
        _CACHE[key] = build(set(phases), fused)
    return _CACHE[key]


def _gather(owns, b):
    return np.stack([owns[b * 4 + c] for c in range(4)], 0)


def kernel(**inputs):
    f32 = np.float32
    com = _prep_common(inputs)
    x = np.asarray(inputs["x"], f32)
    mem = np.asarray(inputs["mem"], f32)
    pos = np.asarray(inputs["positions"], np.int32)
    cores = list(range(8))
    rows = {}
    for core in cores:
        b, c = divmod(core, 4)
        rows[core] = np.concatenate([np.arange(g * 128, (g + 1) * 128) for g in _blocks(c)])
    w_memkv = np.asarray(inputs["mem_w_kv"], f32)
    base = []
    for core in cores:
        b, c = divmod(core, 4)
        d = dict(com)
        d["x"] = np.ascontiguousarray(x[b, rows[core]])
        d["pos"] = np.ascontiguousarray(pos[b, rows[core]].reshape(16, 128).T)
        d["mem"] = mem[b]
        d["w_memkv"] = w_memkv
        d["mk"] = _masks(c)
        d["w_aout"] = np.asarray(inputs["a_w_out"][0], f32)
        d["w_kv"] = np.asarray(inputs["w_kv_shared"], f32)
        d["w_bin"] = np.asarray(inputs["b_w_in"][0], f32)
        d["w_bout"] = np.asarray(inputs["b_w_out"][0], f32)
        base.append(d)

    def run(phases, extra):
        nc, ins_names, outs_names = _get(phases, False)
        maps = []
        for core in cores:
            m = {}
            for nme in ins_names:
                if nme in extra[core]:
                    m["d_" + nme] = extra[core][nme]
                else:
                    m["d_" + nme] = base[core][nme]
            maps.append(m)
        res = run_bass_kernel_spmd(nc, maps, core_ids=cores)
        return [{k[2:]: v for k, v in r.items()} for r in res.results]

    r1 = run([1], [dict() for _ in cores])
    ex = []
    for core in cores:
        b = core // 4
        e = {"kall0": _gather([r["kown0"] for r in r1], b), "vall0": _gather([r["vown0"] for r in r1], b)}
        for nm in ("QT", "gateT", "qmT", "kmpad", "vmpad"):
            e["st_" + nm] = r1[core]["so_" + nm]
        ex.append(e)
    r2 = run([2], ex)
    ex3 = []
    for core in cores:
        b = core // 4
        e = {"kall1": _gather([r["kown1"] for r in r2], b), "vall1": _gather([r["vown1"] for r in r2], b),
             "x1": r2[core]["x1"]}
        for nm in ("QT", "gateT", "qmT"):
            e["st_" + nm] = r2[core]["so_" + nm]
        for nm in ("kmpad", "vmpad"):
            e["st_" + nm] = r1[core]["so_" + nm]
        ex3.append(e)
    r3 = run([3], ex3)
    out = np.zeros((2, 8192, 1024), f32)
    for core in cores:
        b = core // 4
        out[b, rows[core]] = r3[core]["out"]
    return out
```

```python
import math
import numpy as np
import ml_dtypes
import concourse.bass as bass
import concourse.mybir as mybir
from concourse.bass_utils import run_bass_kernel_spmd

F32 = mybir.dt.float32
BF16 = mybir.dt.bfloat16
I32 = mybir.dt.int32
AF = mybir.ActivationFunctionType
ALU = mybir.AluOpType
AX = mybir.AxisListType

NT = 2048
EPS = 1e-6
LAMBDA_INIT0 = 0.8 - 0.6 * math.exp(-0.3 * 0)
SEM_CAP = 20000


class Prog:
    def __init__(self, nc):
        self.nc = nc
        self.ops = []
        self.lw = {}
        self.rd = {}
        self.h = {"pe": nc.tensor, "act": nc.scalar, "dve": nc.vector, "pool": nc.gpsimd, "sp": nc.sync}

    def add(self, eng, fn, r=(), w=(), dma=None):
        idx = len(self.ops)
        deps = set()
        for k in r:
            if k in self.lw:
                deps.add(self.lw[k])
            if k[0] == "ps":
                for x in self.rd.get(k, ()):
                    if self.ops[x][0] != eng:
                        deps.add(x)
        for k in w:
            if k in self.lw:
                deps.add(self.lw[k])
            for x in self.rd.get(k, ()):
                deps.add(x)
        for k in w:
            self.lw[k] = idx
            self.rd[k] = []
        for k in r:
            self.rd.setdefault(k, []).append(idx)
        deps.discard(idx)
        self.ops.append((eng, fn, deps, dma))
        return idx

    def op(self, eng, meth, *args, r=(), w=(), dma=None, **kw):
        return self.add(eng, (meth, args, kw), r, w, dma)

    def emit(self, final_groups=()):
        nc = self.nc
        ops = self.ops
        n = len(ops)
        sig = [False] * n
        for (eng, fn, deps, dma) in ops:
            for d in deps:
                p = ops[d]
                if p[3] is not None:
                    continue
                if p[0] == "pe" and eng == "pe" and dma is None:
                    continue
                sig[d] = True
        import os as _os
        if int(_os.environ.get("KLIMIT", "0")):
            sig = [True] * n
        sems = {}

        def getsem(name):
            if name not in sems:
                sems[name] = nc.semaphore(name).__enter__()
            return sems[name]

        ecount = {}
        sigval = [None] * n
        gcount = {}
        waited = {}
        import os
        limit = int(os.environ.get("KLIMIT", "0")) or n
        for i, (eng, fn, deps, dma) in enumerate(ops):
            if i >= limit:
                break
            E = self.h[eng]
            needs = {}
            for d in deps:
                p = ops[d]
                if p[3] is not None:
                    key = "g_" + p[3]
                    val = gcount[p[3]]
                else:
                    if p[0] == "pe" and eng == "pe" and dma is None:
                        continue
                    key, val = sigval[d]
                if needs.get(key, 0) < val:
                    needs[key] = val
            for key, val in needs.items():
                if waited.get((eng, key), 0) < val:
                    E.wait_ge(getsem(key), val)
                    waited[(eng, key)] = val
            meth, args, kw = fn
            ins = getattr(E, meth)(*args, **kw)
            if dma is not None:
                gcount[dma] = gcount.get(dma, 0) + 16
                ins.then_inc(getsem("g_" + dma), 16)
            elif sig[i]:
                c = ecount.get(eng, 0)
                sname = "e_%s_%d" % (eng, c // SEM_CAP)
                v = c % SEM_CAP + 1
                ecount[eng] = c + 1
                ins.then_inc(getsem(sname), 1)
                sigval[i] = (sname, v)
        sp = self.h["sp"]
        if limit < n:
            for g, v in gcount.items():
                sp.wait_ge(getsem("g_" + g), v)
            for eng_, c_ in ecount.items():
                if c_ > 0:
                    sp.wait_ge(getsem("e_%s_%d" % (eng_, (c_ - 1) // SEM_CAP)), (c_ - 1) % SEM_CAP + 1)
            return
        for g in final_groups:
            sp.wait_ge(getsem("g_" + g), gcount[g])


def _blocks(c):
    return [16 * j + 4 * s + c for j in range(4) for s in range(4)]


def build(phases, fused):
    nc = bass.Bass("TRN2", target_bir_lowering=False, dynamic_dma_scratch_size=2048)
    P = Prog(nc)
    ins_names = []
    outs_names = []

    def din(name, shape, dt):
        ins_names.append(name)
        return nc.dram_tensor("d_" + name, list(shape), dt, kind="ExternalInput").ap()

    def dout(name, shape, dt):
        outs_names.append(name)
        return nc.dram_tensor("d_" + name, list(shape), dt, kind="ExternalOutput").ap()

    def dint(name, shape, dt):
        return nc.dram_tensor("d_" + name, list(shape), dt, kind="Internal").ap()

    def sb(name, shape, dt):
        return nc.sbuf_tensor(name, list(shape), dt).__enter__()

    ident = sb("ident", [128, 128], BF16)
    tri = sb("tri", [128, 128], BF16)
    ones = sb("ones", [128, 128], BF16)
    zer = sb("zer", [128, 512], BF16)
    cst32 = sb("cst32", [128, 2, 128], F32)
    mk = sb("mk", [128, 2, 4, 128], F32)
    gains = sb("gains", [128, 5, 8], F32)
    hg = sb("hg", [128, 6, 64], F32)
    lamv = sb("lamv", [128, 4, 64], F32)
    sml = sb("sml", [128, 16], F32)
    invf = sb("invf", [128, 8], F32)
    cosT = sb("cosT", [128, 16, 8], F32)
    sinT = sb("sinT", [128, 16, 8], F32)
    hy = sb("hy", [128, 8, 2048], BF16)
    QT = sb("QT", [128, 6, 2048], BF16)
    gateT = sb("gateT", [128, 8, 2048], BF16)
    qmT = sb("qmT", [128, 2, 2048], BF16)
    BIG = sb("BIG", [128, 24576], BF16)
    xt = [sb("xt%d" % i, [128, 1024], F32) for i in range(2)]
    xnb = [sb("xnb%d" % i, [128, 1024], BF16) for i in range(2)]
    w32all = sb("w32all", [128, 6, 512], F32)
    w32 = [w32all[:, i, :] for i in range(6)]
    sq = w32all[:, 0:2, :].rearrange("p a b -> p (a b)")
    SQK = [("w32", a_, hb_) for a_ in range(2) for hb_ in range(8)]
    stg = [sb("stg%d" % i, [128, 512], BF16) for i in range(2)]
    tst = [sb("tst%d" % i, [128, 4, 128], BF16) for i in range(2)]
    vst = [sb("vst%d" % i, [128, 768], BF16) for i in range(2)]
    s8 = [sb("s8_%d" % i, [128, 4, 8], F32) for i in range(2)]
    kmpad = sb("kmpad", [128, 2, 4, 256], BF16)
    vmpad = sb("vmpad", [128, 2, 4, 2, 128], BF16)
    if 1 in phases and not fused:
        rp = [sb("rp%d" % i, [128, 3, 8, 16], F32) for i in range(1)]
        memT = sb("memT", [128, 8, 256], BF16)
        posi = sb("posi", [128, 16], I32)
        rt = [sb("rt%d" % i, [128, 16, 8], F32) for i in range(3)]
        rti = sb("rti", [128, 16, 8], I32)
        pb = ab = acc = qpad = None
    else:
        pb = [sb("pb%d" % i, [128, 2, 512], BF16) for i in range(2)]
        ab = [sb("ab%d" % i, [128, 2, 512], BF16) for i in range(2)]
        acc = [sb("acc%d" % i, [128, 2, 512], BF16) for i in range(3)]
        qpad = [sb("qpad%d" % i, [128, 2, 512], BF16) for i in range(2)]
        rp = None

    KTs = [BIG[:, i * 8192:(i + 1) * 8192] for i in range(2)]
    Vs = [BIG[:, 16384:24576].rearrange("p (t d) -> p t d", d=128) for i in range(2)]
    Wst = [BIG[:, i * 8192:(i + 1) * 8192].bitcast(F32).rearrange("p (k n) -> p k n", k=8) for i in range(2)]
    Wb = [BIG[:, 16384 + i * 4096:16384 + (i + 1) * 4096].rearrange("p (k n) -> p k n", k=8) for i in range(2)]

    PS = [nc.psum_tensor("ps%d" % i, [128, 512], F32).__enter__() for i in range(8)]

    def psbf(i):
        return PS[i][:].bitcast(BF16)

    x_d = din("x", [NT, 1024], F32) if (1 in phases or 2 in phases) else None
    cst_d = din("cst", [128, 2, 128], F32)
    if 1 in phases:
        pos_d = din("pos", [128, 16], I32)
        invf_d = din("invf", [128, 8], F32)
        mem_d = din("mem", [256, 1024], F32)
        w_ain = din("w_ain", [1024, 3584], F32)
        w_memkv = din("w_memkv", [2, 1024, 512], F32)
    gains_d = din("gains", [128, 5, 8], F32)
    hg_d = din("hg", [128, 6, 64], F32)
    mk_d = din("mk", [128, 2, 4, 128], F32)
    if 2 in phases:
        lamv_d = din("lamv", [128, 4, 64], F32)
        subln_d = din("subln", [128, 1], F32)
        w_aout = din("w_aout", [1024, 1024], F32)
        w_kv = din("w_kv", [1024, 1536], F32)
        w_bin = din("w_bin", [1024, 2048], F32)
    if 3 in phases:
        w_bout = din("w_bout", [1024, 1024], F32)

    def exch(layer):
        if fused:
            own_k = dint("kown%d" % layer, [6, 128, 2048], BF16)
            own_v = dint("vown%d" % layer, [6, 128, 16, 128], BF16)
            all_k = dint("kall%d" % layer, [4, 6, 128, 2048], BF16)
            all_v = dint("vall%d" % layer, [4, 6, 128, 16, 128], BF16)
            return own_k, own_v, all_k, all_v
        own_k = own_v = all_k = all_v = None
        prod = 1 if layer == 0 else 2
        cons = 2 if layer == 0 else 3
        if prod in phases:
            own_k = dout("kown%d" % layer, [6, 128, 2048], BF16)
            own_v = dout("vown%d" % layer, [6, 128, 16, 128], BF16)
        if cons in phases:
            all_k = din("kall%d" % layer, [4, 6, 128, 2048], BF16)
            all_v = din("vall%d" % layer, [4, 6, 128, 16, 128], BF16)
        return own_k, own_v, all_k, all_v

    ex0 = exch(0)
    ex1 = exch(1)
    if fused:
        x1_d = dint("x1", [NT, 1024], F32)
    else:
        x1_d = None
        if 2 in phases:
            x1_d = dout("x1", [NT, 1024], F32)
        if 3 in phases:
            x1_d = din("x1", [NT, 1024], F32)
    out_d = dout("out", [NT, 1024], F32) if 3 in phases else None

    states = {
        "QT": (QT, [128, 6, 2048], BF16, [("QT", p_, t_) for p_ in range(6) for t_ in range(16)]),
        "gateT": (gateT, [128, 8, 2048], BF16, [("gateT", p_, j_) for p_ in range(8) for j_ in range(4)]),
        "qmT": (qmT, [128, 2, 2048], BF16, [("qmT", p_, t_) for p_ in range(2) for t_ in range(16)]),
        "kmpad": (kmpad, [128, 2, 4, 256], BF16, [("kmpad",)]),
        "vmpad": (vmpad, [128, 2, 4, 2, 128], BF16, [("vmpad",)]),
    }

    def load_state(name):
        t, shape, dt, keys = states[name]
        d = din("st_" + name, shape, dt)
        P.op("sp", "dma_start", out=t[:], in_=d, w=keys, dma="st_" + name)

    def store_state(name):
        t, shape, dt, keys = states[name]
        d = dout("so_" + name, shape, dt)
        P.op("sp", "dma_start", out=d, in_=t[:], r=keys, dma="so")

    P.op("sp", "dma_start", out=cst32[:], in_=cst_d, w=[("cst32",)], dma="c0")
    P.op("sp", "dma_start", out=gains[:], in_=gains_d, w=[("gains",)], dma="c0")
    P.op("sp", "dma_start", out=hg[:], in_=hg_d, w=[("hg",)], dma="c0")
    P.op("sp", "dma_start", out=mk[:], in_=mk_d, w=[("mk",)], dma="c0")
    P.op("dve", "tensor_copy", ident[:], cst32[:, 0, :], r=[("cst32",)], w=[("ident",)])
    P.op("dve", "tensor_copy", tri[:], cst32[:, 1, :], r=[("cst32",)], w=[("tri",)])
    P.op("dve", "memset", ones[:], 1.0, w=[("ones",)])
    P.op("dve", "memset", zer[:], 0.0, w=[("zer",)])

    cnt = [0]
    tc_cnt = [0]

    def uid():
        cnt[0] += 1
        return cnt[0]

    def load_weights(w_ap, col0, ncols, gidx, slot):
        src = w_ap[:, col0:col0 + ncols].rearrange("(k p) n -> p k n", p=128)
        P.op("pool", "dma_start", out=Wst[slot][:, :, 0:ncols], in_=src,
              w=[("KTs", slot)], dma="w%d" % slot)
        for kc in range(8):
            if kc % 2 == 0:
                P.op("act", "activation", out=Wb[slot][:, kc, 0:ncols], in_=Wst[slot][:, kc, 0:ncols],
                                                           func=AF.Copy, scale=gains[:, gidx, kc:kc + 1],
                      r=[("KTs", slot), ("gains",)], w=[("Wb", slot, kc), ("VsW", slot)])
            else:
                P.op("dve", "tensor_scalar", Wb[slot][:, kc, 0:ncols], Wst[slot][:, kc, 0:ncols],
                                                              gains[:, gidx, kc:kc + 1], None, ALU.mult,
                      r=[("KTs", slot), ("gains",)], w=[("Wb", slot, kc), ("VsW", slot)])

    def wb_keys(slot):
        return [("Wb", slot, kc) for kc in range(8)] + [("Vs", slot)]

    def wb_wkeys(slot):
        return [("Vs", slot)]

    def rstd_from_ss(out_ap, ss_ap, n, rkeys, wkeys):
        t = uid()
        P.op("act", "activation", out=out_ap, in_=ss_ap, func=AF.Ln, scale=1.0 / n, bias=EPS,
              r=rkeys, w=wkeys)
        P.op("act", "activation", out=out_ap, in_=out_ap, func=AF.Exp, scale=-0.5,
              r=wkeys, w=wkeys)

    def norm_tile_to_T(src_ap_fn, src_keys, slot, dstT, dkey, tt, ncols_tok=128, width=2048):
        xs = xt[slot]
        P.op("act", "activation", out=sq[:], in_=xs[:], func=AF.Square, r=[("xt", slot)], w=SQK)
        ssk = ("ssx", slot)
        P.op("dve", "tensor_reduce", s8[slot][:, 0, 0:1], sq[:], AX.X, ALU.add, r=SQK, w=[ssk])
        rstd_from_ss(s8[slot][:, 0, 1:2], s8[slot][:, 0, 0:1], 1024.0, [ssk], [("rsx", slot)])
        P.op("dve", "tensor_scalar", xnb[slot][:], xs[:], s8[slot][:, 0, 1:2], None, ALU.mult,
              r=[("xt", slot), ("rsx", slot)], w=[("xnb", slot)])
        for half in range(2):
            bank = 6 + half
            for q4 in range(4):
                kc = half * 4 + q4
                P.op("pe", "transpose", psbf(bank)[:, q4 * 128:(q4 + 1) * 128], xnb[slot][:, kc * 128:(kc + 1) * 128], ident[:],
                    r=[("xnb", slot), ("ident",)], w=[("ps", bank)])
            eng = "act" if half == 0 else "dve"
            dst = dstT[:, half * 4:half * 4 + 4, tt * ncols_tok:(tt + 1) * ncols_tok]
            srcp = psbf(bank)[:, 0:512].rearrange("p (a b) -> p a b", b=128)
            if eng == "act":
                P.op("act", "activation", out=dst, in_=srcp, func=AF.Copy,
                      r=[("ps", bank)], w=[(dkey, kc_, tt) for kc_ in range(half * 4, half * 4 + 4)])
            else:
                P.op("dve", "tensor_copy", dst, srcp,
                      r=[("ps", bank)], w=[(dkey, kc_, tt) for kc_ in range(half * 4, half * 4 + 4)])

    def headnorm(bank, slot, hb0, nhb, gain_idx, out_stg, rope, tt):
        c0, c1 = hb0 * 64, (hb0 + nhb) * 64
        ps3 = PS[bank][:, c0:c1].rearrange("p (h d) -> p h d", d=64)
        sq3 = w32[0][:, c0:c1]
        hbs = range(hb0, hb0 + nhb)
        kq = [("w32", 0, hb) for hb in hbs]
        P.op("act", "activation", out=sq3, in_=PS[bank][:, c0:c1], func=AF.Square, r=[("ps", bank)], w=kq)
        kss = [("s8", slot, hb) for hb in hbs]
        ssap = s8[slot][:, 1, hb0:hb0 + nhb]
        rsap = s8[slot][:, 2, hb0:hb0 + nhb]
        P.op("dve", "tensor_reduce", ssap, sq3.rearrange("p (h d) -> p h d", d=64), AX.X, ALU.add,
              r=kq, w=kss)
        krs = [("s8r", slot, hb) for hb in hbs]
        rstd_from_ss(rsap, ssap, 64.0, kss, krs)
        xn3 = w32[1][:, c0:c1].rearrange("p (h d) -> p h d", d=64)
        kxn = [("w32", 1, hb) for hb in hbs]
        P.op("dve", "tensor_tensor", xn3, ps3, rsap.unsqueeze(2).broadcast_to([128, nhb, 64]), ALU.mult,
              r=[("ps", bank)] + krs, w=kxn)
        o3 = out_stg[:, c0:c1].rearrange("p (h d) -> p h d", d=64)
        g3 = hg[:, gain_idx, :].unsqueeze(1).broadcast_to([128, nhb, 64])
        kst = [("stg", slot, hb) for hb in hbs]
        P.op("dve", "tensor_tensor", o3, xn3, g3, ALU.mult, r=kxn + [("hg",)], w=kst)
        if rope:
            R = rp[0]
            kr = ("rp",)
            xg = R[:, 0, hb0:hb0 + nhb, :]
            g16 = hg[:, gain_idx, 0:16].unsqueeze(1).broadcast_to([128, nhb, 16])
            P.op("dve", "tensor_tensor", xg, xn3[:, :, 0:16], g16, ALU.mult, r=kxn + [("hg",)], w=[kr])
            cs = cosT[:, tt, :].unsqueeze(1).broadcast_to([128, nhb, 8])
            sn = sinT[:, tt, :].unsqueeze(1).broadcast_to([128, nhb, 8])
            x1 = xg[:, :, 0:8]
            x2 = xg[:, :, 8:16]
            t1 = R[:, 1, hb0:hb0 + nhb, 0:8]
            t2 = R[:, 1, hb0:hb0 + nhb, 8:16]
            t3 = R[:, 2, hb0:hb0 + nhb, 0:8]
            t4 = R[:, 2, hb0:hb0 + nhb, 8:16]
            P.op("dve", "tensor_tensor", t1, x1, cs, ALU.mult, r=[kr, ("rope",)], w=[("rp1",)])
            P.op("dve", "tensor_tensor", t2, x2, sn, ALU.mult, r=[kr, ("rope",)], w=[("rp2",)])
            P.op("dve", "tensor_tensor", t3, x2, cs, ALU.mult, r=[kr, ("rope",)], w=[("rp3",)])
            P.op("dve", "tensor_tensor", t4, x1, sn, ALU.mult, r=[kr, ("rope",)], w=[("rp4",)])
            P.op("dve", "tensor_tensor", o3[:, :, 0:8], t1, t2, ALU.subtract,
                  r=[("rp1",), ("rp2",)], w=kst)
            P.op("dve", "tensor_tensor", o3[:, :, 8:16], t3, t4, ALU.add,
                  r=[("rp3",), ("rp4",)], w=kst)
        return kst

    def transpose_chunks(slot, src_stg, chunk_list, src_keys, tbank):
        for i, (ci, dst, dk) in enumerate(chunk_list):
            P.op("pe", "transpose", psbf(tbank)[:, i * 128:(i + 1) * 128],
                                                          src_stg[:, ci * 128:(ci + 1) * 128], ident[:],
                  r=list(src_keys) + [("ident",)], w=[("ps", tbank)])
        tc_cnt[0] += 1
        for i, (ci, dst, dk) in enumerate(chunk_list):
            eng = "act" if tc_cnt[0] % 2 == 0 else "dve"
            if eng == "act":
                P.op("act", "activation", out=dst, in_=psbf(tbank)[:, i * 128:(i + 1) * 128], func=AF.Copy,
                      r=[("ps", tbank)], w=dk)
            else:
                P.op("dve", "tensor_copy", dst, psbf(tbank)[:, i * 128:(i + 1) * 128],
                      r=[("ps", tbank)], w=dk)

    def proj_tok(slot_w, bank, tt, ncols, srcT, skey):
        for kc in range(8):
            P.op("pe", "matmul", PS[bank][:, 0:ncols], srcT[:, kc, tt * 128:(tt + 1) * 128],
                                                  Wb[slot_w][:, kc, 0:ncols], start=(kc == 0), stop=(kc == 7),
                  r=[(skey, kc, tt), ("Wb", slot_w, kc), ("VsW", slot_w)], w=[("ps", bank)])

    def proj_feat(slot_w, bank, chunk, j, srcT, skey):
        for kc in range(8):
            P.op("pe", "matmul", PS[bank][:, :], Wb[slot_w][:, kc, chunk * 128:(chunk + 1) * 128],
                                                  srcT[:, kc, j * 512:(j + 1) * 512], start=(kc == 0), stop=(kc == 7),
                  r=[(skey, kc, t_) for t_ in range(4 * j, 4 * j + 4)] + [("Wb", slot_w, kc), ("VsW", slot_w)], w=[("ps", bank)])

    def silu_evac(bank, dst, dkeys, wi):
        e = w32[wi][:, :]
        ke = ("w32", wi)
        P.op("act", "activation", out=e, in_=PS[bank][:, :], func=AF.Exp, scale=-1.0, r=[("ps", bank)], w=[ke])
        P.op("dve", "tensor_scalar", e, e, 1.0, None, ALU.add, r=[ke], w=[ke])
        P.op("dve", "reciprocal", e, e, r=[ke], w=[ke])
        P.op("dve", "tensor_tensor", dst, PS[bank][:, :], e, ALU.mult, r=[ke, ("ps", bank)], w=dkeys)

    def phase1():
        kown, vown = ex0[0], ex0[1]
        P.op("sp", "dma_start", out=posi[:], in_=pos_d, w=[("posi",)], dma="c1")
        P.op("sp", "dma_start", out=invf[:], in_=invf_d, w=[("invf",)], dma="c1")
        pf = sml
        posf = rt[2][:, :, 0]
        P.op("dve", "tensor_copy", posf, posi[:], r=[("posi",)], w=[("posf",)])
        ang = rt[0]
        P.op("dve", "tensor_tensor", ang[:], posf.unsqueeze(2).broadcast_to([128, 16, 8]),
                                               invf[:].unsqueeze(1).broadcast_to([128, 16, 8]), ALU.mult,
              r=[("posf",), ("invf",)], w=[("ang",)])
        for which, shift, dst in (("s", 0.0, sinT), ("c", math.pi / 2, cosT)):
            a2 = rt[1]
            k2 = ("a2",)
            P.op("dve", "tensor_scalar", a2[:], ang[:], shift, None, ALU.add, r=[("ang",)], w=[k2])
            kfl = rt[2]
            P.op("dve", "tensor_scalar", kfl[:], a2[:], 1.0 / (2 * math.pi), None, ALU.mult, r=[k2], w=[("kfl",), ("posf",)])
            P.op("dve", "tensor_copy", rti[:], kfl[:], r=[("kfl",)], w=[("rti",)])
            P.op("dve", "tensor_copy", kfl[:], rti[:], r=[("rti",)], w=[("kfl",)])
            P.op("dve", "scalar_tensor_tensor", a2[:], kfl[:], -2 * math.pi, a2[:], ALU.mult, ALU.add,
                  r=[("kfl",), k2], w=[k2])
            P.op("dve", "tensor_scalar", a2[:], a2[:], 3.1415925, -3.1415925, ALU.min, ALU.max, r=[k2], w=[k2])
            P.op("act", "activation", out=dst[:], in_=a2[:], func=AF.Sin, r=[k2], w=[("rope",)])
            if which == "s":
                pass

        for mt in range(2):
            slot = mt
            P.op("sp", "dma_start", out=xt[slot][:], in_=mem_d[mt * 128:(mt + 1) * 128, :],
                  w=[("xt", slot)], dma="xt%d" % slot)
            norm_tile_to_T(None, None, slot, memT, "memT", mt, ncols_tok=128)
        for l in range(2):
            slotw = l % 2
            src = w_memkv[l].rearrange("(k p) n -> p k n", p=128)
            P.op("pool", "dma_start", out=Wst[slotw][:, :, :], in_=src,
                  w=[("KTs", slotw)], dma="w%d" % slotw)
            for kc in range(8):
                P.op("dve", "tensor_scalar", Wb[slotw][:, kc, :], Wst[slotw][:, kc, :], gains[:, 3 + l, kc:kc + 1], None, ALU.mult,
                    r=[("KTs", slotw), ("gains",)], w=[("Wb", slotw, kc), ("VsW", slotw)])
            P.op("dve", "memset", kmpad[:, l], 0.0, w=[("kmpad",)])
            P.op("dve", "memset", vmpad[:, l], 0.0, w=[("vmpad",)])
            for mt in range(2):
                bank = mt
                proj_tok(slotw, bank, mt, 512, memT, "memT")
                sl = mt
                kst = headnorm(bank, sl, 0, 4, 3 + 2 * l, stg[sl], False, 0)
                for h in range(4):
                    dst = vmpad[:, l, h, mt, (h % 2) * 64:(h % 2) * 64 + 64]
                    P.op("dve", "tensor_copy", dst, PS[bank][:, 256 + h * 64:256 + (h + 1) * 64],
                        r=[("ps", bank)], w=[("vmpad",)])
                chunk_list = []
                for ci in range(2):
                    chunk_list.append((ci, tst[sl][:, ci, :], [("tst", sl, ci)]))
                transpose_chunks(sl, stg[sl], chunk_list, kst, 6 + mt)
                for h in (3, 2, 1, 0):
                    r0 = (h % 2) * 64
                    P.op("dve", "tensor_copy", kmpad[r0:r0 + 64, l, h, mt * 128:(mt + 1) * 128], tst[sl][r0:r0 + 64, h // 2, :],
                        r=[("tst", sl, h // 2)], w=[("kmpad",)])

        for tt in range(16):
            slot = tt % 2
            P.op("sp", "dma_start", out=xt[slot][:], in_=x_d[tt * 128:(tt + 1) * 128, :],
                  w=[("xt", slot)], dma="xt%d" % slot)
            norm_tile_to_T(None, None, slot, hy, "hy", tt)

        groups = [
            (0, [("q", 0, 8, 0)]),
            (512, [("q", 0, 4, 4), ("k", 4, 4, 0)]),
            (1024, [("k", 0, 8, 2)]),
            (1536, [("v", 0, 512, 0)]),
            (2048, [("v", 0, 256, 512), ("qm", 4, 4, 0)]),
        ]
        for gi, (col0, parts) in enumerate(groups):
            slotw = gi % 2
            load_weights(w_ain, col0, 512, 0, slotw)
            proj_tok(slotw, 0, 0, 512, hy, "hy")
            for tt in range(16):
                bank = tt % 2
                sl = tt % 2
                if tt + 1 < 16:
                    proj_tok(slotw, (tt + 1) % 2, tt + 1, 512, hy, "hy")
                chunk_list = []
                skeys = []
                for part in parts:
                    kind = part[0]
                    if kind in ("q", "k"):
                        _, hb0, nhb, pair0 = part
                        kst = headnorm(bank, sl, hb0, nhb, 0 if kind == "q" else 1, stg[sl], True, tt)
                        skeys.extend(kst)
                        for ci in range(nhb // 2):
                            pair = pair0 + ci
                            if kind == "q":
                                chunk_list.append((hb0 // 2 + ci, QT[:, pair, tt * 128:(tt + 1) * 128], [("QT", pair, tt)]))
                            else:
                                chunk_list.append((hb0 // 2 + ci, tst[sl][:, hb0 // 2 + ci, :], [("tst", sl, hb0 // 2 + ci)]))
                    elif kind == "qm":
                        _, hb0, nhb, _ = part
                        kst = headnorm(bank, sl, hb0, nhb, 2, stg[sl], False, tt)
                        skeys.extend(kst)
                        for ci in range(2):
                            chunk_list.append((hb0 // 2 + ci, qmT[:, ci, tt * 128:(tt + 1) * 128], [("qmT", ci, tt)]))
                    else:
                        _, c0, nc_, vc0 = part
                        P.op("act", "activation", out=vst[sl][:, vc0:vc0 + nc_], in_=PS[bank][:, c0:c0 + nc_], func=AF.Copy,
                            r=[("ps", bank)], w=[("vst", sl, vc0)])
                        dst = vown[vc0 // 128:(vc0 + nc_) // 128, :, tt, :].rearrange("h p d -> p h d")
                        P.op("sp", "dma_start", out=dst, in_=vst[sl][:, vc0:vc0 + nc_].rearrange("p (h d) -> p h d", d=128),
                            r=[("vst", sl, vc0)], w=[("vown", tt, vc0)], dma="vo")
                if chunk_list:
                    transpose_chunks(sl, stg[sl], chunk_list, skeys, 6 + sl)
                    for (ci, dst, dk) in chunk_list:
                        if dk[0][0] == "tst":
                            kpair = None
                            for part in parts:
                                if part[0] == "k":
                                    kpair = part[3] + (ci - part[1] // 2)
                            P.op("sp", "dma_start", out=kown[kpair, :, tt * 128:(tt + 1) * 128], in_=tst[sl][:, ci, :],
                                r=dk, w=[("kown", kpair, tt)], dma="ko")
        for gg in range(2):
            slotw = (5 + gg) % 2
            load_weights(w_ain, 2560 + gg * 512, 512, 0, slotw)
            for ch in range(4):
                for j in range(4):
                    bank = 2 + (ch * 4 + j) % 4
                    proj_feat(slotw, bank, ch, j, hy, "hy")
                    chunk = gg * 4 + ch
                    silu_evac(bank, gateT[:, chunk, j * 512:(j + 1) * 512], [("gateT", chunk, j)], 2 + (ch * 4 + j) % 4)

    def load_kv(all_k, all_v, pair, slot):
        for rnk in range(4):
            P.op("sp", "dma_start", out=KTs[slot][:, rnk * 2048:(rnk + 1) * 2048], in_=all_k[rnk, pair],
                  w=[("KTs", slot)], dma="kt%d" % slot)
            P.op("sp", "dma_start", out=Vs[slot][:, rnk * 16:(rnk + 1) * 16, :], in_=all_v[rnk, pair],
                  w=[("Vs", 0), ("VsW", 0), ("VsW", 1)], dma="vs0")

    def key_steps(j):
        steps = []
        for m in (3, 2, 1, 0):
            for r in (3, 2, 1, 0):
                steps.append((r, 4 * j + m, m, r))
        for g in range(16 * j - 1, -1, -1):
            jj, rem = divmod(g, 16)
            s_, c_ = divmod(rem, 4)
            steps.append((c_, 4 * jj + s_, None, None))
        return steps

    def make_qpad(pair, j, qs, scale):
        P.op("pool", "memset", qpad[qs][:], 0.0, w=[("qpad", qs)])
        for hh in range(2):
            r0 = hh * 64
            if scale == 1.0:
                P.op("pool", "tensor_copy", qpad[qs][r0:r0 + 64, hh, :], QT[r0:r0 + 64, pair, j * 512:(j + 1) * 512],
                      r=[("QT", pair, t_) for t_ in range(4 * j, 4 * j + 4)], w=[("qpad", qs)])
            else:
                P.op("dve", "tensor_scalar", qpad[qs][r0:r0 + 64, hh, :], QT[r0:r0 + 64, pair, j * 512:(j + 1) * 512],
                                                                  scale, None, ALU.mult,
                      r=[("QT", pair, t_) for t_ in range(4 * j, 4 * j + 4)], w=[("qpad", qs)])

    def attention0(all_k, all_v, finalize):
        it = 0
        for pair in range(6):
            slot = pair % 2
            load_kv(all_k, all_v, pair, slot)
            for j in range(4):
                qs = (pair * 4 + j) % 2
                make_qpad(pair, j, qs, 1.0)
                for b in (4, 5, 6, 7):
                    P.op("pe", "matmul", PS[b][:, :], zer[:, 0:128], zer[:, :], start=True, stop=False,
                          r=[("zer",)], w=[("ps", b)])
                steps = key_steps(j)
                prev = None
                for si, (rnk, lt, m, r) in enumerate(steps):
                    buf = it % 2
                    it += 1
                    c0 = 0 if m is None else 128 * m
                    kcol = rnk * 2048 + lt * 128
                    vt = rnk * 16 + lt
                    last = (si == len(steps) - 1)
                    for c in range(2):
                        P.op("pe", "matmul", PS[buf * 2 + c][:, c0:512], KTs[slot][:, kcol:kcol + 128], qpad[qs][:, c, c0:512],
                            start=True, stop=True,
                            r=[("KTs", slot), ("qpad", qs)], w=[("ps", buf * 2 + c)])
                    for c in range(2):
                        P.op("act", "activation", out=pb[buf][:, c, c0:512], in_=PS[buf * 2 + c][:, c0:512], func=AF.Exp, scale=0.125,
                            r=[("ps", buf * 2 + c)], w=[("pb", buf, c)])
                    if m is not None:
                        P.op("dve", "tensor_tensor", pb[buf][:, :, c0:c0 + 128], pb[buf][:, :, c0:c0 + 128],
                            mk[:, 0, r, :].unsqueeze(1).broadcast_to([128, 2, 128]), ALU.mult,
                            r=[("pb", buf, 0), ("pb", buf, 1), ("mk",)], w=[("pb", buf, 0), ("pb", buf, 1)])
                    if prev is not None:
                        emit_pv0(prev, slot, False)
                    prev = (buf, c0, vt)
                emit_pv0(prev, slot, True)
                finalize(pair, j)

    def emit_pv0(prev, slot, last):
        buf, c0, vt = prev
        for c in range(2):
            P.op("pe", "matmul", PS[4 + c][:, c0:512], Vs[slot][:, vt, :], pb[buf][:, c, c0:512],
                                                start=False, stop=last,
                  r=[("Vs", 0), ("VsW", 0), ("VsW", 1), ("pb", buf, c)], w=[("ps", 4 + c)])
        for c in range(2):
            P.op("pe", "matmul", PS[6 + c][:, c0:512], ones[:], pb[buf][:, c, c0:512],
                                                start=False, stop=last,
                  r=[("ones",), ("pb", buf, c)], w=[("ps", 6 + c)])

    def finalize0(pair, j):
        r0, r1, t0, t1 = w32[2], w32[3], w32[4], w32[5]
        P.op("dve", "reciprocal", r0[:], PS[6][:, :], r=[("ps", 6)], w=[("w32", 2)])
        P.op("dve", "reciprocal", r1[:], PS[7][:, :], r=[("ps", 7)], w=[("w32", 3)])
        P.op("dve", "tensor_tensor", t0[:], PS[4][:, :], r0[:], ALU.mult, r=[("ps", 4), ("w32", 2)], w=[("w32", 4)])
        P.op("dve", "tensor_tensor", t1[:], PS[5][:, :], r1[:], ALU.mult, r=[("ps", 5), ("w32", 3)], w=[("w32", 5)])
        P.op("dve", "scalar_tensor_tensor", t0[:], t1[:], sml[:, 2:3], t0[:], ALU.mult, ALU.add,
              r=[("w32", 4), ("w32", 5), ("sml",)], w=[("w32", 4)])
        P.op("act", "activation", out=pb[0][:, 0, :], in_=t0[:], func=AF.Square, r=[("w32", 4)], w=[("pb", 0, 0)])
        P.op("pe", "matmul", PS[6][:, :], ones[:], pb[0][:, 0, :], start=True, stop=True,
              r=[("ones",), ("pb", 0, 0)], w=[("ps", 6)])
        P.op("act", "activation", out=r0[:], in_=PS[6][:, :], func=AF.Ln, scale=1.0 / 128, bias=EPS, r=[("ps", 6)], w=[("w32", 2)])
        P.op("act", "activation", out=r0[:], in_=r0[:], func=AF.Exp, scale=-0.5, r=[("w32", 2)], w=[("w32", 2)])
        P.op("dve", "scalar_tensor_tensor", t0[:], t0[:], sml[:, 1:2], r0[:], ALU.mult, ALU.mult,
              r=[("w32", 4), ("w32", 2), ("sml",)], w=[("w32", 4)])
        P.op("dve", "tensor_tensor", hy[:, pair, j * 512:(j + 1) * 512], t0[:], gateT[:, pair, j * 512:(j + 1) * 512], ALU.mult,
              r=[("w32", 4), ("gateT", pair, j)], w=[("hy", pair, t_) for t_ in range(4 * j, 4 * j + 4)])

    def attention1(all_k, all_v, finalize):
        for pair in range(6):
            slot = pair % 2
            load_kv(all_k, all_v, pair, slot)
            for j in range(4):
                qs = (pair * 4 + j) % 2
                make_qpad(pair, j, qs, 0.125)
                for b in (6, 7):
                    P.op("pe", "matmul", PS[b][:, :], zer[:, 0:128], zer[:, :], start=True, stop=False,
                          r=[("zer",)], w=[("ps", b)])
                steps = key_steps(j)
                n = len(steps)
                P.op("pool", "memset", acc[0][:], 0.0, w=[("acc", 0)])

                def info(si):
                    rnk, lt, m, r = steps[si]
                    c0 = 0 if m is None else 128 * m
                    kcol = rnk * 2048 + lt * 128
                    return c0, KTs[slot][:, kcol:kcol + 128], rnk * 16 + lt, m, r

                def stageA(si):
                    c0, ksl, vt, m, r = info(si)
                    buf = si % 2
                    for hh in range(2):
                        zb = buf * 2 + hh
                        P.op("pe", "matmul", PS[zb][:, c0:512], ksl, qpad[qs][:, hh, c0:512], start=True, stop=True,
                              r=[("KTs", slot), ("qpad", qs)], w=[("ps", zb)])
                    for hh in range(2):
                        zb = buf * 2 + hh
                        P.op("act", "activation", out=PS[zb][:, c0:512], in_=PS[zb][:, c0:512], func=AF.Exp,
                              r=[("ps", zb)], w=[("ps", zb)])
                    for hh in range(2):
                        zb = buf * 2 + hh
                        P.op("act", "activation", out=pb[buf][:, hh, c0:512], in_=PS[zb][:, c0:512], func=AF.Ln, bias=1.0,
                              r=[("ps", zb)], w=[("pb", buf, hh)])
                    if m is not None:
                        P.op("dve", "tensor_tensor", pb[buf][:, :, c0:c0 + 128], pb[buf][:, :, c0:c0 + 128],
                              mk[:, 1, r, :].unsqueeze(1).broadcast_to([128, 2, 128]), ALU.mult,
                              r=[("pb", buf, 0), ("pb", buf, 1), ("mk",)], w=[("pb", buf, 0), ("pb", buf, 1)])
                    if si < n - 1:
                        cur, nxt = si % 3, (si + 1) % 3
                        if c0 > 0:
                            P.op("pool", "memset", acc[nxt][:, :, 0:c0], 0.0, w=[("acc", nxt)])
                        P.op("dve", "tensor_tensor", acc[nxt][:, :, c0:512], acc[cur][:, :, c0:512], pb[buf][:, :, c0:512], ALU.subtract,
                              r=[("acc", cur), ("pb", buf, 0), ("pb", buf, 1)], w=[("acc", nxt)])

                def stageB(si):
                    c0, ksl, vt, m, r = info(si)
                    buf = si % 2
                    cur = si % 3
                    for hh in range(2):
                        cb = 4 + hh
                        P.op("pe", "matmul", PS[cb][:, c0:512], ksl, qpad[qs][:, hh, c0:512], start=True, stop=False,
                              r=[("KTs", slot), ("qpad", qs)], w=[("ps", cb)])
                        P.op("pe", "matmul", PS[cb][:, c0:512], tri[:], pb[buf][:, hh, c0:512], start=False, stop=(si == 0),
                              r=[("tri",), ("pb", buf, hh)], w=[("ps", cb)])
                        if si > 0:
                            P.op("pe", "matmul", PS[cb][:, c0:512], ones[:], acc[cur][:, hh, c0:512], start=False, stop=True,
                                  r=[("ones",), ("acc", cur)], w=[("ps", cb)])
                    for hh in range(2):
                        cb = 4 + hh
                        P.op("act", "activation", out=ab[buf][:, hh, c0:512], in_=PS[cb][:, c0:512], func=AF.Exp,
                              r=[("ps", cb)], w=[("ab", buf, hh)])
                    if m is not None:
                        P.op("dve", "tensor_tensor", ab[buf][:, :, c0:c0 + 128], ab[buf][:, :, c0:c0 + 128],
                              mk[:, 1, r, :].unsqueeze(1).broadcast_to([128, 2, 128]), ALU.mult,
                              r=[("ab", buf, 0), ("ab", buf, 1), ("mk",)], w=[("ab", buf, 0), ("ab", buf, 1)])

                def stagePV(si, last):
                    c0, ksl, vt, m, r = info(si)
                    buf = si % 2
                    for hh in range(2):
                        P.op("pe", "matmul", PS[6 + hh][:, c0:512], Vs[slot][:, vt, :], ab[buf][:, hh, c0:512],
                              start=False, stop=last,
                              r=[("Vs", 0), ("VsW", 0), ("VsW", 1), ("ab", buf, hh)], w=[("ps", 6 + hh)])

                stageA(0)
                for si in range(n):
                    if si + 1 < n:
                        stageA(si + 1)
                    stageB(si)
                    if si > 0:
                        stagePV(si - 1, False)
                stagePV(n - 1, True)
                finalize(pair, j)

    def emit_pv1(prev, slot, last):
        buf, c0, vt = prev
        for hh in range(2):
            P.op("pe", "matmul", PS[6 + hh][:, c0:512], Vs[slot][:, vt, :], ab[buf][:, hh, c0:512],
                                                  start=False, stop=last,
                  r=[("Vs", 0), ("VsW", 0), ("VsW", 1), ("ab", buf, hh)], w=[("ps", 6 + hh)])

    def finalize1(pair, j):
        for hh in range(2):
            r0 = hh * 64
            P.op("dve", "tensor_tensor", hy[r0:r0 + 64, pair, j * 512:(j + 1) * 512], PS[6 + hh][r0:r0 + 64, :],
                gateT[r0:r0 + 64, pair, j * 512:(j + 1) * 512], ALU.mult,
                r=[("ps", 6 + hh), ("gateT", pair, j)], w=[("hy", pair, t_) for t_ in range(4 * j, 4 * j + 4)])

    def mem_attention(l):
        for ci in range(2):
            for j in range(4):
                P.op("pe", "matmul", PS[3][:, :], zer[:, 0:128], zer[:, :], start=True, stop=False,
                      r=[("zer",)], w=[("ps", 3)])
                for hh in range(2):
                    h = ci * 2 + hh
                    buf = hh
                    for mt in range(2):
                        P.op("pe", "matmul", PS[mt][:, :], kmpad[:, l, h, mt * 128:(mt + 1) * 128],
                                                                   qmT[:, ci, j * 512:(j + 1) * 512], start=True, stop=True,
                              r=[("kmpad",)] + [("qmT", ci, t_) for t_ in range(4 * j, 4 * j + 4)], w=[("ps", mt)])
                        P.op("act", "activation", out=pb[buf][:, mt, :], in_=PS[mt][:, :], func=AF.Exp, scale=0.125,
                              r=[("ps", mt)], w=[("pb", buf, mt)])
                    for mt in range(2):
                        P.op("pe", "matmul", PS[2][:, :], ones[:], pb[buf][:, mt, :], start=(mt == 0), stop=(mt == 1),
                              r=[("ones",), ("pb", buf, mt)], w=[("ps", 2)])
                    rl = w32[2 + hh]
                    P.op("dve", "reciprocal", rl[:], PS[2][:, :], r=[("ps", 2)], w=[("w32", 2 + hh)])
                    P.op("dve", "tensor_tensor", ab[buf][:, :, :], pb[buf][:, :, :],
                                                                           rl[:].unsqueeze(1).broadcast_to([128, 2, 512]), ALU.mult,
                          r=[("w32", 2 + hh), ("pb", buf, 0), ("pb", buf, 1)], w=[("ab", buf, 0), ("ab", buf, 1)])
                    for mt in range(2):
                        P.op("pe", "matmul", PS[3][:, :], vmpad[:, l, h, mt, :], ab[buf][:, mt, :],
                                                                                   start=False, stop=(hh == 1 and mt == 1),
                              r=[("vmpad",), ("ab", buf, mt)], w=[("ps", 3)])
                ch = 6 + ci
                P.op("dve", "tensor_tensor", hy[:, ch, j * 512:(j + 1) * 512], PS[3][:, :],
                                                              gateT[:, ch, j * 512:(j + 1) * 512], ALU.mult,
                      r=[("ps", 3), ("gateT", ch, j)], w=[("hy", ch, t_) for t_ in range(4 * j, 4 * j + 4)])

    def out_proj(w_ap, res_d, dst_d, after_tile, res_key, dst_key):
        for half in range(2):
            src = w_ap[:, half * 512:(half + 1) * 512].rearrange("(k p) n -> p k n", p=128)
            P.op("pool", "dma_start", out=Wst[half][:, :, :], in_=src,
                  w=[("KTs", half)], dma="w%d" % half)
            for kc in range(8):
                eng = "act" if kc % 2 == 0 else "dve"
                if eng == "act":
                    P.op("act", "activation", out=Wb[half][:, kc, :], in_=Wst[half][:, kc, :], func=AF.Copy,
                          r=[("KTs", half)], w=[("Wb", half, kc), ("VsW", half)])
                else:
                    P.op("dve", "tensor_copy", Wb[half][:, kc, :], Wst[half][:, kc, :],
                          r=[("KTs", half)], w=[("Wb", half, kc), ("VsW", half)])
        def op_mm(tt):
            for half in range(2):
                bank = (tt % 2) * 2 + half
                for kc in range(8):
                    P.op("pe", "matmul", PS[bank][:, :], hy[:, kc, tt * 128:(tt + 1) * 128], Wb[half][:, kc, :], start=(kc == 0), stop=(kc == 7),
                        r=[("hy", kc, tt), ("Wb", half, kc), ("VsW", half)], w=[("ps", bank)])

        op_mm(0)
        for tt in range(16):
            slot = tt % 2
            P.op("sp", "dma_start", out=xt[slot][:], in_=res_d[tt * 128:(tt + 1) * 128, :],
                  r=[("dram", res_key, tt)], w=[("xt", slot)], dma="xt%d" % slot)
            if tt + 1 < 16:
                op_mm(tt + 1)
            for half in range(2):
                bank = (tt % 2) * 2 + half
                P.op("dve", "tensor_tensor", xt[slot][:, half * 512:(half + 1) * 512], xt[slot][:, half * 512:(half + 1) * 512], PS[bank][:, :], ALU.add,
                    r=[("ps", bank), ("xt", slot)], w=[("xt", slot)])
            P.op("sp", "dma_start", out=dst_d[tt * 128:(tt + 1) * 128, :], in_=xt[slot][:],
                  r=[("xt", slot)], w=[("dram", dst_key, tt)], dma="od")
            if after_tile is not None:
                after_tile(tt, slot)

    def phase2():
        kown1, vown1 = ex1[0], ex1[1]
        P.op("sp", "dma_start", out=lamv[:], in_=lamv_d, w=[("lamv",)], dma="c2")
        P.op("sp", "dma_start", out=sml[:, 0:1], in_=subln_d, w=[("sml",)], dma="c2")
        pr = w32[0][:, 0:128].rearrange("p (a d) -> p a d", d=64)
        P.op("dve", "tensor_tensor", pr[:, 0, :], lamv[:, 0, :], lamv[:, 1, :], ALU.mult, r=[("lamv",)], w=[("w32", 0, 0)])
        P.op("dve", "tensor_tensor", pr[:, 1, :], lamv[:, 2, :], lamv[:, 3, :], ALU.mult, r=[("lamv",)], w=[("w32", 0, 1)])
        P.op("dve", "tensor_reduce", sml[:, 4:6], pr, AX.X, ALU.add, r=[("w32", 0, 0), ("w32", 0, 1)], w=[("sml4",)])
        P.op("act", "activation", out=sml[:, 6:8], in_=sml[:, 4:6], func=AF.Exp, r=[("sml4",)], w=[("sml6",)])
        P.op("dve", "tensor_tensor", sml[:, 2:3], sml[:, 7:8], sml[:, 6:7], ALU.subtract, r=[("sml6",)], w=[("sml2",)])
        P.op("dve", "tensor_scalar", sml[:, 2:3], sml[:, 2:3], -LAMBDA_INIT0, None, ALU.add, r=[("sml2",)], w=[("sml2",)])
        P.op("dve", "tensor_scalar", sml[:, 1:2], sml[:, 0:1], 1.0 - LAMBDA_INIT0, None, ALU.mult,
              r=[("sml",), ("sml2",)], w=[("sml",)])

        attention0(ex0[2], ex0[3], finalize0)
        mem_attention(0)

        def after(tt, slot):
            norm_tile_to_T(None, None, slot, hy, "hy", tt)
        out_proj(w_aout, x_d, x1_d, after, "x", "x1")

        for gi, (col0, ncols) in enumerate(((0, 512), (512, 256))):
            slotw = gi % 2
            load_weights(w_kv, col0, ncols, 1, slotw)
            for ch in range(ncols // 128):
                pair = col0 // 128 + ch
                for j in range(4):
                    bank = 2 + (ch * 4 + j) % 4
                    proj_feat(slotw, bank, ch, j, hy, "hy")
                    sl = (ch * 4 + j) % 2
                    P.op("act", "activation", out=stg[sl][:, :], in_=PS[bank][:, :], func=AF.Copy,
                          r=[("ps", bank)], w=[("stg", sl, hb) for hb in range(8)])
                    P.op("sp", "dma_start", out=kown1[pair, :, j * 512:(j + 1) * 512], in_=stg[sl][:, :],
                          r=[("stg", sl, hb) for hb in range(8)], w=[("kown1", pair, j)], dma="ko")
        for gi, (col0, ncols, vc0) in enumerate(((768, 512, 0), (1280, 256, 512))):
            slotw = gi % 2
            load_weights(w_kv, col0, ncols, 1, slotw)
            for tt in range(16):
                bank = tt % 2
                sl = tt % 2
                proj_tok(slotw, bank, tt, ncols, hy, "hy")
                P.op("act", "activation", out=vst[sl][:, vc0:vc0 + ncols], in_=PS[bank][:, 0:ncols], func=AF.Copy,
                    r=[("ps", bank)], w=[("vst", sl, vc0)])
                dst = vown1[vc0 // 128:(vc0 + ncols) // 128, :, tt, :].rearrange("h p d -> p h d")
                P.op("sp", "dma_start", out=dst, in_=vst[sl][:, vc0:vc0 + ncols].rearrange("p (h d) -> p h d", d=128),
                    r=[("vst", sl, vc0)], w=[("vown1", tt, vc0)], dma="vo")
        for gi, (col0, ncols) in enumerate(((0, 512), (512, 256))):
            slotw = gi % 2
            load_weights(w_bin, col0, ncols, 2, slotw)
            for ch in range(ncols // 128):
                pair = col0 // 128 + ch
                for j in range(4):
                    bank = 2 + (ch * 4 + j) % 4
                    proj_feat(slotw, bank, ch, j, hy, "hy")
                    if (ch * 4 + j) % 2 == 0:
                        P.op("act", "activation", out=QT[:, pair, j * 512:(j + 1) * 512], in_=PS[bank][:, :], func=AF.Copy,
                              r=[("ps", bank)], w=[("QT", pair, t_) for t_ in range(4 * j, 4 * j + 4)])
                    else:
                        P.op("dve", "tensor_copy", QT[:, pair, j * 512:(j + 1) * 512], PS[bank][:, :],
                              r=[("ps", bank)], w=[("QT", pair, t_) for t_ in range(4 * j, 4 * j + 4)])
        load_weights(w_bin, 768, 256, 2, 0)
        proj_tok(0, 0, 0, 256, hy, "hy")
        for tt in range(16):
            bank = tt % 2
            sl = tt % 2
            if tt + 1 < 16:
                proj_tok(0, (tt + 1) % 2, tt + 1, 256, hy, "hy")
            kst = headnorm(bank, sl, 0, 4, 4, stg[sl], False, tt)
            chunk_list = [(ci, qmT[:, ci, tt * 128:(tt + 1) * 128], [("qmT", ci, tt)]) for ci in range(2)]
            transpose_chunks(sl, stg[sl], chunk_list, kst, 6 + sl)
        for gg in range(2):
            slotw = (1 + gg) % 2
            load_weights(w_bin, 1024 + gg * 512, 512, 2, slotw)
            for ch in range(4):
                for j in range(4):
                    bank = 2 + (ch * 4 + j) % 4
                    proj_feat(slotw, bank, ch, j, hy, "hy")
                    chunk = gg * 4 + ch
                    silu_evac(bank, gateT[:, chunk, j * 512:(j + 1) * 512], [("gateT", chunk, j)], 2 + (ch * 4 + j) % 4)

    def phase3():
        attention1(ex1[2], ex1[3], finalize1)
        mem_attention(1)
        out_proj(w_bout, x1_d, out_d, None, "x1", "out")

    final_groups = []
    if fused:
        raise NotImplementedError
    else:
        if 1 in phases:
            phase1()
            for nm in states:
                store_state(nm)
            final_groups += ["ko", "vo", "so"]
        if 2 in phases:
            for nm in ("QT", "gateT", "qmT", "kmpad", "vmpad"):
                load_state(nm)
            phase2()
            for nm in ("QT", "gateT", "qmT"):
                store_state(nm)
            final_groups += ["ko", "vo", "so", "od"]
        if 3 in phases:
            for nm in ("QT", "gateT", "qmT", "kmpad", "vmpad"):
                load_state(nm)
            phase3()
            final_groups += ["od"]
    P.emit(final_groups)
    return nc, ins_names, outs_names


def _prep_common(inputs):
    f32 = np.float32
    cst = np.zeros((128, 2, 128), f32)
    cst[:, 0, :] = np.eye(128, dtype=f32)
    jj = np.arange(128)[:, None]
    ss = np.arange(128)[None, :]
    cst[:, 1, :] = -(jj >= ss).astype(f32)
    inv = (np.float32(500000.0) ** (-(np.arange(0, 16, 2, dtype=np.float32)) / np.float32(16))).astype(f32)
    invf = np.broadcast_to(inv[None, :], (128, 8)).copy()

    def pk(v):
        return np.ascontiguousarray(np.asarray(v, f32).reshape(8, 128).T)

    gains = np.stack([pk(inputs["a_norm"][0]), pk(inputs["kv_norm"]), pk(inputs["b_norm"][0]),
                      pk(inputs["mem_norm"][0]), pk(inputs["mem_norm"][1])], axis=1)
    hgl = [inputs["a_q_norm"][0], inputs["a_k_norm"][0], inputs["mem_q_norm"][0], inputs["mem_k_norm"][0],
           inputs["mem_q_norm"][1], inputs["mem_k_norm"][1]]
    hg = np.broadcast_to(np.stack([np.asarray(v, f32) for v in hgl], 0)[None], (128, 6, 64)).copy()
    lamv = np.broadcast_to(np.stack([np.asarray(inputs[k][0], f32) for k in
                                     ("a_lambda_q1", "a_lambda_k1", "a_lambda_q2", "a_lambda_k2")], 0)[None], (128, 4, 64)).copy()
    subln = np.asarray(inputs["a_subln"][0], f32).reshape(128, 1).copy()
    w = np.asarray(inputs["a_w_in"][0], f32)
    perm = []
    for base in (0, 768):
        for h in range(6):
            for c in range(2):
                perm.extend(range(base + c * 384 + h * 64, base + c * 384 + h * 64 + 64))
    perm.extend(range(1536, 3584))
    w_ain = np.ascontiguousarray(w[:, perm])
    return dict(cst=cst, invf=invf, gains=gains, hg=hg, lamv=lamv, subln=subln, w_ain=w_ain)


def _masks(c):
    mk = np.zeros((128, 2, 4, 128), np.float32)
    p = np.arange(128)[:, None]
    q = np.arange(128)[None, :]
    for r in range(4):
        if r < c:
            mk[:, :, r, :] = 1.0
        elif r == c:
            mk[:, 0, r, :] = (p <= q)
            mk[:, 1, r, :] = (p < q)
    return mk


_CACHE = {}


def _get(phases, fused):
    key = (tuple(sorted(phases)), fused)
    if key not in _CACHE:
        _CACHE[key] = build(set(phases), fused)
    return _CACHE[key]


def _gather(owns, b):
    return np.stack([owns[b * 4 + c] for c in range(4)], 0)


def kernel(**inputs):
    f32 = np.float32
    com = _prep_common(inputs)
    x = np.asarray(inputs["x"], f32)
    mem = np.asarray(inputs["mem"], f32)
    pos = np.asarray(inputs["positions"], np.int32)
    cores = list(range(8))
    rows = {}
    for core in cores:
        b, c = divmod(core, 4)
        rows[core] = np.concatenate([np.arange(g * 128, (g + 1) * 128) for g in _blocks(c)])
    w_memkv = np.asarray(inputs["mem_w_kv"], f32)
    base = []
    for core in cores:
        b, c = divmod(core, 4)
        d = dict(com)
        d["x"] = np.ascontiguousarray(x[b, rows[core]])
        d["pos"] = np.ascontiguousarray(pos[b, rows[core]].reshape(16, 128).T)
        d["mem"] = mem[b]
        d["w_memkv"] = w_memkv
        d["mk"] = _masks(c)
        d["w_aout"] = np.asarray(inputs["a_w_out"][0], f32)
        d["w_kv"] = np.asarray(inputs["w_kv_shared"], f32)
        d["w_bin"] = np.asarray(inputs["b_w_in"][0], f32)
        d["w_bout"] = np.asarray(inputs["b_w_out"][0], f32)
        base.append(d)

    def run(phases, extra):
        nc, ins_names, outs_names = _get(phases, False)
        maps = []
        for core in cores:
            m = {}
            for nme in ins_names:
                if nme in extra[core]:
                    m["d_" + nme] = extra[core][nme]
                else:
                    m["d_" + nme] = base[core][nme]
            maps.append(m)
        res = run_bass_kernel_spmd(nc, maps, core_ids=cores)
        return [{k[2:]: v for k, v in r.items()} for r in res.results]

    r1 = run([1], [dict() for _ in cores])
    ex = []
    for core in cores:
        b = core // 4
        e = {"kall0": _gather([r["kown0"] for r in r1], b), "vall0": _gather([r["vown0"] for r in r1], b)}
        for nm in ("QT", "gateT", "qmT", "kmpad", "vmpad"):
            e["st_" + nm] = r1[core]["so_" + nm]
        ex.append(e)
    r2 = run([2], ex)
    ex3 = []
    for core in cores:
        b = core // 4
        e = {"kall1": _gather([r["kown1"] for r in r2], b), "vall1": _gather([r["vown1"] for r in r2], b),
             "x1": r2[core]["x1"]}
        for nm in ("QT", "gateT", "qmT"):
            e["st_" + nm] = r2[core]["so_" + nm]
        for nm in ("kmpad", "vmpad"):
            e["st_" + nm] = r1[core]["so_" + nm]
        ex3.append(e)
    r3 = run([3], ex3)
    out = np.zeros((2, 8192, 1024), f32)
    for core in cores:
        b = core // 4
        out[b, rows[core]] = r3[core]["out"]
    return out
```

```python
import math
import numpy as np
import ml_dtypes
import concourse.bass as bass
import concourse.mybir as mybir
from concourse.bass_utils import run_bass_kernel_spmd

F32 = mybir.dt.float32
BF16 = mybir.dt.bfloat16
I32 = mybir.dt.int32
AF = mybir.ActivationFunctionType
ALU = mybir.AluOpType
AX = mybir.AxisListType

NT = 2048
EPS = 1e-6
LAMBDA_INIT0 = 0.8 - 0.6 * math.exp(-0.3 * 0)
SEM_CAP = 20000


class Prog:
    def __init__(self, nc):
        self.nc = nc
        self.ops = []
        self.lw = {}
        self.rd = {}
        self.h = {"pe": nc.tensor, "act": nc.scalar, "dve": nc.vector, "pool": nc.gpsimd, "sp": nc.sync}

    def add(self, eng, fn, r=(), w=(), dma=None):
        idx = len(self.ops)
        deps = set()
        for k in r:
            if k in self.lw:
                deps.add(self.lw[k])
            if k[0] == "ps":
                for x in self.rd.get(k, ()):
                    if self.ops[x][0] != eng:
                        deps.add(x)
        for k in w:
            if k in self.lw:
                deps.add(self.lw[k])
            for x in self.rd.get(k, ()):
                deps.add(x)
        for k in w:
            self.lw[k] = idx
            self.rd[k] = []
        for k in r:
            self.rd.setdefault(k, []).append(idx)
        deps.discard(idx)
        self.ops.append((eng, fn, deps, dma))
        return idx

    def op(self, eng, meth, *args, r=(), w=(), dma=None, **kw):
        return self.add(eng, (meth, args, kw), r, w, dma)

    def emit(self, final_groups=()):
        nc = self.nc
        ops = self.ops
        n = len(ops)
        sig = [False] * n
        for (eng, fn, deps, dma) in ops:
            for d in deps:
                p = ops[d]
                if p[3] is not None:
                    continue
                if p[0] == "pe" and eng == "pe" and dma is None:
                    continue
                sig[d] = True
        import os as _os
        if int(_os.environ.get("KLIMIT", "0")):
            sig = [True] * n
        sems = {}

        def getsem(name):
            if name not in sems:
                sems[name] = nc.semaphore(name).__enter__()
            return sems[name]

        ecount = {}
        sigval = [None] * n
        gcount = {}
        waited = {}
        import os
        limit = int(os.environ.get("KLIMIT", "0")) or n
        for i, (eng, fn, deps, dma) in enumerate(ops):
            if i >= limit:
                break
            E = self.h[eng]
            needs = {}
            for d in deps:
                p = ops[d]
                if p[3] is not None:
                    key = "g_" + p[3]
                    val = gcount[p[3]]
                else:
                    if p[0] == "pe" and eng == "pe" and dma is None:
                        continue
                    key, val = sigval[d]
                if needs.get(key, 0) < val:
                    needs[key] = val
            for key, val in needs.items():
                if waited.get((eng, key), 0) < val:
                    E.wait_ge(getsem(key), val)
                    waited[(eng, key)] = val
            meth, args, kw = fn
            ins = getattr(E, meth)(*args, **kw)
            if dma is not None:
                gcount[dma] = gcount.get(dma, 0) + 16
                ins.then_inc(getsem("g_" + dma), 16)
            elif sig[i]:
                c = ecount.get(eng, 0)
                sname = "e_%s_%d" % (eng, c // SEM_CAP)
                v = c % SEM_CAP + 1
                ecount[eng] = c + 1
                ins.then_inc(getsem(sname), 1)
                sigval[i] = (sname, v)
        sp = self.h["sp"]
        if limit < n:
            for g, v in gcount.items():
                sp.wait_ge(getsem("g_" + g), v)
            for eng_, c_ in ecount.items():
                if c_ > 0:
                    sp.wait_ge(getsem("e_%s_%d" % (eng_, (c_ - 1) // SEM_CAP)), (c_ - 1) % SEM_CAP + 1)
            return
        for g in final_groups:
            sp.wait_ge(getsem("g_" + g), gcount[g])


def _blocks(c):
    return [16 * j + 4 * s + c for j in range(4) for s in range(4)]


def build(phases, fused):
    nc = bass.Bass("TRN2", target_bir_lowering=False, dynamic_dma_scratch_size=2048)
    P = Prog(nc)
    ins_names = []
    outs_names = []

    def din(name, shape, dt):
        ins_names.append(name)
        return nc.dram_tensor("d_" + name, list(shape), dt, kind="ExternalInput").ap()

    def dout(name, shape, dt):
        outs_names.append(name)
        return nc.dram_tensor("d_" + name, list(shape), dt, kind="ExternalOutput").ap()

    def dint(name, shape, dt):
        return nc.dram_tensor("d_" + name, list(shape), dt, kind="Internal").ap()

    def sb(name, shape, dt):
        return nc.sbuf_tensor(name, list(shape), dt).__enter__()

    ident = sb("ident", [128, 128], BF16)
    tri = sb("tri", [128, 128], BF16)
    ones = sb("ones", [128, 128], BF16)
    zer = sb("zer", [128, 512], BF16)
    cst32 = sb("cst32", [128, 2, 128], F32)
    mk = sb("mk", [128, 2, 4, 128], F32)
    gains = sb("gains", [128, 5, 8], F32)
    hg = sb("hg", [128, 6, 64], F32)
    lamv = sb("lamv", [128, 4, 64], F32)
    sml = sb("sml", [128, 16], F32)
    invf = sb("invf", [128, 8], F32)
    cosT = sb("cosT", [128, 16, 8], F32)
    sinT = sb("sinT", [128, 16, 8], F32)
    hy = sb("hy", [128, 8, 2048], BF16)
    QT = sb("QT", [128, 6, 2048], BF16)
    gateT = sb("gateT", [128, 8, 2048], BF16)
    qmT = sb("qmT", [128, 2, 2048], BF16)
    BIG = sb("BIG", [128, 24576], BF16)
    xt = [sb("xt%d" % i, [128, 1024], F32) for i in range(2)]
    xnb = [sb("xnb%d" % i, [128, 1024], BF16) for i in range(2)]
    w32all = sb("w32all", [128, 6, 512], F32)
    w32 = [w32all[:, i, :] for i in range(6)]
    sq = w32all[:, 0:2, :].rearrange("p a b -> p (a b)")
    SQK = [("w32", a_, hb_) for a_ in range(2) for hb_ in range(8)]
    stg = [sb("stg%d" % i, [128, 512], BF16) for i in range(2)]
    tst = [sb("tst%d" % i, [128, 4, 128], BF16) for i in range(2)]
    vst = [sb("vst%d" % i, [128, 768], BF16) for i in range(2)]
    s8 = [sb("s8_%d" % i, [128, 4, 8], F32) for i in range(2)]
    kmpad = sb("kmpad", [128, 2, 4, 256], BF16)
    vmpad = sb("vmpad", [128, 2, 4, 2, 128], BF16)
    if 1 in phases and not fused:
        rp = [sb("rp%d" % i, [128, 3, 8, 16], F32) for i in range(1)]
        memT = sb("memT", [128, 8, 256], BF16)
        posi = sb("posi", [128, 16], I32)
        rt = [sb("rt%d" % i, [128, 16, 8], F32) for i in range(3)]
        rti = sb("rti", [128, 16, 8], I32)
        pb = ab = acc = qpad = None
    else:
        pb = [sb("pb%d" % i, [128, 2, 512], BF16) for i in range(2)]
        ab = [sb("ab%d" % i, [128, 2, 512], BF16) for i in range(2)]
        acc = [sb("acc%d" % i, [128, 2, 512], BF16) for i in range(3)]
        qpad = [sb("qpad%d" % i, [128, 2, 512], BF16) for i in range(2)]
        accl = sb("accl", [128, 2, 512], BF16)
        rp = None

    KTs = [BIG[:, i * 8192:(i + 1) * 8192] for i in range(2)]
    Vs = [BIG[:, 16384:24576].rearrange("p (t d) -> p t d", d=128) for i in range(2)]
    Wst = [BIG[:, i * 8192:(i + 1) * 8192].bitcast(F32).rearrange("p (k n) -> p k n", k=8) for i in range(2)]
    Wb = [BIG[:, 16384 + i * 4096:16384 + (i + 1) * 4096].rearrange("p (k n) -> p k n", k=8) for i in range(2)]

    PS = [nc.psum_tensor("ps%d" % i, [128, 512], F32).__enter__() for i in range(8)]

    def psbf(i):
        return PS[i][:].bitcast(BF16)

    x_d = din("x", [NT, 1024], F32) if (1 in phases or 2 in phases) else None
    cst_d = din("cst", [128, 2, 128], F32)
    if 1 in phases:
        pos_d = din("pos", [128, 16], I32)
        invf_d = din("invf", [128, 8], F32)
        mem_d = din("mem", [256, 1024], F32)
        w_ain = din("w_ain", [1024, 3584], F32)
        w_memkv = din("w_memkv", [2, 1024, 512], F32)
    gains_d = din("gains", [128, 5, 8], F32)
    hg_d = din("hg", [128, 6, 64], F32)
    mk_d = din("mk", [128, 2, 4, 128], F32)
    if 2 in phases:
        lamv_d = din("lamv", [128, 4, 64], F32)
        subln_d = din("subln", [128, 1], F32)
        w_aout = din("w_aout", [1024, 1024], F32)
        w_kv = din("w_kv", [1024, 1536], F32)
        w_bin = din("w_bin", [1024, 2048], F32)
    if 3 in phases:
        w_bout = din("w_bout", [1024, 1024], F32)

    def exch(layer):
        if fused:
            own_k = dint("kown%d" % layer, [6, 128, 2048], BF16)
            own_v = dint("vown%d" % layer, [6, 128, 16, 128], BF16)
            all_k = dint("kall%d" % layer, [4, 6, 128, 2048], BF16)
            all_v = dint("vall%d" % layer, [4, 6, 128, 16, 128], BF16)
            return own_k, own_v, all_k, all_v
        own_k = own_v = all_k = all_v = None
        prod = 1 if layer == 0 else 2
        cons = 2 if layer == 0 else 3
        if prod in phases:
            own_k = dout("kown%d" % layer, [6, 128, 2048], BF16)
            own_v = dout("vown%d" % layer, [6, 128, 16, 128], BF16)
        if cons in phases:
            all_k = din("kall%d" % layer, [4, 6, 128, 2048], BF16)
            all_v = din("vall%d" % layer, [4, 6, 128, 16, 128], BF16)
        return own_k, own_v, all_k, all_v

    ex0 = exch(0)
    ex1 = exch(1)
    if fused:
        x1_d = dint("x1", [NT, 1024], F32)
    else:
        x1_d = None
        if 2 in phases:
            x1_d = dout("x1", [NT, 1024], F32)
        if 3 in phases:
            x1_d = din("x1", [NT, 1024], F32)
    out_d = dout("out", [NT, 1024], F32) if 3 in phases else None

    states = {
        "QT": (QT, [128, 6, 2048], BF16, [("QT", p_, t_) for p_ in range(6) for t_ in range(16)]),
        "gateT": (gateT, [128, 8, 2048], BF16, [("gateT", p_, j_) for p_ in range(8) for j_ in range(4)]),
        "qmT": (qmT, [128, 2, 2048], BF16, [("qmT", p_, t_) for p_ in range(2) for t_ in range(16)]),
        "kmpad": (kmpad, [128, 2, 4, 256], BF16, [("kmpad",)]),
        "vmpad": (vmpad, [128, 2, 4, 2, 128], BF16, [("vmpad",)]),
    }

    def load_state(name):
        t, shape, dt, keys = states[name]
        d = din("st_" + name, shape, dt)
        P.op("sp", "dma_start", out=t[:], in_=d, w=keys, dma="st_" + name)

    def store_state(name):
        t, shape, dt, keys = states[name]
        d = dout("so_" + name, shape, dt)
        P.op("sp", "dma_start", out=d, in_=t[:], r=keys, dma="so")

    P.op("sp", "dma_start", out=cst32[:], in_=cst_d, w=[("cst32",)], dma="c0")
    P.op("sp", "dma_start", out=gains[:], in_=gains_d, w=[("gains",)], dma="c0")
    P.op("sp", "dma_start", out=hg[:], in_=hg_d, w=[("hg",)], dma="c0")
    P.op("sp", "dma_start", out=mk[:], in_=mk_d, w=[("mk",)], dma="c0")
    P.op("dve", "tensor_copy", ident[:], cst32[:, 0, :], r=[("cst32",)], w=[("ident",)])
    P.op("dve", "tensor_copy", tri[:], cst32[:, 1, :], r=[("cst32",)], w=[("tri",)])
    P.op("dve", "memset", ones[:], 1.0, w=[("ones",)])
    P.op("dve", "memset", zer[:], 0.0, w=[("zer",)])

    cnt = [0]
    tc_cnt = [0]

    def uid():
        cnt[0] += 1
        return cnt[0]

    def load_weights(w_ap, col0, ncols, gidx, slot):
        src = w_ap[:, col0:col0 + ncols].rearrange("(k p) n -> p k n", p=128)
        P.op("pool", "dma_start", out=Wst[slot][:, :, 0:ncols], in_=src,
              w=[("KTs", slot)], dma="w%d" % slot)
        for kc in range(8):
            if kc % 2 == 0:
                P.op("act", "activation", out=Wb[slot][:, kc, 0:ncols], in_=Wst[slot][:, kc, 0:ncols],
                                                           func=AF.Copy, scale=gains[:, gidx, kc:kc + 1],
                      r=[("KTs", slot), ("gains",)], w=[("Wb", slot, kc), ("VsW", slot)])
            else:
                P.op("dve", "tensor_scalar", Wb[slot][:, kc, 0:ncols], Wst[slot][:, kc, 0:ncols],
                                                              gains[:, gidx, kc:kc + 1], None, ALU.mult,
                      r=[("KTs", slot), ("gains",)], w=[("Wb", slot, kc), ("VsW", slot)])

    def wb_keys(slot):
        return [("Wb", slot, kc) for kc in range(8)] + [("Vs", slot)]

    def wb_wkeys(slot):
        return [("Vs", slot)]

    def rstd_from_ss(out_ap, ss_ap, n, rkeys, wkeys):
        t = uid()
        P.op("act", "activation", out=out_ap, in_=ss_ap, func=AF.Ln, scale=1.0 / n, bias=EPS,
              r=rkeys, w=wkeys)
        P.op("act", "activation", out=out_ap, in_=out_ap, func=AF.Exp, scale=-0.5,
              r=wkeys, w=wkeys)

    def norm_tile_to_T(src_ap_fn, src_keys, slot, dstT, dkey, tt, ncols_tok=128, width=2048):
        xs = xt[slot]
        P.op("act", "activation", out=sq[:], in_=xs[:], func=AF.Square, r=[("xt", slot)], w=SQK)
        ssk = ("ssx", slot)
        P.op("dve", "tensor_reduce", s8[slot][:, 0, 0:1], sq[:], AX.X, ALU.add, r=SQK, w=[ssk])
        rstd_from_ss(s8[slot][:, 0, 1:2], s8[slot][:, 0, 0:1], 1024.0, [ssk], [("rsx", slot)])
        P.op("dve", "tensor_scalar", xnb[slot][:], xs[:], s8[slot][:, 0, 1:2], None, ALU.mult,
              r=[("xt", slot), ("rsx", slot)], w=[("xnb", slot)])
        for half in range(2):
            bank = 6 + half
            for q4 in range(4):
                kc = half * 4 + q4
                P.op("pe", "transpose", psbf(bank)[:, q4 * 128:(q4 + 1) * 128], xnb[slot][:, kc * 128:(kc + 1) * 128], ident[:],
                    r=[("xnb", slot), ("ident",)], w=[("ps", bank)])
            eng = "act" if half == 0 else "dve"
            dst = dstT[:, half * 4:half * 4 + 4, tt * ncols_tok:(tt + 1) * ncols_tok]
            srcp = psbf(bank)[:, 0:512].rearrange("p (a b) -> p a b", b=128)
            if eng == "act":
                P.op("act", "activation", out=dst, in_=srcp, func=AF.Copy,
                      r=[("ps", bank)], w=[(dkey, kc_, tt) for kc_ in range(half * 4, half * 4 + 4)])
            else:
                P.op("dve", "tensor_copy", dst, srcp,
                      r=[("ps", bank)], w=[(dkey, kc_, tt) for kc_ in range(half * 4, half * 4 + 4)])

    def headnorm(bank, slot, hb0, nhb, gain_idx, out_stg, rope, tt):
        c0, c1 = hb0 * 64, (hb0 + nhb) * 64
        ps3 = PS[bank][:, c0:c1].rearrange("p (h d) -> p h d", d=64)
        sq3 = w32[0][:, c0:c1]
        hbs = range(hb0, hb0 + nhb)
        kq = [("w32", 0, hb) for hb in hbs]
        P.op("act", "activation", out=sq3, in_=PS[bank][:, c0:c1], func=AF.Square, r=[("ps", bank)], w=kq)
        kss = [("s8", slot, hb) for hb in hbs]
        ssap = s8[slot][:, 1, hb0:hb0 + nhb]
        rsap = s8[slot][:, 2, hb0:hb0 + nhb]
        P.op("dve", "tensor_reduce", ssap, sq3.rearrange("p (h d) -> p h d", d=64), AX.X, ALU.add,
              r=kq, w=kss)
        krs = [("s8r", slot, hb) for hb in hbs]
        rstd_from_ss(rsap, ssap, 64.0, kss, krs)
        xn3 = w32[1][:, c0:c1].rearrange("p (h d) -> p h d", d=64)
        kxn = [("w32", 1, hb) for hb in hbs]
        P.op("dve", "tensor_tensor", xn3, ps3, rsap.unsqueeze(2).broadcast_to([128, nhb, 64]), ALU.mult,
              r=[("ps", bank)] + krs, w=kxn)
        o3 = out_stg[:, c0:c1].rearrange("p (h d) -> p h d", d=64)
        g3 = hg[:, gain_idx, :].unsqueeze(1).broadcast_to([128, nhb, 64])
        kst = [("stg", slot, hb) for hb in hbs]
        P.op("dve", "tensor_tensor", o3, xn3, g3, ALU.mult, r=kxn + [("hg",)], w=kst)
        if rope:
            R = rp[0]
            kr = ("rp",)
            xg = R[:, 0, hb0:hb0 + nhb, :]
            g16 = hg[:, gain_idx, 0:16].unsqueeze(1).broadcast_to([128, nhb, 16])
            P.op("dve", "tensor_tensor", xg, xn3[:, :, 0:16], g16, ALU.mult, r=kxn + [("hg",)], w=[kr])
            cs = cosT[:, tt, :].unsqueeze(1).broadcast_to([128, nhb, 8])
            sn = sinT[:, tt, :].unsqueeze(1).broadcast_to([128, nhb, 8])
            x1 = xg[:, :, 0:8]
            x2 = xg[:, :, 8:16]
            t1 = R[:, 1, hb0:hb0 + nhb, 0:8]
            t2 = R[:, 1, hb0:hb0 + nhb, 8:16]
            t3 = R[:, 2, hb0:hb0 + nhb, 0:8]
            t4 = R[:, 2, hb0:hb0 + nhb, 8:16]
            P.op("dve", "tensor_tensor", t1, x1, cs, ALU.mult, r=[kr, ("rope",)], w=[("rp1",)])
            P.op("dve", "tensor_tensor", t2, x2, sn, ALU.mult, r=[kr, ("rope",)], w=[("rp2",)])
            P.op("dve", "tensor_tensor", t3, x2, cs, ALU.mult, r=[kr, ("rope",)], w=[("rp3",)])
            P.op("dve", "tensor_tensor", t4, x1, sn, ALU.mult, r=[kr, ("rope",)], w=[("rp4",)])
            P.op("dve", "tensor_tensor", o3[:, :, 0:8], t1, t2, ALU.subtract,
                  r=[("rp1",), ("rp2",)], w=kst)
            P.op("dve", "tensor_tensor", o3[:, :, 8:16], t3, t4, ALU.add,
                  r=[("rp3",), ("rp4",)], w=kst)
        return kst

    def transpose_chunks(slot, src_stg, chunk_list, src_keys, tbank):
        for i, (ci, dst, dk) in enumerate(chunk_list):
            P.op("pe", "transpose", psbf(tbank)[:, i * 128:(i + 1) * 128],
                                                          src_stg[:, ci * 128:(ci + 1) * 128], ident[:],
                  r=list(src_keys) + [("ident",)], w=[("ps", tbank)])
        tc_cnt[0] += 1
        for i, (ci, dst, dk) in enumerate(chunk_list):
            eng = "act" if tc_cnt[0] % 2 == 0 else "dve"
            if eng == "act":
                P.op("act", "activation", out=dst, in_=psbf(tbank)[:, i * 128:(i + 1) * 128], func=AF.Copy,
                      r=[("ps", tbank)], w=dk)
            else:
                P.op("dve", "tensor_copy", dst, psbf(tbank)[:, i * 128:(i + 1) * 128],
                      r=[("ps", tbank)], w=dk)

    def proj_tok(slot_w, bank, tt, ncols, srcT, skey):
        for kc in range(8):
            P.op("pe", "matmul", PS[bank][:, 0:ncols], srcT[:, kc, tt * 128:(tt + 1) * 128],
                                                  Wb[slot_w][:, kc, 0:ncols], start=(kc == 0), stop=(kc == 7),
                  r=[(skey, kc, tt), ("Wb", slot_w, kc), ("VsW", slot_w)], w=[("ps", bank)])

    def proj_feat(slot_w, bank, chunk, j, srcT, skey):
        for kc in range(8):
            P.op("pe", "matmul", PS[bank][:, :], Wb[slot_w][:, kc, chunk * 128:(chunk + 1) * 128],
                                                  srcT[:, kc, j * 512:(j + 1) * 512], start=(kc == 0), stop=(kc == 7),
                  r=[(skey, kc, t_) for t_ in range(4 * j, 4 * j + 4)] + [("Wb", slot_w, kc), ("VsW", slot_w)], w=[("ps", bank)])

    def silu_evac(bank, dst, dkeys, wi):
        e = w32[wi][:, :]
        ke = ("w32", wi)
        P.op("act", "activation", out=e, in_=PS[bank][:, :], func=AF.Exp, scale=-1.0, r=[("ps", bank)], w=[ke])
        P.op("dve", "tensor_scalar", e, e, 1.0, None, ALU.add, r=[ke], w=[ke])
        P.op("dve", "reciprocal", e, e, r=[ke], w=[ke])
        P.op("dve", "tensor_tensor", dst, PS[bank][:, :], e, ALU.mult, r=[ke, ("ps", bank)], w=dkeys)

    def phase1():
        kown, vown = ex0[0], ex0[1]
        P.op("sp", "dma_start", out=posi[:], in_=pos_d, w=[("posi",)], dma="c1")
        P.op("sp", "dma_start", out=invf[:], in_=invf_d, w=[("invf",)], dma="c1")
        pf = sml
        posf = rt[2][:, :, 0]
        P.op("dve", "tensor_copy", posf, posi[:], r=[("posi",)], w=[("posf",)])
        ang = rt[0]
        P.op("dve", "tensor_tensor", ang[:], posf.unsqueeze(2).broadcast_to([128, 16, 8]),
                                               invf[:].unsqueeze(1).broadcast_to([128, 16, 8]), ALU.mult,
              r=[("posf",), ("invf",)], w=[("ang",)])
        for which, shift, dst in (("s", 0.0, sinT), ("c", math.pi / 2, cosT)):
            a2 = rt[1]
            k2 = ("a2",)
            P.op("dve", "tensor_scalar", a2[:], ang[:], shift, None, ALU.add, r=[("ang",)], w=[k2])
            kfl = rt[2]
            P.op("dve", "tensor_scalar", kfl[:], a2[:], 1.0 / (2 * math.pi), None, ALU.mult, r=[k2], w=[("kfl",), ("posf",)])
            P.op("dve", "tensor_copy", rti[:], kfl[:], r=[("kfl",)], w=[("rti",)])
            P.op("dve", "tensor_copy", kfl[:], rti[:], r=[("rti",)], w=[("kfl",)])
            P.op("dve", "scalar_tensor_tensor", a2[:], kfl[:], -2 * math.pi, a2[:], ALU.mult, ALU.add,
                  r=[("kfl",), k2], w=[k2])
            P.op("dve", "tensor_scalar", a2[:], a2[:], 3.1415925, -3.1415925, ALU.min, ALU.max, r=[k2], w=[k2])
            P.op("act", "activation", out=dst[:], in_=a2[:], func=AF.Sin, r=[k2], w=[("rope",)])
            if which == "s":
                pass

        for mt in range(2):
            slot = mt
            P.op("sp", "dma_start", out=xt[slot][:], in_=mem_d[mt * 128:(mt + 1) * 128, :],
                  w=[("xt", slot)], dma="xt%d" % slot)
            norm_tile_to_T(None, None, slot, memT, "memT", mt, ncols_tok=128)
        for l in range(2):
            slotw = l % 2
            src = w_memkv[l].rearrange("(k p) n -> p k n", p=128)
            P.op("pool", "dma_start", out=Wst[slotw][:, :, :], in_=src,
                  w=[("KTs", slotw)], dma="w%d" % slotw)
            for kc in range(8):
                P.op("dve", "tensor_scalar", Wb[slotw][:, kc, :], Wst[slotw][:, kc, :], gains[:, 3 + l, kc:kc + 1], None, ALU.mult,
                    r=[("KTs", slotw), ("gains",)], w=[("Wb", slotw, kc), ("VsW", slotw)])
            P.op("dve", "memset", kmpad[:, l], 0.0, w=[("kmpad",)])
            P.op("dve", "memset", vmpad[:, l], 0.0, w=[("vmpad",)])
            for mt in range(2):
                bank = mt
                proj_tok(slotw, bank, mt, 512, memT, "memT")
                sl = mt
                kst = headnorm(bank, sl, 0, 4, 3 + 2 * l, stg[sl], False, 0)
                for h in range(4):
                    dst = vmpad[:, l, h, mt, (h % 2) * 64:(h % 2) * 64 + 64]
                    P.op("dve", "tensor_copy", dst, PS[bank][:, 256 + h * 64:256 + (h + 1) * 64],
                        r=[("ps", bank)], w=[("vmpad",)])
                chunk_list = []
                for ci in range(2):
                    chunk_list.append((ci, tst[sl][:, ci, :], [("tst", sl, ci)]))
                transpose_chunks(sl, stg[sl], chunk_list, kst, 6 + mt)
                for h in (3, 2, 1, 0):
                    r0 = (h % 2) * 64
                    P.op("dve", "tensor_copy", kmpad[r0:r0 + 64, l, h, mt * 128:(mt + 1) * 128], tst[sl][r0:r0 + 64, h // 2, :],
                        r=[("tst", sl, h // 2)], w=[("kmpad",)])

        for tt in range(16):
            slot = tt % 2
            P.op("sp", "dma_start", out=xt[slot][:], in_=x_d[tt * 128:(tt + 1) * 128, :],
                  w=[("xt", slot)], dma="xt%d" % slot)
            norm_tile_to_T(None, None, slot, hy, "hy", tt)

        groups = [
            (0, [("q", 0, 8, 0)]),
            (512, [("q", 0, 4, 4), ("k", 4, 4, 0)]),
            (1024, [("k", 0, 8, 2)]),
            (1536, [("v", 0, 512, 0)]),
            (2048, [("v", 0, 256, 512), ("qm", 4, 4, 0)]),
        ]
        for gi, (col0, parts) in enumerate(groups):
            slotw = gi % 2
            load_weights(w_ain, col0, 512, 0, slotw)
            proj_tok(slotw, 0, 0, 512, hy, "hy")
            for tt in range(16):
                bank = tt % 2
                sl = tt % 2
                if tt + 1 < 16:
                    proj_tok(slotw, (tt + 1) % 2, tt + 1, 512, hy, "hy")
                chunk_list = []
                skeys = []
                for part in parts:
                    kind = part[0]
                    if kind in ("q", "k"):
                        _, hb0, nhb, pair0 = part
                        kst = headnorm(bank, sl, hb0, nhb, 0 if kind == "q" else 1, stg[sl], True, tt)
                        skeys.extend(kst)
                        for ci in range(nhb // 2):
                            pair = pair0 + ci
                            if kind == "q":
                                chunk_list.append((hb0 // 2 + ci, QT[:, pair, tt * 128:(tt + 1) * 128], [("QT", pair, tt)]))
                            else:
                                chunk_list.append((hb0 // 2 + ci, tst[sl][:, hb0 // 2 + ci, :], [("tst", sl, hb0 // 2 + ci)]))
                    elif kind == "qm":
                        _, hb0, nhb, _ = part
                        kst = headnorm(bank, sl, hb0, nhb, 2, stg[sl], False, tt)
                        skeys.extend(kst)
                        for ci in range(2):
                            chunk_list.append((hb0 // 2 + ci, qmT[:, ci, tt * 128:(tt + 1) * 128], [("qmT", ci, tt)]))
                    else:
                        _, c0, nc_, vc0 = part
                        P.op("act", "activation", out=vst[sl][:, vc0:vc0 + nc_], in_=PS[bank][:, c0:c0 + nc_], func=AF.Copy,
                            r=[("ps", bank)], w=[("vst", sl, vc0)])
                        dst = vown[vc0 // 128:(vc0 + nc_) // 128, :, tt, :].rearrange("h p d -> p h d")
                        P.op("sp", "dma_start", out=dst, in_=vst[sl][:, vc0:vc0 + nc_].rearrange("p (h d) -> p h d", d=128),
                            r=[("vst", sl, vc0)], w=[("vown", tt, vc0)], dma="vo")
                if chunk_list:
                    transpose_chunks(sl, stg[sl], chunk_list, skeys, 6 + sl)
                    for (ci, dst, dk) in chunk_list:
                        if dk[0][0] == "tst":
                            kpair = None
                            for part in parts:
                                if part[0] == "k":
                                    kpair = part[3] + (ci - part[1] // 2)
                            P.op("sp", "dma_start", out=kown[kpair, :, tt * 128:(tt + 1) * 128], in_=tst[sl][:, ci, :],
                                r=dk, w=[("kown", kpair, tt)], dma="ko")
        for gg in range(2):
            slotw = (5 + gg) % 2
            load_weights(w_ain, 2560 + gg * 512, 512, 0, slotw)
            for ch in range(4):
                for j in range(4):
                    bank = 2 + (ch * 4 + j) % 4
                    proj_feat(slotw, bank, ch, j, hy, "hy")
                    chunk = gg * 4 + ch
                    silu_evac(bank, gateT[:, chunk, j * 512:(j + 1) * 512], [("gateT", chunk, j)], 2 + (ch * 4 + j) % 4)

    def load_kv(all_k, all_v, pair, slot):
        for rnk in range(4):
            P.op("sp", "dma_start", out=KTs[slot][:, rnk * 2048:(rnk + 1) * 2048], in_=all_k[rnk, pair],
                  w=[("KTs", slot)], dma="kt%d" % slot)
            P.op("sp", "dma_start", out=Vs[slot][:, rnk * 16:(rnk + 1) * 16, :], in_=all_v[rnk, pair],
                  w=[("Vs", 0), ("VsW", 0), ("VsW", 1)], dma="vs0")

    def key_steps(j):
        steps = []
        for m in (3, 2, 1, 0):
            for r in (3, 2, 1, 0):
                steps.append((r, 4 * j + m, m, r))
        for g in range(16 * j - 1, -1, -1):
            jj, rem = divmod(g, 16)
            s_, c_ = divmod(rem, 4)
            steps.append((c_, 4 * jj + s_, None, None))
        return steps

    def make_qpad(pair, j, qs, scale):
        P.op("pool", "memset", qpad[qs][:], 0.0, w=[("qpad", qs)])
        for hh in range(2):
            r0 = hh * 64
            if scale == 1.0:
                P.op("pool", "tensor_copy", qpad[qs][r0:r0 + 64, hh, :], QT[r0:r0 + 64, pair, j * 512:(j + 1) * 512],
                      r=[("QT", pair, t_) for t_ in range(4 * j, 4 * j + 4)], w=[("qpad", qs)])
            else:
                P.op("dve", "tensor_scalar", qpad[qs][r0:r0 + 64, hh, :], QT[r0:r0 + 64, pair, j * 512:(j + 1) * 512],
                                                                  scale, None, ALU.mult,
                      r=[("QT", pair, t_) for t_ in range(4 * j, 4 * j + 4)], w=[("qpad", qs)])

    def attention0(all_k, all_v, finalize):
        it = 0
        for pair in range(6):
            slot = pair % 2
            load_kv(all_k, all_v, pair, slot)
            for j in range(4):
                qs = (pair * 4 + j) % 2
                make_qpad(pair, j, qs, 1.0)
                for b in (4, 5):
                    P.op("pe", "matmul", PS[b][:, :], zer[:, 0:128], zer[:, :], start=True, stop=False,
                          r=[("zer",)], w=[("ps", b)])
                P.op("pool", "memset", accl[:], 0.0, w=[("accl",)])
                steps = key_steps(j)
                prev = None
                for si, (rnk, lt, m, r) in enumerate(steps):
                    buf = it % 2
                    it += 1
                    c0 = 0 if m is None else 128 * m
                    kcol = rnk * 2048 + lt * 128
                    vt = rnk * 16 + lt
                    last = (si == len(steps) - 1)
                    for c in range(2):
                        P.op("pe", "matmul", PS[buf * 2 + c][:, c0:512], KTs[slot][:, kcol:kcol + 128], qpad[qs][:, c, c0:512],
                            start=True, stop=True,
                            r=[("KTs", slot), ("qpad", qs)], w=[("ps", buf * 2 + c)])
                    for c in range(2):
                        P.op("act", "activation", out=pb[buf][:, c, c0:512], in_=PS[buf * 2 + c][:, c0:512], func=AF.Exp, scale=0.125,
                            r=[("ps", buf * 2 + c)], w=[("pb", buf, c)])
                    if m is not None:
                        P.op("dve", "tensor_tensor", pb[buf][:, :, c0:c0 + 128], pb[buf][:, :, c0:c0 + 128],
                            mk[:, 0, r, :].unsqueeze(1).broadcast_to([128, 2, 128]), ALU.mult,
                            r=[("pb", buf, 0), ("pb", buf, 1), ("mk",)], w=[("pb", buf, 0), ("pb", buf, 1)])
                    P.op("dve", "tensor_tensor", accl[:, :, c0:512], accl[:, :, c0:512], pb[buf][:, :, c0:512], ALU.add,
                          r=[("accl",), ("pb", buf, 0), ("pb", buf, 1)], w=[("accl",)])
                    if prev is not None:
                        emit_pv0(prev, slot, False)
                    prev = (buf, c0, vt)
                emit_pv0(prev, slot, True)
                for c in range(2):
                    P.op("pe", "matmul", PS[6 + c][:, :], ones[:], accl[:, c, :], start=True, stop=True,
                          r=[("ones",), ("accl",)], w=[("ps", 6 + c)])
                finalize(pair, j)

    def emit_pv0(prev, slot, last):
        buf, c0, vt = prev
        for c in range(2):
            P.op("pe", "matmul", PS[4 + c][:, c0:512], Vs[slot][:, vt, :], pb[buf][:, c, c0:512],
                                                start=False, stop=last,
                  r=[("Vs", 0), ("VsW", 0), ("VsW", 1), ("pb", buf, c)], w=[("ps", 4 + c)])

    def finalize0(pair, j):
        r0, r1, t0, t1 = w32[2], w32[3], w32[4], w32[5]
        P.op("dve", "reciprocal", r0[:], PS[6][:, :], r=[("ps", 6)], w=[("w32", 2)])
        P.op("dve", "reciprocal", r1[:], PS[7][:, :], r=[("ps", 7)], w=[("w32", 3)])
        P.op("dve", "tensor_tensor", t0[:], PS[4][:, :], r0[:], ALU.mult, r=[("ps", 4), ("w32", 2)], w=[("w32", 4)])
        P.op("dve", "tensor_tensor", t1[:], PS[5][:, :], r1[:], ALU.mult, r=[("ps", 5), ("w32", 3)], w=[("w32", 5)])
        P.op("dve", "scalar_tensor_tensor", t0[:], t1[:], sml[:, 2:3], t0[:], ALU.mult, ALU.add,
              r=[("w32", 4), ("w32", 5), ("sml",)], w=[("w32", 4)])
        P.op("act", "activation", out=pb[0][:, 0, :], in_=t0[:], func=AF.Square, r=[("w32", 4)], w=[("pb", 0, 0)])
        P.op("pe", "matmul", PS[6][:, :], ones[:], pb[0][:, 0, :], start=True, stop=True,
              r=[("ones",), ("pb", 0, 0)], w=[("ps", 6)])
        P.op("act", "activation", out=r0[:], in_=PS[6][:, :], func=AF.Ln, scale=1.0 / 128, bias=EPS, r=[("ps", 6)], w=[("w32", 2)])
        P.op("act", "activation", out=r0[:], in_=r0[:], func=AF.Exp, scale=-0.5, r=[("w32", 2)], w=[("w32", 2)])
        P.op("dve", "scalar_tensor_tensor", t0[:], t0[:], sml[:, 1:2], r0[:], ALU.mult, ALU.mult,
              r=[("w32", 4), ("w32", 2), ("sml",)], w=[("w32", 4)])
        P.op("dve", "tensor_tensor", hy[:, pair, j * 512:(j + 1) * 512], t0[:], gateT[:, pair, j * 512:(j + 1) * 512], ALU.mult,
              r=[("w32", 4), ("gateT", pair, j)], w=[("hy", pair, t_) for t_ in range(4 * j, 4 * j + 4)])

    def attention1(all_k, all_v, finalize):
        for pair in range(6):
            slot = pair % 2
            load_kv(all_k, all_v, pair, slot)
            for j in range(4):
                qs = (pair * 4 + j) % 2
                make_qpad(pair, j, qs, 0.125)
                for b in (6, 7):
                    P.op("pe", "matmul", PS[b][:, :], zer[:, 0:128], zer[:, :], start=True, stop=False,
                          r=[("zer",)], w=[("ps", b)])
                steps = key_steps(j)
                n = len(steps)
                P.op("pool", "memset", acc[0][:], 0.0, w=[("acc", 0)])

                def info(si):
                    rnk, lt, m, r = steps[si]
                    c0 = 0 if m is None else 128 * m
                    kcol = rnk * 2048 + lt * 128
                    return c0, KTs[slot][:, kcol:kcol + 128], rnk * 16 + lt, m, r

                def stageA(si):
                    c0, ksl, vt, m, r = info(si)
                    buf = si % 2
                    for hh in range(2):
                        zb = buf * 2 + hh
                        P.op("pe", "matmul", PS[zb][:, c0:512], ksl, qpad[qs][:, hh, c0:512], start=True, stop=True,
                              r=[("KTs", slot), ("qpad", qs)], w=[("ps", zb)])
                    for hh in range(2):
                        zb = buf * 2 + hh
                        P.op("act", "activation", out=PS[zb][:, c0:512], in_=PS[zb][:, c0:512], func=AF.Exp,
                              r=[("ps", zb)], w=[("ps", zb)])
                    for hh in range(2):
                        zb = buf * 2 + hh
                        P.op("act", "activation", out=pb[buf][:, hh, c0:512], in_=PS[zb][:, c0:512], func=AF.Ln, bias=1.0,
                              r=[("ps", zb)], w=[("pb", buf, hh)])
                    if m is not None:
                        P.op("dve", "tensor_tensor", pb[buf][:, :, c0:c0 + 128], pb[buf][:, :, c0:c0 + 128],
                              mk[:, 1, r, :].unsqueeze(1).broadcast_to([128, 2, 128]), ALU.mult,
                              r=[("pb", buf, 0), ("pb", buf, 1), ("mk",)], w=[("pb", buf, 0), ("pb", buf, 1)])
                    if si < n - 1:
                        cur, nxt = si % 3, (si + 1) % 3
                        if c0 > 0:
                            P.op("pool", "memset", acc[nxt][:, :, 0:c0], 0.0, w=[("acc", nxt)])
                        P.op("dve", "tensor_tensor", acc[nxt][:, :, c0:512], acc[cur][:, :, c0:512], pb[buf][:, :, c0:512], ALU.subtract,
                              r=[("acc", cur), ("pb", buf, 0), ("pb", buf, 1)], w=[("acc", nxt)])

                def stageB(si):
                    c0, ksl, vt, m, r = info(si)
                    buf = si % 2
                    cur = si % 3
                    for hh in range(2):
                        cb = 4 + hh
                        P.op("pe", "matmul", PS[cb][:, c0:512], ksl, qpad[qs][:, hh, c0:512], start=True, stop=False,
                              r=[("KTs", slot), ("qpad", qs)], w=[("ps", cb)])
                        P.op("pe", "matmul", PS[cb][:, c0:512], tri[:], pb[buf][:, hh, c0:512], start=False, stop=(si == 0),
                              r=[("tri",), ("pb", buf, hh)], w=[("ps", cb)])
                        if si > 0:
                            P.op("pe", "matmul", PS[cb][:, c0:512], ones[:], acc[cur][:, hh, c0:512], start=False, stop=True,
                                  r=[("ones",), ("acc", cur)], w=[("ps", cb)])
                    for hh in range(2):
                        cb = 4 + hh
                        P.op("act", "activation", out=ab[buf][:, hh, c0:512], in_=PS[cb][:, c0:512], func=AF.Exp,
                              r=[("ps", cb)], w=[("ab", buf, hh)])
                    if m is not None:
                        P.op("dve", "tensor_tensor", ab[buf][:, :, c0:c0 + 128], ab[buf][:, :, c0:c0 + 128],
                              mk[:, 1, r, :].unsqueeze(1).broadcast_to([128, 2, 128]), ALU.mult,
                              r=[("ab", buf, 0), ("ab", buf, 1), ("mk",)], w=[("ab", buf, 0), ("ab", buf, 1)])

                def stagePV(si, last):
                    c0, ksl, vt, m, r = info(si)
                    buf = si % 2
                    for hh in range(2):
                        P.op("pe", "matmul", PS[6 + hh][:, c0:512], Vs[slot][:, vt, :], ab[buf][:, hh, c0:512],
                              start=False, stop=last,
                              r=[("Vs", 0), ("VsW", 0), ("VsW", 1), ("ab", buf, hh)], w=[("ps", 6 + hh)])

                stageA(0)
                for si in range(n):
                    if si + 1 < n:
                        stageA(si + 1)
                    stageB(si)
                    if si > 0:
                        stagePV(si - 1, False)
                stagePV(n - 1, True)
                finalize(pair, j)

    def emit_pv1(prev, slot, last):
        buf, c0, vt = prev
        for hh in range(2):
            P.op("pe", "matmul", PS[6 + hh][:, c0:512], Vs[slot][:, vt, :], ab[buf][:, hh, c0:512],
                                                  start=False, stop=last,
                  r=[("Vs", 0), ("VsW", 0), ("VsW", 1), ("ab", buf, hh)], w=[("ps", 6 + hh)])

    def finalize1(pair, j):
        for hh in range(2):
            r0 = hh * 64
            P.op("dve", "tensor_tensor", hy[r0:r0 + 64, pair, j * 512:(j + 1) * 512], PS[6 + hh][r0:r0 + 64, :],
                gateT[r0:r0 + 64, pair, j * 512:(j + 1) * 512], ALU.mult,
                r=[("ps", 6 + hh), ("gateT", pair, j)], w=[("hy", pair, t_) for t_ in range(4 * j, 4 * j + 4)])

    def mem_attention(l):
        for ci in range(2):
            for j in range(4):
                P.op("pe", "matmul", PS[3][:, :], zer[:, 0:128], zer[:, :], start=True, stop=False,
                      r=[("zer",)], w=[("ps", 3)])
                for hh in range(2):
                    h = ci * 2 + hh
                    buf = hh
                    for mt in range(2):
                        P.op("pe", "matmul", PS[mt][:, :], kmpad[:, l, h, mt * 128:(mt + 1) * 128],
                                                                   qmT[:, ci, j * 512:(j + 1) * 512], start=True, stop=True,
                              r=[("kmpad",)] + [("qmT", ci, t_) for t_ in range(4 * j, 4 * j + 4)], w=[("ps", mt)])
                        P.op("act", "activation", out=pb[buf][:, mt, :], in_=PS[mt][:, :], func=AF.Exp, scale=0.125,
                              r=[("ps", mt)], w=[("pb", buf, mt)])
                    for mt in range(2):
                        P.op("pe", "matmul", PS[2][:, :], ones[:], pb[buf][:, mt, :], start=(mt == 0), stop=(mt == 1),
                              r=[("ones",), ("pb", buf, mt)], w=[("ps", 2)])
                    rl = w32[2 + hh]
                    P.op("dve", "reciprocal", rl[:], PS[2][:, :], r=[("ps", 2)], w=[("w32", 2 + hh)])
                    P.op("dve", "tensor_tensor", ab[buf][:, :, :], pb[buf][:, :, :],
                                                                           rl[:].unsqueeze(1).broadcast_to([128, 2, 512]), ALU.mult,
                          r=[("w32", 2 + hh), ("pb", buf, 0), ("pb", buf, 1)], w=[("ab", buf, 0), ("ab", buf, 1)])
                    for mt in range(2):
                        P.op("pe", "matmul", PS[3][:, :], vmpad[:, l, h, mt, :], ab[buf][:, mt, :],
                                                                                   start=False, stop=(hh == 1 and mt == 1),
                              r=[("vmpad",), ("ab", buf, mt)], w=[("ps", 3)])
                ch = 6 + ci
                P.op("dve", "tensor_tensor", hy[:, ch, j * 512:(j + 1) * 512], PS[3][:, :],
                                                              gateT[:, ch, j * 512:(j + 1) * 512], ALU.mult,
                      r=[("ps", 3), ("gateT", ch, j)], w=[("hy", ch, t_) for t_ in range(4 * j, 4 * j + 4)])

    def out_proj(w_ap, res_d, dst_d, after_tile, res_key, dst_key):
        for half in range(2):
            src = w_ap[:, half * 512:(half + 1) * 512].rearrange("(k p) n -> p k n", p=128)
            P.op("pool", "dma_start", out=Wst[half][:, :, :], in_=src,
                  w=[("KTs", half)], dma="w%d" % half)
            for kc in range(8):
                eng = "act" if kc % 2 == 0 else "dve"
                if eng == "act":
                    P.op("act", "activation", out=Wb[half][:, kc, :], in_=Wst[half][:, kc, :], func=AF.Copy,
                          r=[("KTs", half)], w=[("Wb", half, kc), ("VsW", half)])
                else:
                    P.op("dve", "tensor_copy", Wb[half][:, kc, :], Wst[half][:, kc, :],
                          r=[("KTs", half)], w=[("Wb", half, kc), ("VsW", half)])
        def op_mm(tt):
            for half in range(2):
                bank = (tt % 2) * 2 + half
                for kc in range(8):
                    P.op("pe", "matmul", PS[bank][:, :], hy[:, kc, tt * 128:(tt + 1) * 128], Wb[half][:, kc, :], start=(kc == 0), stop=(kc == 7),
                        r=[("hy", kc, tt), ("Wb", half, kc), ("VsW", half)], w=[("ps", bank)])

        op_mm(0)
        for tt in range(16):
            slot = tt % 2
            P.op("sp", "dma_start", out=xt[slot][:], in_=res_d[tt * 128:(tt + 1) * 128, :],
                  r=[("dram", res_key, tt)], w=[("xt", slot)], dma="xt%d" % slot)
            if tt + 1 < 16:
                op_mm(tt + 1)
            for half in range(2):
                bank = (tt % 2) * 2 + half
                P.op("dve", "tensor_tensor", xt[slot][:, half * 512:(half + 1) * 512], xt[slot][:, half * 512:(half + 1) * 512], PS[bank][:, :], ALU.add,
                    r=[("ps", bank), ("xt", slot)], w=[("xt", slot)])
            P.op("sp", "dma_start", out=dst_d[tt * 128:(tt + 1) * 128, :], in_=xt[slot][:],
                  r=[("xt", slot)], w=[("dram", dst_key, tt)], dma="od")
            if after_tile is not None:
                after_tile(tt, slot)

    def phase2():
        kown1, vown1 = ex1[0], ex1[1]
        P.op("sp", "dma_start", out=lamv[:], in_=lamv_d, w=[("lamv",)], dma="c2")
        P.op("sp", "dma_start", out=sml[:, 0:1], in_=subln_d, w=[("sml",)], dma="c2")
        pr = w32[0][:, 0:128].rearrange("p (a d) -> p a d", d=64)
        P.op("dve", "tensor_tensor", pr[:, 0, :], lamv[:, 0, :], lamv[:, 1, :], ALU.mult, r=[("lamv",)], w=[("w32", 0, 0)])
        P.op("dve", "tensor_tensor", pr[:, 1, :], lamv[:, 2, :], lamv[:, 3, :], ALU.mult, r=[("lamv",)], w=[("w32", 0, 1)])
        P.op("dve", "tensor_reduce", sml[:, 4:6], pr, AX.X, ALU.add, r=[("w32", 0, 0), ("w32", 0, 1)], w=[("sml4",)])
        P.op("act", "activation", out=sml[:, 6:8], in_=sml[:, 4:6], func=AF.Exp, r=[("sml4",)], w=[("sml6",)])
        P.op("dve", "tensor_tensor", sml[:, 2:3], sml[:, 7:8], sml[:, 6:7], ALU.subtract, r=[("sml6",)], w=[("sml2",)])
        P.op("dve", "tensor_scalar", sml[:, 2:3], sml[:, 2:3], -LAMBDA_INIT0, None, ALU.add, r=[("sml2",)], w=[("sml2",)])
        P.op("dve", "tensor_scalar", sml[:, 1:2], sml[:, 0:1], 1.0 - LAMBDA_INIT0, None, ALU.mult,
              r=[("sml",), ("sml2",)], w=[("sml",)])

        attention0(ex0[2], ex0[3], finalize0)
        mem_attention(0)

        def after(tt, slot):
            norm_tile_to_T(None, None, slot, hy, "hy", tt)
        out_proj(w_aout, x_d, x1_d, after, "x", "x1")

        for gi, (col0, ncols) in enumerate(((0, 512), (512, 256))):
            slotw = gi % 2
            load_weights(w_kv, col0, ncols, 1, slotw)
            for ch in range(ncols // 128):
                pair = col0 // 128 + ch
                for j in range(4):
                    bank = 2 + (ch * 4 + j) % 4
                    proj_feat(slotw, bank, ch, j, hy, "hy")
                    sl = (ch * 4 + j) % 2
                    P.op("act", "activation", out=stg[sl][:, :], in_=PS[bank][:, :], func=AF.Copy,
                          r=[("ps", bank)], w=[("stg", sl, hb) for hb in range(8)])
                    P.op("sp", "dma_start", out=kown1[pair, :, j * 512:(j + 1) * 512], in_=stg[sl][:, :],
                          r=[("stg", sl, hb) for hb in range(8)], w=[("kown1", pair, j)], dma="ko")
        for gi, (col0, ncols, vc0) in enumerate(((768, 512, 0), (1280, 256, 512))):
            slotw = gi % 2
            load_weights(w_kv, col0, ncols, 1, slotw)
            for tt in range(16):
                bank = tt % 2
                sl = tt % 2
                proj_tok(slotw, bank, tt, ncols, hy, "hy")
                P.op("act", "activation", out=vst[sl][:, vc0:vc0 + ncols], in_=PS[bank][:, 0:ncols], func=AF.Copy,
                    r=[("ps", bank)], w=[("vst", sl, vc0)])
                dst = vown1[vc0 // 128:(vc0 + ncols) // 128, :, tt, :].rearrange("h p d -> p h d")
                P.op("sp", "dma_start", out=dst, in_=vst[sl][:, vc0:vc0 + ncols].rearrange("p (h d) -> p h d", d=128),
                    r=[("vst", sl, vc0)], w=[("vown1", tt, vc0)], dma="vo")
        for gi, (col0, ncols) in enumerate(((0, 512), (512, 256))):
            slotw = gi % 2
            load_weights(w_bin, col0, ncols, 2, slotw)
            for ch in range(ncols // 128):
                pair = col0 // 128 + ch
                for j in range(4):
                    bank = 2 + (ch * 4 + j) % 4
                    proj_feat(slotw, bank, ch, j, hy, "hy")
                    if (ch * 4 + j) % 2 == 0:
                        P.op("act", "activation", out=QT[:, pair, j * 512:(j + 1) * 512], in_=PS[bank][:, :], func=AF.Copy,
                              r=[("ps", bank)], w=[("QT", pair, t_) for t_ in range(4 * j, 4 * j + 4)])
                    else:
                        P.op("dve", "tensor_copy", QT[:, pair, j * 512:(j + 1) * 512], PS[bank][:, :],
                              r=[("ps", bank)], w=[("QT", pair, t_) for t_ in range(4 * j, 4 * j + 4)])
        load_weights(w_bin, 768, 256, 2, 0)
        proj_tok(0, 0, 0, 256, hy, "hy")
        for tt in range(16):
            bank = tt % 2
            sl = tt % 2
            if tt + 1 < 16:
                proj_tok(0, (tt + 1) % 2, tt + 1, 256, hy, "hy")
            kst = headnorm(bank, sl, 0, 4, 4, stg[sl], False, tt)
            chunk_list = [(ci, qmT[:, ci, tt * 128:(tt + 1) * 128], [("qmT", ci, tt)]) for ci in range(2)]
            transpose_chunks(sl, stg[sl], chunk_list, kst, 6 + sl)
        for gg in range(2):
            slotw = (1 + gg) % 2
            load_weights(w_bin, 1024 + gg * 512, 512, 2, slotw)
            for ch in range(4):
                for j in range(4):
                    bank = 2 + (ch * 4 + j) % 4
                    proj_feat(slotw, bank, ch, j, hy, "hy")
                    chunk = gg * 4 + ch
                    silu_evac(bank, gateT[:, chunk, j * 512:(j + 1) * 512], [("gateT", chunk, j)], 2 + (ch * 4 + j) % 4)

    def phase3():
        attention1(ex1[2], ex1[3], finalize1)
        mem_attention(1)
        out_proj(w_bout, x1_d, out_d, None, "x1", "out")

    final_groups = []
    if fused:
        raise NotImplementedError
    else:
        if 1 in phases:
            phase1()
            for nm in states:
                store_state(nm)
            final_groups += ["ko", "vo", "so"]
        if 2 in phases:
            for nm in ("QT", "gateT", "qmT", "kmpad", "vmpad"):
                load_state(nm)
            phase2()
            for nm in ("QT", "gateT", "qmT"):
                store_state(nm)
            final_groups += ["ko", "vo", "so", "od"]
        if 3 in phases:
            for nm in ("QT", "gateT", "qmT", "kmpad", "vmpad"):
                load_state(nm)
            phase3()
            final_groups += ["od"]
    P.emit(final_groups)
    return nc, ins_names, outs_names


def _prep_common(inputs):
    f32 = np.float32
    cst = np.zeros((128, 2, 128), f32)
    cst[:, 0, :] = np.eye(128, dtype=f32)
    jj = np.arange(128)[:, None]
    ss = np.arange(128)[None, :]
    cst[:, 1, :] = -(jj >= ss).astype(f32)
    inv = (np.float32(500000.0) ** (-(np.arange(0, 16, 2, dtype=np.float32)) / np.float32(16))).astype(f32)
    invf = np.broadcast_to(inv[None, :], (128, 8)).copy()

    def pk(v):
        return np.ascontiguousarray(np.asarray(v, f32).reshape(8, 128).T)

    gains = np.stack([pk(inputs["a_norm"][0]), pk(inputs["kv_norm"]), pk(inputs["b_norm"][0]),
                      pk(inputs["mem_norm"][0]), pk(inputs["mem_norm"][1])], axis=1)
    hgl = [inputs["a_q_norm"][0], inputs["a_k_norm"][0], inputs["mem_q_norm"][0], inputs["mem_k_norm"][0],
           inputs["mem_q_norm"][1], inputs["mem_k_norm"][1]]
    hg = np.broadcast_to(np.stack([np.asarray(v, f32) for v in hgl], 0)[None], (128, 6, 64)).copy()
    lamv = np.broadcast_to(np.stack([np.asarray(inputs[k][0], f32) for k in
                                     ("a_lambda_q1", "a_lambda_k1", "a_lambda_q2", "a_lambda_k2")], 0)[None], (128, 4, 64)).copy()
    subln = np.asarray(inputs["a_subln"][0], f32).reshape(128, 1).copy()
    w = np.asarray(inputs["a_w_in"][0], f32)
    perm = []
    for base in (0, 768):
        for h in range(6):
            for c in range(2):
                perm.extend(range(base + c * 384 + h * 64, base + c * 384 + h * 64 + 64))
    perm.extend(range(1536, 3584))
    w_ain = np.ascontiguousarray(w[:, perm])
    return dict(cst=cst, invf=invf, gains=gains, hg=hg, lamv=lamv, subln=subln, w_ain=w_ain)


def _masks(c):
    mk = np.zeros((128, 2, 4, 128), np.float32)
    p = np.arange(128)[:, None]
    q = np.arange(128)[None, :]
    for r in range(4):
        if r < c:
            mk[:, :, r, :] = 1.0
        elif r == c:
            mk[:, 0, r, :] = (p <= q)
            mk[:, 1, r, :] = (p < q)
    return mk


_CACHE = {}


def _get(phases, fused):
    key = (tuple(sorted(phases)), fused)
    if key not in _CACHE:
        _CACHE[key] = build(set(phases), fused)
    return _CACHE[key]


def _gather(owns, b):
    return np.stack([owns[b * 4 + c] for c in range(4)], 0)


def kernel(**inputs):
    f32 = np.float32
    com = _prep_common(inputs)
    x = np.asarray(inputs["x"], f32)
    mem = np.asarray(inputs["mem"], f32)
    pos = np.asarray(inputs["positions"], np.int32)
    cores = list(range(8))
    rows = {}
    for core in cores:
        b, c = divmod(core, 4)
        rows[core] = np.concatenate([np.arange(g * 128, (g + 1) * 128) for g in _blocks(c)])
    w_memkv = np.asarray(inputs["mem_w_kv"], f32)
    base = []
    for core in cores:
        b, c = divmod(core, 4)
        d = dict(com)
        d["x"] = np.ascontiguousarray(x[b, rows[core]])
        d["pos"] = np.ascontiguousarray(pos[b, rows[core]].reshape(16, 128).T)
        d["mem"] = mem[b]
        d["w_memkv"] = w_memkv
        d["mk"] = _masks(c)
        d["w_aout"] = np.asarray(inputs["a_w_out"][0], f32)
        d["w_kv"] = np.asarray(inputs["w_kv_shared"], f32)
        d["w_bin"] = np.asarray(inputs["b_w_in"][0], f32)
        d["w_bout"] = np.asarray(inputs["b_w_out"][0], f32)
        base.append(d)

    def run(phases, extra):
        nc, ins_names, outs_names = _get(phases, False)
        maps = []
        for core in cores:
            m = {}
            for nme in ins_names:
                if nme in extra[core]:
                    m["d_" + nme] = extra[core][nme]
                else:
                    m["d_" + nme] = base[core][nme]
            maps.append(m)
        res = run_bass_kernel_spmd(nc, maps, core_ids=cores)
        return [{k[2:]: v for k, v in r.items()} for r in res.results]

    r1 = run([1], [dict() for _ in cores])
    ex = []
    for core in cores:
        b = core // 4
        e = {"kall0": _gather([r["kown0"] for r in r1], b), "vall0": _gather([r["vown0"] for r in r1], b)}
        for nm in ("QT", "gateT", "qmT", "kmpad", "vmpad"):
            e["st_" + nm] = r1[core]["so_" + nm]
        ex.append(e)
    r2 = run([2], ex)
    ex3 = []
    for core in cores:
        b = core // 4
        e = {"kall1": _gather([r["kown1"] for r in r2], b), "vall1": _gather([r["vown1"] for r in r2], b),
             "x1": r2[core]["x1"]}
        for nm in ("QT", "gateT", "qmT"):
            e["st_" + nm] = r2[core]["so_" + nm]
        for nm in ("kmpad", "vmpad"):
            e["st_" + nm] = r1[core]["so_" + nm]
        ex3.append(e)
    r3 = run([3], ex3)
    out = np.zeros((2, 8192, 1024), f32)
    for core in cores:
        b = core // 4
        out[b, rows[core]] = r3[core]["out"]
    return out
```

```python
import math
import numpy as np
import ml_dtypes
import concourse.bass as bass
import concourse.mybir as mybir
from concourse.bass_utils import run_bass_kernel_spmd

F32 = mybir.dt.float32
BF16 = mybir.dt.bfloat16
I32 = mybir.dt.int32
AF = mybir.ActivationFunctionType
ALU = mybir.AluOpType
AX = mybir.AxisListType

NT = 2048
EPS = 1e-6
LAMBDA_INIT0 = 0.8 - 0.6 * math.exp(-0.3 * 0)
SEM_CAP = 20000


class Prog:
    def __init__(self, nc):
        self.nc = nc
        self.ops = []
        self.lw = {}
        self.rd = {}
        self.h = {"pe": nc.tensor, "act": nc.scalar, "dve": nc.vector, "pool": nc.gpsimd, "sp": nc.sync}

    def add(self, eng, fn, r=(), w=(), dma=None):
        idx = len(self.ops)
        deps = set()
        for k in r:
            if k in self.lw:
                deps.add(self.lw[k])
            if k[0] == "ps":
                for x in self.rd.get(k, ()):
                    if self.ops[x][0] != eng:
                        deps.add(x)
        for k in w:
            if k in self.lw:
                deps.add(self.lw[k])
            for x in self.rd.get(k, ()):
                deps.add(x)
        for k in w:
            self.lw[k] = idx
            self.rd[k] = []
        for k in r:
            self.rd.setdefault(k, []).append(idx)
        deps.discard(idx)
        self.ops.append((eng, fn, deps, dma))
        return idx

    def op(self, eng, meth, *args, r=(), w=(), dma=None, **kw):
        return self.add(eng, (meth, args, kw), r, w, dma)

    def emit(self, final_groups=()):
        nc = self.nc
        ops = self.ops
        n = len(ops)
        sig = [False] * n
        for (eng, fn, deps, dma) in ops:
            for d in deps:
                p = ops[d]
                if p[3] is not None:
                    continue
                if p[0] == "pe" and eng == "pe" and dma is None:
                    continue
                sig[d] = True
        import os as _os
        if int(_os.environ.get("KLIMIT", "0")):
            sig = [True] * n
        sems = {}

        def getsem(name):
            if name not in sems:
                sems[name] = nc.semaphore(name).__enter__()
            return sems[name]

        ecount = {}
        sigval = [None] * n
        gcount = {}
        waited = {}
        import os
        limit = int(os.environ.get("KLIMIT", "0")) or n
        for i, (eng, fn, deps, dma) in enumerate(ops):
            if i >= limit:
                break
            E = self.h[eng]
            needs = {}
            for d in deps:
                p = ops[d]
                if p[3] is not None:
                    key = "g_" + p[3]
                    val = gcount[p[3]]
                else:
                    if p[0] == "pe" and eng == "pe" and dma is None:
                        continue
                    key, val = sigval[d]
                if needs.get(key, 0) < val:
                    needs[key] = val
            for key, val in needs.items():
                if waited.get((eng, key), 0) < val:
                    E.wait_ge(getsem(key), val)
                    waited[(eng, key)] = val
            meth, args, kw = fn
            ins = getattr(E, meth)(*args, **kw)
            if dma is not None:
                gcount[dma] = gcount.get(dma, 0) + 16
                ins.then_inc(getsem("g_" + dma), 16)
            elif sig[i]:
                c = ecount.get(eng, 0)
                sname = "e_%s_%d" % (eng, c // SEM_CAP)
                v = c % SEM_CAP + 1
                ecount[eng] = c + 1
                ins.then_inc(getsem(sname), 1)
                sigval[i] = (sname, v)
        sp = self.h["sp"]
        if limit < n:
            for g, v in gcount.items():
                sp.wait_ge(getsem("g_" + g), v)
            for eng_, c_ in ecount.items():
                if c_ > 0:
                    sp.wait_ge(getsem("e_%s_%d" % (eng_, (c_ - 1) // SEM_CAP)), (c_ - 1) % SEM_CAP + 1)
            return
        for g in final_groups:
            sp.wait_ge(getsem("g_" + g), gcount[g])


def _blocks(c):
    return [16 * j + 4 * s + c for j in range(4) for s in range(4)]


def build(phases, fused):
    nc = bass.Bass("TRN2", target_bir_lowering=False, dynamic_dma_scratch_size=2048)
    P = Prog(nc)
    ins_names = []
    outs_names = []

    def din(name, shape, dt):
        ins_names.append(name)
        return nc.dram_tensor("d_" + name, list(shape), dt, kind="ExternalInput").ap()

    def dout(name, shape, dt):
        outs_names.append(name)
        return nc.dram_tensor("d_" + name, list(shape), dt, kind="ExternalOutput").ap()

    def dint(name, shape, dt):
        return nc.dram_tensor("d_" + name, list(shape), dt, kind="Internal").ap()

    def sb(name, shape, dt):
        return nc.sbuf_tensor(name, list(shape), dt).__enter__()

    ident = sb("ident", [128, 128], BF16)
    tri = sb("tri", [128, 128], BF16)
    ones = sb("ones", [128, 128], BF16)
    zer = sb("zer", [128, 512], BF16)
    cst32 = sb("cst32", [128, 2, 128], F32)
    mk = sb("mk", [128, 2, 4, 128], BF16)
    gains = sb("gains", [128, 5, 8], F32)
    hg = sb("hg", [128, 6, 64], F32)
    lamv = sb("lamv", [128, 4, 64], F32)
    sml = sb("sml", [128, 16], F32)
    invf = sb("invf", [128, 8], F32)
    cosT = sb("cosT", [128, 16, 8], F32)
    sinT = sb("sinT", [128, 16, 8], F32)
    hy = sb("hy", [128, 8, 2048], BF16)
    QT = sb("QT", [128, 6, 2048], BF16)
    gateT = sb("gateT", [128, 8, 2048], BF16)
    qmT = sb("qmT", [128, 2, 2048], BF16)
    BIG = sb("BIG", [128, 24576], BF16)
    xt = [sb("xt%d" % i, [128, 1024], F32) for i in range(2)]
    xnb = [sb("xnb%d" % i, [128, 1024], BF16) for i in range(2)]
    w32all = sb("w32all", [128, 6, 512], F32)
    w32 = [w32all[:, i, :] for i in range(6)]
    sq = w32all[:, 0:2, :].rearrange("p a b -> p (a b)")
    SQK = [("w32", a_, hb_) for a_ in range(2) for hb_ in range(8)]
    stg = [sb("stg%d" % i, [128, 512], BF16) for i in range(2)]
    tst = [sb("tst%d" % i, [128, 4, 128], BF16) for i in range(2)]
    vst = [sb("vst%d" % i, [128, 768], BF16) for i in range(2)]
    s8 = [sb("s8_%d" % i, [128, 4, 8], F32) for i in range(2)]
    kmpad = sb("kmpad", [128, 2, 4, 256], BF16)
    vmpad = sb("vmpad", [128, 2, 4, 2, 128], BF16)
    if 1 in phases and not fused:
        rp = [sb("rp%d" % i, [128, 3, 8, 16], F32) for i in range(1)]
        memT = sb("memT", [128, 8, 256], BF16)
        posi = sb("posi", [128, 16], I32)
        rt = [sb("rt%d" % i, [128, 16, 8], F32) for i in range(3)]
        rti = sb("rti", [128, 16, 8], I32)
        pb = ab = acc = qpad = None
    else:
        pb = [sb("pb%d" % i, [128, 2, 512], BF16) for i in range(2)]
        ab = [sb("ab%d" % i, [128, 2, 512], BF16) for i in range(2)]
        acc = [sb("acc%d" % i, [128, 2, 512], BF16) for i in range(3)]
        qpad = [sb("qpad%d" % i, [128, 2, 512], BF16) for i in range(2)]
        accl = sb("accl", [128, 2, 512], BF16)
        rp = None

    KTs = [BIG[:, i * 8192:(i + 1) * 8192] for i in range(2)]
    Vs = [BIG[:, 16384:24576].rearrange("p (t d) -> p t d", d=128) for i in range(2)]
    Wst = [BIG[:, i * 8192:(i + 1) * 8192].bitcast(F32).rearrange("p (k n) -> p k n", k=8) for i in range(2)]
    Wb = [BIG[:, 16384 + i * 4096:16384 + (i + 1) * 4096].rearrange("p (k n) -> p k n", k=8) for i in range(2)]

    PS = [nc.psum_tensor("ps%d" % i, [128, 512], F32).__enter__() for i in range(8)]

    def psbf(i):
        return PS[i][:].bitcast(BF16)

    x_d = din("x", [NT, 1024], F32) if (1 in phases or 2 in phases) else None
    cst_d = din("cst", [128, 2, 128], F32)
    if 1 in phases:
        pos_d = din("pos", [128, 16], I32)
        invf_d = din("invf", [128, 8], F32)
        mem_d = din("mem", [256, 1024], F32)
        w_ain = din("w_ain", [1024, 3584], F32)
        w_memkv = din("w_memkv", [2, 1024, 512], F32)
    gains_d = din("gains", [128, 5, 8], F32)
    hg_d = din("hg", [128, 6, 64], F32)
    mk_d = din("mk", [128, 2, 4, 128], BF16)
    if 2 in phases:
        lamv_d = din("lamv", [128, 4, 64], F32)
        subln_d = din("subln", [128, 1], F32)
        w_aout = din("w_aout", [1024, 1024], F32)
        w_kv = din("w_kv", [1024, 1536], F32)
        w_bin = din("w_bin", [1024, 2048], F32)
    if 3 in phases:
        w_bout = din("w_bout", [1024, 1024], F32)

    def exch(layer):
        if fused:
            own_k = dint("kown%d" % layer, [6, 128, 2048], BF16)
            own_v = dint("vown%d" % layer, [6, 128, 16, 128], BF16)
            all_k = dint("kall%d" % layer, [4, 6, 128, 2048], BF16)
            all_v = dint("vall%d" % layer, [4, 6, 128, 16, 128], BF16)
            return own_k, own_v, all_k, all_v
        own_k = own_v = all_k = all_v = None
        prod = 1 if layer == 0 else 2
        cons = 2 if layer == 0 else 3
        if prod in phases:
            own_k = dout("kown%d" % layer, [6, 128, 2048], BF16)
            own_v = dout("vown%d" % layer, [6, 128, 16, 128], BF16)
        if cons in phases:
            all_k = din("kall%d" % layer, [4, 6, 128, 2048], BF16)
            all_v = din("vall%d" % layer, [4, 6, 128, 16, 128], BF16)
        return own_k, own_v, all_k, all_v

    ex0 = exch(0)
    ex1 = exch(1)
    if fused:
        x1_d = dint("x1", [NT, 1024], F32)
    else:
        x1_d = None
        if 2 in phases:
            x1_d = dout("x1", [NT, 1024], F32)
        if 3 in phases:
            x1_d = din("x1", [NT, 1024], F32)
    out_d = dout("out", [NT, 1024], F32) if 3 in phases else None

    states = {
        "QT": (QT, [128, 6, 2048], BF16, [("QT", p_, t_) for p_ in range(6) for t_ in range(16)]),
        "gateT": (gateT, [128, 8, 2048], BF16, [("gateT", p_, j_) for p_ in range(8) for j_ in range(4)]),
        "qmT": (qmT, [128, 2, 2048], BF16, [("qmT", p_, t_) for p_ in range(2) for t_ in range(16)]),
        "kmpad": (kmpad, [128, 2, 4, 256], BF16, [("kmpad",)]),
        "vmpad": (vmpad, [128, 2, 4, 2, 128], BF16, [("vmpad",)]),
    }

    def load_state(name):
        t, shape, dt, keys = states[name]
        d = din("st_" + name, shape, dt)
        P.op("sp", "dma_start", out=t[:], in_=d, w=keys, dma="st_" + name)

    def store_state(name):
        t, shape, dt, keys = states[name]
        d = dout("so_" + name, shape, dt)
        P.op("sp", "dma_start", out=d, in_=t[:], r=keys, dma="so")

    P.op("sp", "dma_start", out=cst32[:], in_=cst_d, w=[("cst32",)], dma="c0")
    P.op("sp", "dma_start", out=gains[:], in_=gains_d, w=[("gains",)], dma="c0")
    P.op("sp", "dma_start", out=hg[:], in_=hg_d, w=[("hg",)], dma="c0")
    P.op("sp", "dma_start", out=mk[:], in_=mk_d, w=[("mk",)], dma="c0")
    P.op("dve", "tensor_copy", ident[:], cst32[:, 0, :], r=[("cst32",)], w=[("ident",)])
    P.op("dve", "tensor_copy", tri[:], cst32[:, 1, :], r=[("cst32",)], w=[("tri",)])
    P.op("dve", "memset", ones[:], 1.0, w=[("ones",)])
    P.op("dve", "memset", zer[:], 0.0, w=[("zer",)])

    cnt = [0]
    tc_cnt = [0]

    def uid():
        cnt[0] += 1
        return cnt[0]

    def load_weights(w_ap, col0, ncols, gidx, slot):
        src = w_ap[:, col0:col0 + ncols].rearrange("(k p) n -> p k n", p=128)
        P.op("pool", "dma_start", out=Wst[slot][:, :, 0:ncols], in_=src,
              w=[("KTs", slot)], dma="w%d" % slot)
        for kc in range(8):
            if kc % 2 == 0:
                P.op("act", "activation", out=Wb[slot][:, kc, 0:ncols], in_=Wst[slot][:, kc, 0:ncols],
                                                           func=AF.Copy, scale=gains[:, gidx, kc:kc + 1],
                      r=[("KTs", slot), ("gains",)], w=[("Wb", slot, kc), ("VsW", slot)])
            else:
                P.op("dve", "tensor_scalar", Wb[slot][:, kc, 0:ncols], Wst[slot][:, kc, 0:ncols],
                                                              gains[:, gidx, kc:kc + 1], None, ALU.mult,
                      r=[("KTs", slot), ("gains",)], w=[("Wb", slot, kc), ("VsW", slot)])

    def wb_keys(slot):
        return [("Wb", slot, kc) for kc in range(8)] + [("Vs", slot)]

    def wb_wkeys(slot):
        return [("Vs", slot)]

    def rstd_from_ss(out_ap, ss_ap, n, rkeys, wkeys):
        t = uid()
        P.op("act", "activation", out=out_ap, in_=ss_ap, func=AF.Ln, scale=1.0 / n, bias=EPS,
              r=rkeys, w=wkeys)
        P.op("act", "activation", out=out_ap, in_=out_ap, func=AF.Exp, scale=-0.5,
              r=wkeys, w=wkeys)

    def norm_tile_to_T(src_ap_fn, src_keys, slot, dstT, dkey, tt, ncols_tok=128, width=2048):
        xs = xt[slot]
        P.op("act", "activation", out=sq[:], in_=xs[:], func=AF.Square, r=[("xt", slot)], w=SQK)
        ssk = ("ssx", slot)
        P.op("dve", "tensor_reduce", s8[slot][:, 0, 0:1], sq[:], AX.X, ALU.add, r=SQK, w=[ssk])
        rstd_from_ss(s8[slot][:, 0, 1:2], s8[slot][:, 0, 0:1], 1024.0, [ssk], [("rsx", slot)])
        P.op("dve", "tensor_scalar", xnb[slot][:], xs[:], s8[slot][:, 0, 1:2], None, ALU.mult,
              r=[("xt", slot), ("rsx", slot)], w=[("xnb", slot)])
        for half in range(2):
            bank = 6 + half
            for q4 in range(4):
                kc = half * 4 + q4
                P.op("pe", "transpose", psbf(bank)[:, q4 * 128:(q4 + 1) * 128], xnb[slot][:, kc * 128:(kc + 1) * 128], ident[:],
                    r=[("xnb", slot), ("ident",)], w=[("ps", bank)])
            eng = "act" if half == 0 else "dve"
            dst = dstT[:, half * 4:half * 4 + 4, tt * ncols_tok:(tt + 1) * ncols_tok]
            srcp = psbf(bank)[:, 0:512].rearrange("p (a b) -> p a b", b=128)
            if eng == "act":
                P.op("act", "activation", out=dst, in_=srcp, func=AF.Copy,
                      r=[("ps", bank)], w=[(dkey, kc_, tt) for kc_ in range(half * 4, half * 4 + 4)])
            else:
                P.op("dve", "tensor_copy", dst, srcp,
                      r=[("ps", bank)], w=[(dkey, kc_, tt) for kc_ in range(half * 4, half * 4 + 4)])

    def headnorm(bank, slot, hb0, nhb, gain_idx, out_stg, rope, tt):
        c0, c1 = hb0 * 64, (hb0 + nhb) * 64
        ps3 = PS[bank][:, c0:c1].rearrange("p (h d) -> p h d", d=64)
        sq3 = w32[0][:, c0:c1]
        hbs = range(hb0, hb0 + nhb)
        kq = [("w32", 0, hb) for hb in hbs]
        P.op("act", "activation", out=sq3, in_=PS[bank][:, c0:c1], func=AF.Square, r=[("ps", bank)], w=kq)
        kss = [("s8", slot, hb) for hb in hbs]
        ssap = s8[slot][:, 1, hb0:hb0 + nhb]
        rsap = s8[slot][:, 2, hb0:hb0 + nhb]
        P.op("dve", "tensor_reduce", ssap, sq3.rearrange("p (h d) -> p h d", d=64), AX.X, ALU.add,
              r=kq, w=kss)
        krs = [("s8r", slot, hb) for hb in hbs]
        rstd_from_ss(rsap, ssap, 64.0, kss, krs)
        xn3 = w32[1][:, c0:c1].rearrange("p (h d) -> p h d", d=64)
        kxn = [("w32", 1, hb) for hb in hbs]
        P.op("dve", "tensor_tensor", xn3, ps3, rsap.unsqueeze(2).broadcast_to([128, nhb, 64]), ALU.mult,
              r=[("ps", bank)] + krs, w=kxn)
        o3 = out_stg[:, c0:c1].rearrange("p (h d) -> p h d", d=64)
        g3 = hg[:, gain_idx, :].unsqueeze(1).broadcast_to([128, nhb, 64])
        kst = [("stg", slot, hb) for hb in hbs]
        P.op("dve", "tensor_tensor", o3, xn3, g3, ALU.mult, r=kxn + [("hg",)], w=kst)
        if rope:
            R = rp[0]
            kr = ("rp",)
            xg = R[:, 0, hb0:hb0 + nhb, :]
            g16 = hg[:, gain_idx, 0:16].unsqueeze(1).broadcast_to([128, nhb, 16])
            P.op("dve", "tensor_tensor", xg, xn3[:, :, 0:16], g16, ALU.mult, r=kxn + [("hg",)], w=[kr])
            cs = cosT[:, tt, :].unsqueeze(1).broadcast_to([128, nhb, 8])
            sn = sinT[:, tt, :].unsqueeze(1).broadcast_to([128, nhb, 8])
            x1 = xg[:, :, 0:8]
            x2 = xg[:, :, 8:16]
            t1 = R[:, 1, hb0:hb0 + nhb, 0:8]
            t2 = R[:, 1, hb0:hb0 + nhb, 8:16]
            t3 = R[:, 2, hb0:hb0 + nhb, 0:8]
            t4 = R[:, 2, hb0:hb0 + nhb, 8:16]
            P.op("dve", "tensor_tensor", t1, x1, cs, ALU.mult, r=[kr, ("rope",)], w=[("rp1",)])
            P.op("dve", "tensor_tensor", t2, x2, sn, ALU.mult, r=[kr, ("rope",)], w=[("rp2",)])
            P.op("dve", "tensor_tensor", t3, x2, cs, ALU.mult, r=[kr, ("rope",)], w=[("rp3",)])
            P.op("dve", "tensor_tensor", t4, x1, sn, ALU.mult, r=[kr, ("rope",)], w=[("rp4",)])
            P.op("dve", "tensor_tensor", o3[:, :, 0:8], t1, t2, ALU.subtract,
                  r=[("rp1",), ("rp2",)], w=kst)
            P.op("dve", "tensor_tensor", o3[:, :, 8:16], t3, t4, ALU.add,
                  r=[("rp3",), ("rp4",)], w=kst)
        return kst

    def transpose_chunks(slot, src_stg, chunk_list, src_keys, tbank):
        for i, (ci, dst, dk) in enumerate(chunk_list):
            P.op("pe", "transpose", psbf(tbank)[:, i * 128:(i + 1) * 128],
                                                          src_stg[:, ci * 128:(ci + 1) * 128], ident[:],
                  r=list(src_keys) + [("ident",)], w=[("ps", tbank)])
        tc_cnt[0] += 1
        for i, (ci, dst, dk) in enumerate(chunk_list):
            eng = "act" if tc_cnt[0] % 2 == 0 else "dve"
            if eng == "act":
                P.op("act", "activation", out=dst, in_=psbf(tbank)[:, i * 128:(i + 1) * 128], func=AF.Copy,
                      r=[("ps", tbank)], w=dk)
            else:
                P.op("dve", "tensor_copy", dst, psbf(tbank)[:, i * 128:(i + 1) * 128],
                      r=[("ps", tbank)], w=dk)

    def proj_tok(slot_w, bank, tt, ncols, srcT, skey):
        for kc in range(8):
            P.op("pe", "matmul", PS[bank][:, 0:ncols], srcT[:, kc, tt * 128:(tt + 1) * 128],
                                                  Wb[slot_w][:, kc, 0:ncols], start=(kc == 0), stop=(kc == 7),
                  r=[(skey, kc, tt), ("Wb", slot_w, kc), ("VsW", slot_w)], w=[("ps", bank)])

    def proj_feat(slot_w, bank, chunk, j, srcT, skey):
        for kc in range(8):
            P.op("pe", "matmul", PS[bank][:, :], Wb[slot_w][:, kc, chunk * 128:(chunk + 1) * 128],
                                                  srcT[:, kc, j * 512:(j + 1) * 512], start=(kc == 0), stop=(kc == 7),
                  r=[(skey, kc, t_) for t_ in range(4 * j, 4 * j + 4)] + [("Wb", slot_w, kc), ("VsW", slot_w)], w=[("ps", bank)])

    def silu_evac(bank, dst, dkeys, wi):
        e = w32[wi][:, :]
        ke = ("w32", wi)
        P.op("act", "activation", out=e, in_=PS[bank][:, :], func=AF.Exp, scale=-1.0, r=[("ps", bank)], w=[ke])
        P.op("dve", "tensor_scalar", e, e, 1.0, None, ALU.add, r=[ke], w=[ke])
        P.op("dve", "reciprocal", e, e, r=[ke], w=[ke])
        P.op("dve", "tensor_tensor", dst, PS[bank][:, :], e, ALU.mult, r=[ke, ("ps", bank)], w=dkeys)

    def phase1():
        kown, vown = ex0[0], ex0[1]
        P.op("sp", "dma_start", out=posi[:], in_=pos_d, w=[("posi",)], dma="c1")
        P.op("sp", "dma_start", out=invf[:], in_=invf_d, w=[("invf",)], dma="c1")
        pf = sml
        posf = rt[2][:, :, 0]
        P.op("dve", "tensor_copy", posf, posi[:], r=[("posi",)], w=[("posf",)])
        ang = rt[0]
        P.op("dve", "tensor_tensor", ang[:], posf.unsqueeze(2).broadcast_to([128, 16, 8]),
                                               invf[:].unsqueeze(1).broadcast_to([128, 16, 8]), ALU.mult,
              r=[("posf",), ("invf",)], w=[("ang",)])
        for which, shift, dst in (("s", 0.0, sinT), ("c", math.pi / 2, cosT)):
            a2 = rt[1]
            k2 = ("a2",)
            P.op("dve", "tensor_scalar", a2[:], ang[:], shift, None, ALU.add, r=[("ang",)], w=[k2])
            kfl = rt[2]
            P.op("dve", "tensor_scalar", kfl[:], a2[:], 1.0 / (2 * math.pi), None, ALU.mult, r=[k2], w=[("kfl",), ("posf",)])
            P.op("dve", "tensor_copy", rti[:], kfl[:], r=[("kfl",)], w=[("rti",)])
            P.op("dve", "tensor_copy", kfl[:], rti[:], r=[("rti",)], w=[("kfl",)])
            P.op("dve", "scalar_tensor_tensor", a2[:], kfl[:], -2 * math.pi, a2[:], ALU.mult, ALU.add,
                  r=[("kfl",), k2], w=[k2])
            P.op("dve", "tensor_scalar", a2[:], a2[:], 3.1415925, -3.1415925, ALU.min, ALU.max, r=[k2], w=[k2])
            P.op("act", "activation", out=dst[:], in_=a2[:], func=AF.Sin, r=[k2], w=[("rope",)])
            if which == "s":
                pass

        for mt in range(2):
            slot = mt
            P.op("sp", "dma_start", out=xt[slot][:], in_=mem_d[mt * 128:(mt + 1) * 128, :],
                  w=[("xt", slot)], dma="xt%d" % slot)
            norm_tile_to_T(None, None, slot, memT, "memT", mt, ncols_tok=128)
        for l in range(2):
            slotw = l % 2
            src = w_memkv[l].rearrange("(k p) n -> p k n", p=128)
            P.op("pool", "dma_start", out=Wst[slotw][:, :, :], in_=src,
                  w=[("KTs", slotw)], dma="w%d" % slotw)
            for kc in range(8):
                P.op("dve", "tensor_scalar", Wb[slotw][:, kc, :], Wst[slotw][:, kc, :], gains[:, 3 + l, kc:kc + 1], None, ALU.mult,
                    r=[("KTs", slotw), ("gains",)], w=[("Wb", slotw, kc), ("VsW", slotw)])
            P.op("dve", "memset", kmpad[:, l], 0.0, w=[("kmpad",)])
            P.op("dve", "memset", vmpad[:, l], 0.0, w=[("vmpad",)])
            for mt in range(2):
                bank = mt
                proj_tok(slotw, bank, mt, 512, memT, "memT")
                sl = mt
                kst = headnorm(bank, sl, 0, 4, 3 + 2 * l, stg[sl], False, 0)
                for h in range(4):
                    dst = vmpad[:, l, h, mt, (h % 2) * 64:(h % 2) * 64 + 64]
                    P.op("dve", "tensor_copy", dst, PS[bank][:, 256 + h * 64:256 + (h + 1) * 64],
                        r=[("ps", bank)], w=[("vmpad",)])
                chunk_list = []
                for ci in range(2):
                    chunk_list.append((ci, tst[sl][:, ci, :], [("tst", sl, ci)]))
                transpose_chunks(sl, stg[sl], chunk_list, kst, 6 + mt)
                for h in (3, 2, 1, 0):
                    r0 = (h % 2) * 64
                    P.op("dve", "tensor_copy", kmpad[r0:r0 + 64, l, h, mt * 128:(mt + 1) * 128], tst[sl][r0:r0 + 64, h // 2, :],
                        r=[("tst", sl, h // 2)], w=[("kmpad",)])

        for tt in range(16):
            slot = tt % 2
            P.op("sp", "dma_start", out=xt[slot][:], in_=x_d[tt * 128:(tt + 1) * 128, :],
                  w=[("xt", slot)], dma="xt%d" % slot)
            norm_tile_to_T(None, None, slot, hy, "hy", tt)

        groups = [
            (0, [("q", 0, 8, 0)]),
            (512, [("q", 0, 4, 4), ("k", 4, 4, 0)]),
            (1024, [("k", 0, 8, 2)]),
            (1536, [("v", 0, 512, 0)]),
            (2048, [("v", 0, 256, 512), ("qm", 4, 4, 0)]),
        ]
        for gi, (col0, parts) in enumerate(groups):
            slotw = gi % 2
            load_weights(w_ain, col0, 512, 0, slotw)
            proj_tok(slotw, 0, 0, 512, hy, "hy")
            for tt in range(16):
                bank = tt % 2
                sl = tt % 2
                if tt + 1 < 16:
                    proj_tok(slotw, (tt + 1) % 2, tt + 1, 512, hy, "hy")
                chunk_list = []
                skeys = []
                for part in parts:
                    kind = part[0]
                    if kind in ("q", "k"):
                        _, hb0, nhb, pair0 = part
                        kst = headnorm(bank, sl, hb0, nhb, 0 if kind == "q" else 1, stg[sl], True, tt)
                        skeys.extend(kst)
                        for ci in range(nhb // 2):
                            pair = pair0 + ci
                            if kind == "q":
                                chunk_list.append((hb0 // 2 + ci, QT[:, pair, tt * 128:(tt + 1) * 128], [("QT", pair, tt)]))
                            else:
                                chunk_list.append((hb0 // 2 + ci, tst[sl][:, hb0 // 2 + ci, :], [("tst", sl, hb0 // 2 + ci)]))
                    elif kind == "qm":
                        _, hb0, nhb, _ = part
                        kst = headnorm(bank, sl, hb0, nhb, 2, stg[sl], False, tt)
                        skeys.extend(kst)
                        for ci in range(2):
                            chunk_list.append((hb0 // 2 + ci, qmT[:, ci, tt * 128:(tt + 1) * 128], [("qmT", ci, tt)]))
                    else:
                        _, c0, nc_, vc0 = part
                        P.op("act", "activation", out=vst[sl][:, vc0:vc0 + nc_], in_=PS[bank][:, c0:c0 + nc_], func=AF.Copy,
                            r=[("ps", bank)], w=[("vst", sl, vc0)])
                        dst = vown[vc0 // 128:(vc0 + nc_) // 128, :, tt, :].rearrange("h p d -> p h d")
                        P.op("sp", "dma_start", out=dst, in_=vst[sl][:, vc0:vc0 + nc_].rearrange("p (h d) -> p h d", d=128),
                            r=[("vst", sl, vc0)], w=[("vown", tt, vc0)], dma="vo")
                if chunk_list:
                    transpose_chunks(sl, stg[sl], chunk_list, skeys, 6 + sl)
                    for (ci, dst, dk) in chunk_list:
                        if dk[0][0] == "tst":
                            kpair = None
                            for part in parts:
                                if part[0] == "k":
                                    kpair = part[3] + (ci - part[1] // 2)
                            P.op("sp", "dma_start", out=kown[kpair, :, tt * 128:(tt + 1) * 128], in_=tst[sl][:, ci, :],
                                r=dk, w=[("kown", kpair, tt)], dma="ko")
        for gg in range(2):
            slotw = (5 + gg) % 2
            load_weights(w_ain, 2560 + gg * 512, 512, 0, slotw)
            for ch in range(4):
                for j in range(4):
                    bank = 2 + (ch * 4 + j) % 4
                    proj_feat(slotw, bank, ch, j, hy, "hy")
                    chunk = gg * 4 + ch
                    silu_evac(bank, gateT[:, chunk, j * 512:(j + 1) * 512], [("gateT", chunk, j)], 2 + (ch * 4 + j) % 4)

    def load_kv(all_k, all_v, pair, slot):
        for rnk in range(4):
            P.op("sp", "dma_start", out=KTs[slot][:, rnk * 2048:(rnk + 1) * 2048], in_=all_k[rnk, pair],
                  w=[("KTs", slot)], dma="kt%d" % slot)
            P.op("sp", "dma_start", out=Vs[slot][:, rnk * 16:(rnk + 1) * 16, :], in_=all_v[rnk, pair],
                  w=[("Vs", 0), ("VsW", 0), ("VsW", 1)], dma="vs0")

    def key_steps(j):
        steps = []
        for m in (3, 2, 1, 0):
            for r in (3, 2, 1, 0):
                steps.append((r, 4 * j + m, m, r))
        for g in range(16 * j - 1, -1, -1):
            jj, rem = divmod(g, 16)
            s_, c_ = divmod(rem, 4)
            steps.append((c_, 4 * jj + s_, None, None))
        return steps

    def make_qpad(pair, j, qs, scale):
        P.op("pool", "memset", qpad[qs][:], 0.0, w=[("qpad", qs)])
        for hh in range(2):
            r0 = hh * 64
            if scale == 1.0:
                P.op("pool", "tensor_copy", qpad[qs][r0:r0 + 64, hh, :], QT[r0:r0 + 64, pair, j * 512:(j + 1) * 512],
                      r=[("QT", pair, t_) for t_ in range(4 * j, 4 * j + 4)], w=[("qpad", qs)])
            else:
                P.op("dve", "tensor_scalar", qpad[qs][r0:r0 + 64, hh, :], QT[r0:r0 + 64, pair, j * 512:(j + 1) * 512],
                                                                  scale, None, ALU.mult,
                      r=[("QT", pair, t_) for t_ in range(4 * j, 4 * j + 4)], w=[("qpad", qs)])

    def attention0(all_k, all_v, finalize):
        it = 0
        for pair in range(6):
            slot = pair % 2
            load_kv(all_k, all_v, pair, slot)
            for j in range(4):
                qs = (pair * 4 + j) % 2
                make_qpad(pair, j, qs, 1.0)
                for b in (4, 5):
                    P.op("pe", "matmul", PS[b][:, :], zer[:, 0:128], zer[:, :], start=True, stop=False,
                          r=[("zer",)], w=[("ps", b)])
                P.op("pool", "memset", accl[:], 0.0, w=[("accl",)])
                steps = key_steps(j)
                prev = None
                for si, (rnk, lt, m, r) in enumerate(steps):
                    buf = it % 2
                    it += 1
                    c0 = 0 if m is None else 128 * m
                    kcol = rnk * 2048 + lt * 128
                    vt = rnk * 16 + lt
                    last = (si == len(steps) - 1)
                    for c in range(2):
                        P.op("pe", "matmul", PS[buf * 2 + c][:, c0:512], KTs[slot][:, kcol:kcol + 128], qpad[qs][:, c, c0:512],
                            start=True, stop=True,
                            r=[("KTs", slot), ("qpad", qs)], w=[("ps", buf * 2 + c)])
                    for c in range(2):
                        P.op("act", "activation", out=pb[buf][:, c, c0:512], in_=PS[buf * 2 + c][:, c0:512], func=AF.Exp, scale=0.125,
                            r=[("ps", buf * 2 + c)], w=[("pb", buf, c)])
                    if m is not None:
                        P.op("dve", "tensor_tensor", pb[buf][:, :, c0:c0 + 128], pb[buf][:, :, c0:c0 + 128],
                            mk[:, 0, r, :].unsqueeze(1).broadcast_to([128, 2, 128]), ALU.mult,
                            r=[("pb", buf, 0), ("pb", buf, 1), ("mk",)], w=[("pb", buf, 0), ("pb", buf, 1)])
                    P.op("dve", "tensor_tensor", accl[:, :, c0:512], accl[:, :, c0:512], pb[buf][:, :, c0:512], ALU.add,
                          r=[("accl",), ("pb", buf, 0), ("pb", buf, 1)], w=[("accl",)])
                    if prev is not None:
                        emit_pv0(prev, slot, False)
                    prev = (buf, c0, vt)
                emit_pv0(prev, slot, True)
                for c in range(2):
                    P.op("pe", "matmul", PS[6 + c][:, :], ones[:], accl[:, c, :], start=True, stop=True,
                          r=[("ones",), ("accl",)], w=[("ps", 6 + c)])
                finalize(pair, j)

    def emit_pv0(prev, slot, last):
        buf, c0, vt = prev
        for c in range(2):
            P.op("pe", "matmul", PS[4 + c][:, c0:512], Vs[slot][:, vt, :], pb[buf][:, c, c0:512],
                                                start=False, stop=last,
                  r=[("Vs", 0), ("VsW", 0), ("VsW", 1), ("pb", buf, c)], w=[("ps", 4 + c)])

    def finalize0(pair, j):
        r0, r1, t0, t1 = w32[2], w32[3], w32[4], w32[5]
        P.op("dve", "reciprocal", r0[:], PS[6][:, :], r=[("ps", 6)], w=[("w32", 2)])
        P.op("dve", "reciprocal", r1[:], PS[7][:, :], r=[("ps", 7)], w=[("w32", 3)])
        P.op("dve", "tensor_tensor", t0[:], PS[4][:, :], r0[:], ALU.mult, r=[("ps", 4), ("w32", 2)], w=[("w32", 4)])
        P.op("dve", "tensor_tensor", t1[:], PS[5][:, :], r1[:], ALU.mult, r=[("ps", 5), ("w32", 3)], w=[("w32", 5)])
        P.op("dve", "scalar_tensor_tensor", t0[:], t1[:], sml[:, 2:3], t0[:], ALU.mult, ALU.add,
              r=[("w32", 4), ("w32", 5), ("sml",)], w=[("w32", 4)])
        P.op("act", "activation", out=pb[0][:, 0, :], in_=t0[:], func=AF.Square, r=[("w32", 4)], w=[("pb", 0, 0)])
        P.op("pe", "matmul", PS[6][:, :], ones[:], pb[0][:, 0, :], start=True, stop=True,
              r=[("ones",), ("pb", 0, 0)], w=[("ps", 6)])
        P.op("act", "activation", out=r0[:], in_=PS[6][:, :], func=AF.Ln, scale=1.0 / 128, bias=EPS, r=[("ps", 6)], w=[("w32", 2)])
        P.op("act", "activation", out=r0[:], in_=r0[:], func=AF.Exp, scale=-0.5, r=[("w32", 2)], w=[("w32", 2)])
        P.op("dve", "scalar_tensor_tensor", t0[:], t0[:], sml[:, 1:2], r0[:], ALU.mult, ALU.mult,
              r=[("w32", 4), ("w32", 2), ("sml",)], w=[("w32", 4)])
        P.op("dve", "tensor_tensor", hy[:, pair, j * 512:(j + 1) * 512], t0[:], gateT[:, pair, j * 512:(j + 1) * 512], ALU.mult,
              r=[("w32", 4), ("gateT", pair, j)], w=[("hy", pair, t_) for t_ in range(4 * j, 4 * j + 4)])

    def attention1(all_k, all_v, finalize):
        for pair in range(6):
            slot = pair % 2
            load_kv(all_k, all_v, pair, slot)
            for j in range(4):
                qs = (pair * 4 + j) % 2
                make_qpad(pair, j, qs, 0.125)
                for b in (6, 7):
                    P.op("pe", "matmul", PS[b][:, :], zer[:, 0:128], zer[:, :], start=True, stop=False,
                          r=[("zer",)], w=[("ps", b)])
                steps = key_steps(j)
                n = len(steps)
                P.op("pool", "memset", acc[0][:], 0.0, w=[("acc", 0)])

                def info(si):
                    rnk, lt, m, r = steps[si]
                    c0 = 0 if m is None else 128 * m
                    kcol = rnk * 2048 + lt * 128
                    return c0, KTs[slot][:, kcol:kcol + 128], rnk * 16 + lt, m, r

                def stageA(si):
                    c0, ksl, vt, m, r = info(si)
                    buf = si % 2
                    for hh in range(2):
                        zb = buf * 2 + hh
                        P.op("pe", "matmul", PS[zb][:, c0:512], ksl, qpad[qs][:, hh, c0:512], start=True, stop=True,
                              r=[("KTs", slot), ("qpad", qs)], w=[("ps", zb)])
                    for hh in range(2):
                        zb = buf * 2 + hh
                        P.op("act", "activation", out=PS[zb][:, c0:512], in_=PS[zb][:, c0:512], func=AF.Exp,
                              r=[("ps", zb)], w=[("ps", zb)])
                    for hh in range(2):
                        zb = buf * 2 + hh
                        P.op("act", "activation", out=pb[buf][:, hh, c0:512], in_=PS[zb][:, c0:512], func=AF.Ln, bias=1.0,
                              r=[("ps", zb)], w=[("pb", buf, hh)])
                    if m is not None:
                        P.op("dve", "tensor_tensor", pb[buf][:, :, c0:c0 + 128], pb[buf][:, :, c0:c0 + 128],
                              mk[:, 1, r, :].unsqueeze(1).broadcast_to([128, 2, 128]), ALU.mult,
                              r=[("pb", buf, 0), ("pb", buf, 1), ("mk",)], w=[("pb", buf, 0), ("pb", buf, 1)])
                    if si < n - 1:
                        cur, nxt = si % 3, (si + 1) % 3
                        if c0 > 0:
                            P.op("pool", "memset", acc[nxt][:, :, 0:c0], 0.0, w=[("acc", nxt)])
                        P.op("dve", "tensor_tensor", acc[nxt][:, :, c0:512], acc[cur][:, :, c0:512], pb[buf][:, :, c0:512], ALU.subtract,
                              r=[("acc", cur), ("pb", buf, 0), ("pb", buf, 1)], w=[("acc", nxt)])

                def stageB(si):
                    c0, ksl, vt, m, r = info(si)
                    buf = si % 2
                    cur = si % 3
                    for hh in range(2):
                        cb = 4 + hh
                        P.op("pe", "matmul", PS[cb][:, c0:512], ksl, qpad[qs][:, hh, c0:512], start=True, stop=False,
                              r=[("KTs", slot), ("qpad", qs)], w=[("ps", cb)])
                        P.op("pe", "matmul", PS[cb][:, c0:512], tri[:], pb[buf][:, hh, c0:512], start=False, stop=(si == 0),
                              r=[("tri",), ("pb", buf, hh)], w=[("ps", cb)])
                        if si > 0:
                            P.op("pe", "matmul", PS[cb][:, c0:512], ones[:], acc[cur][:, hh, c0:512], start=False, stop=True,
                                  r=[("ones",), ("acc", cur)], w=[("ps", cb)])
                    for hh in range(2):
                        cb = 4 + hh
                        P.op("act", "activation", out=ab[buf][:, hh, c0:512], in_=PS[cb][:, c0:512], func=AF.Exp,
                              r=[("ps", cb)], w=[("ab", buf, hh)])
                    if m is not None:
                        P.op("dve", "tensor_tensor", ab[buf][:, :, c0:c0 + 128], ab[buf][:, :, c0:c0 + 128],
                              mk[:, 1, r, :].unsqueeze(1).broadcast_to([128, 2, 128]), ALU.mult,
                              r=[("ab", buf, 0), ("ab", buf, 1), ("mk",)], w=[("ab", buf, 0), ("ab", buf, 1)])

                def stagePV(si, last):
                    c0, ksl, vt, m, r = info(si)
                    buf = si % 2
                    for hh in range(2):
                        P.op("pe", "matmul", PS[6 + hh][:, c0:512], Vs[slot][:, vt, :], ab[buf][:, hh, c0:512],
                              start=False, stop=last,
                              r=[("Vs", 0), ("VsW", 0), ("VsW", 1), ("ab", buf, hh)], w=[("ps", 6 + hh)])

                stageA(0)
                for si in range(n):
                    if si + 1 < n:
                        stageA(si + 1)
                    stageB(si)
                    if si > 0:
                        stagePV(si - 1, False)
                stagePV(n - 1, True)
                finalize(pair, j)

    def emit_pv1(prev, slot, last):
        buf, c0, vt = prev
        for hh in range(2):
            P.op("pe", "matmul", PS[6 + hh][:, c0:512], Vs[slot][:, vt, :], ab[buf][:, hh, c0:512],
                                                  start=False, stop=last,
                  r=[("Vs", 0), ("VsW", 0), ("VsW", 1), ("ab", buf, hh)], w=[("ps", 6 + hh)])

    def finalize1(pair, j):
        for hh in range(2):
            r0 = hh * 64
            P.op("dve", "tensor_tensor", hy[r0:r0 + 64, pair, j * 512:(j + 1) * 512], PS[6 + hh][r0:r0 + 64, :],
                gateT[r0:r0 + 64, pair, j * 512:(j + 1) * 512], ALU.mult,
                r=[("ps", 6 + hh), ("gateT", pair, j)], w=[("hy", pair, t_) for t_ in range(4 * j, 4 * j + 4)])

    def mem_attention(l):
        for ci in range(2):
            for j in range(4):
                P.op("pe", "matmul", PS[3][:, :], zer[:, 0:128], zer[:, :], start=True, stop=False,
                      r=[("zer",)], w=[("ps", 3)])
                for hh in range(2):
                    h = ci * 2 + hh
                    buf = hh
                    for mt in range(2):
                        P.op("pe", "matmul", PS[mt][:, :], kmpad[:, l, h, mt * 128:(mt + 1) * 128],
                                                                   qmT[:, ci, j * 512:(j + 1) * 512], start=True, stop=True,
                              r=[("kmpad",)] + [("qmT", ci, t_) for t_ in range(4 * j, 4 * j + 4)], w=[("ps", mt)])
                        P.op("act", "activation", out=pb[buf][:, mt, :], in_=PS[mt][:, :], func=AF.Exp, scale=0.125,
                              r=[("ps", mt)], w=[("pb", buf, mt)])
                    for mt in range(2):
                        P.op("pe", "matmul", PS[2][:, :], ones[:], pb[buf][:, mt, :], start=(mt == 0), stop=(mt == 1),
                              r=[("ones",), ("pb", buf, mt)], w=[("ps", 2)])
                    rl = w32[2 + hh]
                    P.op("dve", "reciprocal", rl[:], PS[2][:, :], r=[("ps", 2)], w=[("w32", 2 + hh)])
                    P.op("dve", "tensor_tensor", ab[buf][:, :, :], pb[buf][:, :, :],
                                                                           rl[:].unsqueeze(1).broadcast_to([128, 2, 512]), ALU.mult,
                          r=[("w32", 2 + hh), ("pb", buf, 0), ("pb", buf, 1)], w=[("ab", buf, 0), ("ab", buf, 1)])
                    for mt in range(2):
                        P.op("pe", "matmul", PS[3][:, :], vmpad[:, l, h, mt, :], ab[buf][:, mt, :],
                                                                                   start=False, stop=(hh == 1 and mt == 1),
                              r=[("vmpad",), ("ab", buf, mt)], w=[("ps", 3)])
                ch = 6 + ci
                P.op("dve", "tensor_tensor", hy[:, ch, j * 512:(j + 1) * 512], PS[3][:, :],
                                                              gateT[:, ch, j * 512:(j + 1) * 512], ALU.mult,
                      r=[("ps", 3), ("gateT", ch, j)], w=[("hy", ch, t_) for t_ in range(4 * j, 4 * j + 4)])

    def out_proj(w_ap, res_d, dst_d, after_tile, res_key, dst_key):
        for half in range(2):
            src = w_ap[:, half * 512:(half + 1) * 512].rearrange("(k p) n -> p k n", p=128)
            P.op("pool", "dma_start", out=Wst[half][:, :, :], in_=src,
                  w=[("KTs", half)], dma="w%d" % half)
            for kc in range(8):
                eng = "act" if kc % 2 == 0 else "dve"
                if eng == "act":
                    P.op("act", "activation", out=Wb[half][:, kc, :], in_=Wst[half][:, kc, :], func=AF.Copy,
                          r=[("KTs", half)], w=[("Wb", half, kc), ("VsW", half)])
                else:
                    P.op("dve", "tensor_copy", Wb[half][:, kc, :], Wst[half][:, kc, :],
                          r=[("KTs", half)], w=[("Wb", half, kc), ("VsW", half)])
        def op_mm(tt):
            for half in range(2):
                bank = (tt % 2) * 2 + half
                for kc in range(8):
                    P.op("pe", "matmul", PS[bank][:, :], hy[:, kc, tt * 128:(tt + 1) * 128], Wb[half][:, kc, :], start=(kc == 0), stop=(kc == 7),
                        r=[("hy", kc, tt), ("Wb", half, kc), ("VsW", half)], w=[("ps", bank)])

        op_mm(0)
        for tt in range(16):
            slot = tt % 2
            P.op("sp", "dma_start", out=xt[slot][:], in_=res_d[tt * 128:(tt + 1) * 128, :],
                  r=[("dram", res_key, tt)], w=[("xt", slot)], dma="xt%d" % slot)
            if tt + 1 < 16:
                op_mm(tt + 1)
            for half in range(2):
                bank = (tt % 2) * 2 + half
                P.op("dve", "tensor_tensor", xt[slot][:, half * 512:(half + 1) * 512], xt[slot][:, half * 512:(half + 1) * 512], PS[bank][:, :], ALU.add,
                    r=[("ps", bank), ("xt", slot)], w=[("xt", slot)])
            P.op("sp", "dma_start", out=dst_d[tt * 128:(tt + 1) * 128, :], in_=xt[slot][:],
                  r=[("xt", slot)], w=[("dram", dst_key, tt)], dma="od")
            if after_tile is not None:
                after_tile(tt, slot)

    def phase2():
        kown1, vown1 = ex1[0], ex1[1]
        P.op("sp", "dma_start", out=lamv[:], in_=lamv_d, w=[("lamv",)], dma="c2")
        P.op("sp", "dma_start", out=sml[:, 0:1], in_=subln_d, w=[("sml",)], dma="c2")
        pr = w32[0][:, 0:128].rearrange("p (a d) -> p a d", d=64)
        P.op("dve", "tensor_tensor", pr[:, 0, :], lamv[:, 0, :], lamv[:, 1, :], ALU.mult, r=[("lamv",)], w=[("w32", 0, 0)])
        P.op("dve", "tensor_tensor", pr[:, 1, :], lamv[:, 2, :], lamv[:, 3, :], ALU.mult, r=[("lamv",)], w=[("w32", 0, 1)])
        P.op("dve", "tensor_reduce", sml[:, 4:6], pr, AX.X, ALU.add, r=[("w32", 0, 0), ("w32", 0, 1)], w=[("sml4",)])
        P.op("act", "activation", out=sml[:, 6:8], in_=sml[:, 4:6], func=AF.Exp, r=[("sml4",)], w=[("sml6",)])
        P.op("dve", "tensor_tensor", sml[:, 2:3], sml[:, 7:8], sml[:, 6:7], ALU.subtract, r=[("sml6",)], w=[("sml2",)])
        P.op("dve", "tensor_scalar", sml[:, 2:3], sml[:, 2:3], -LAMBDA_INIT0, None, ALU.add, r=[("sml2",)], w=[("sml2",)])
        P.op("dve", "tensor_scalar", sml[:, 1:2], sml[:, 0:1], 1.0 - LAMBDA_INIT0, None, ALU.mult,
              r=[("sml",), ("sml2",)], w=[("sml",)])

        attention0(ex0[2], ex0[3], finalize0)
        mem_attention(0)

        def after(tt, slot):
            norm_tile_to_T(None, None, slot, hy, "hy", tt)
        out_proj(w_aout, x_d, x1_d, after, "x", "x1")

        for gi, (col0, ncols) in enumerate(((0, 512), (512, 256))):
            slotw = gi % 2
            load_weights(w_kv, col0, ncols, 1, slotw)
            for ch in range(ncols // 128):
                pair = col0 // 128 + ch
                for j in range(4):
                    bank = 2 + (ch * 4 + j) % 4
                    proj_feat(slotw, bank, ch, j, hy, "hy")
                    sl = (ch * 4 + j) % 2
                    P.op("act", "activation", out=stg[sl][:, :], in_=PS[bank][:, :], func=AF.Copy,
                          r=[("ps", bank)], w=[("stg", sl, hb) for hb in range(8)])
                    P.op("sp", "dma_start", out=kown1[pair, :, j * 512:(j + 1) * 512], in_=stg[sl][:, :],
                          r=[("stg", sl, hb) for hb in range(8)], w=[("kown1", pair, j)], dma="ko")
        for gi, (col0, ncols, vc0) in enumerate(((768, 512, 0), (1280, 256, 512))):
            slotw = gi % 2
            load_weights(w_kv, col0, ncols, 1, slotw)
            for tt in range(16):
                bank = tt % 2
                sl = tt % 2
                proj_tok(slotw, bank, tt, ncols, hy, "hy")
                P.op("act", "activation", out=vst[sl][:, vc0:vc0 + ncols], in_=PS[bank][:, 0:ncols], func=AF.Copy,
                    r=[("ps", bank)], w=[("vst", sl, vc0)])
                dst = vown1[vc0 // 128:(vc0 + ncols) // 128, :, tt, :].rearrange("h p d -> p h d")
                P.op("sp", "dma_start", out=dst, in_=vst[sl][:, vc0:vc0 + ncols].rearrange("p (h d) -> p h d", d=128),
                    r=[("vst", sl, vc0)], w=[("vown1", tt, vc0)], dma="vo")
        for gi, (col0, ncols) in enumerate(((0, 512), (512, 256))):
            slotw = gi % 2
            load_weights(w_bin, col0, ncols, 2, slotw)
            for ch in range(ncols // 128):
                pair = col0 // 128 + ch
                for j in range(4):
                    bank = 2 + (ch * 4 + j) % 4
                    proj_feat(slotw, bank, ch, j, hy, "hy")
                    if (ch * 4 + j) % 2 == 0:
                        P.op("act", "activation", out=QT[:, pair, j * 512:(j + 1) * 512], in_=PS[bank][:, :], func=AF.Copy,
                              r=[("ps", bank)], w=[("QT", pair, t_) for t_ in range(4 * j, 4 * j + 4)])
                    else:
                        P.op("dve", "tensor_copy", QT[:, pair, j * 512:(j + 1) * 512], PS[bank][:, :],
                              r=[("ps", bank)], w=[("QT", pair, t_) for t_ in range(4 * j, 4 * j + 4)])
        load_weights(w_bin, 768, 256, 2, 0)
        proj_tok(0, 0, 0, 256, hy, "hy")
        for tt in range(16):
            bank = tt % 2
            sl = tt % 2
            if tt + 1 < 16:
                proj_tok(0, (tt + 1) % 2, tt + 1, 256, hy, "hy")
            kst = headnorm(bank, sl, 0, 4, 4, stg[sl], False, tt)
            chunk_list = [(ci, qmT[:, ci, tt * 128:(tt + 1) * 128], [("qmT", ci, tt)]) for ci in range(2)]
            transpose_chunks(sl, stg[sl], chunk_list, kst, 6 + sl)
        for gg in range(2):
            slotw = (1 + gg) % 2
            load_weights(w_bin, 1024 + gg * 512, 512, 2, slotw)
            for ch in range(4):
                for j in range(4):
                    bank = 2 + (ch * 4 + j) % 4
                    proj_feat(slotw, bank, ch, j, hy, "hy")
                    chunk = gg * 4 + ch
                    silu_evac(bank, gateT[:, chunk, j * 512:(j + 1) * 512], [("gateT", chunk, j)], 2 + (ch * 4 + j) % 4)

    def phase3():
        attention1(ex1[2], ex1[3], finalize1)
        mem_attention(1)
        out_proj(w_bout, x1_d, out_d, None, "x1", "out")

    final_groups = []
    if fused:
        raise NotImplementedError
    else:
        if 1 in phases:
            phase1()
            for nm in states:
                store_state(nm)
            final_groups += ["ko", "vo", "so"]
        if 2 in phases:
            for nm in ("QT", "gateT", "qmT", "kmpad", "vmpad"):
                load_state(nm)
            phase2()
            for nm in ("QT", "gateT", "qmT"):
                store_state(nm)
            final_groups += ["ko", "vo", "so", "od"]
        if 3 in phases:
            for nm in ("QT", "gateT", "qmT", "kmpad", "vmpad"):
                load_state(nm)
            phase3()
            final_groups += ["od"]
    P.emit(final_groups)
    return nc, ins_names, outs_names


def _prep_common(inputs):
    f32 = np.float32
    cst = np.zeros((128, 2, 128), f32)
    cst[:, 0, :] = np.eye(128, dtype=f32)
    jj = np.arange(128)[:, None]
    ss = np.arange(128)[None, :]
    cst[:, 1, :] = -(jj >= ss).astype(f32)
    inv = (np.float32(500000.0) ** (-(np.arange(0, 16, 2, dtype=np.float32)) / np.float32(16))).astype(f32)
    invf = np.broadcast_to(inv[None, :], (128, 8)).copy()

    def pk(v):
        return np.ascontiguousarray(np.asarray(v, f32).reshape(8, 128).T)

    gains = np.stack([pk(inputs["a_norm"][0]), pk(inputs["kv_norm"]), pk(inputs["b_norm"][0]),
                      pk(inputs["mem_norm"][0]), pk(inputs["mem_norm"][1])], axis=1)
    hgl = [inputs["a_q_norm"][0], inputs["a_k_norm"][0], inputs["mem_q_norm"][0], inputs["mem_k_norm"][0],
           inputs["mem_q_norm"][1], inputs["mem_k_norm"][1]]
    hg = np.broadcast_to(np.stack([np.asarray(v, f32) for v in hgl], 0)[None], (128, 6, 64)).copy()
    lamv = np.broadcast_to(np.stack([np.asarray(inputs[k][0], f32) for k in
                                     ("a_lambda_q1", "a_lambda_k1", "a_lambda_q2", "a_lambda_k2")], 0)[None], (128, 4, 64)).copy()
    subln = np.asarray(inputs["a_subln"][0], f32).reshape(128, 1).copy()
    w = np.asarray(inputs["a_w_in"][0], f32)
    perm = []
    for base in (0, 768):
        for h in range(6):
            for c in range(2):
                perm.extend(range(base + c * 384 + h * 64, base + c * 384 + h * 64 + 64))
    perm.extend(range(1536, 3584))
    w_ain = np.ascontiguousarray(w[:, perm])
    return dict(cst=cst, invf=invf, gains=gains, hg=hg, lamv=lamv, subln=subln, w_ain=w_ain)


def _masks(c):
    mk = np.zeros((128, 2, 4, 128), np.float32)
    p = np.arange(128)[:, None]
    q = np.arange(128)[None, :]
    for r in range(4):
        if r < c:
            mk[:, :, r, :] = 1.0
        elif r == c:
            mk[:, 0, r, :] = (p <= q)
            mk[:, 1, r, :] = (p < q)
    return mk.astype(ml_dtypes.bfloat16)


_CACHE = {}


def _get(phases, fused):
    key = (tuple(sorted(phases)), fused)
    if key not in _CACHE:
        _CACHE[key] = build(set(phases), fused)
    return _CACHE[key]


def _gather(owns, b):
    return np.stack([owns[b * 4 + c] for c in range(4)], 0)


def kernel(**inputs):
    f32 = np.float32
    com = _prep_common(inputs)
    x = np.asarray(inputs["x"], f32)
    mem = np.asarray(inputs["mem"], f32)
    pos = np.asarray(inputs["positions"], np.int32)
    cores = list(range(8))
    rows = {}
    for core in cores:
        b, c = divmod(core, 4)
        rows[core] = np.concatenate([np.arange(g * 128, (g + 1) * 128) for g in _blocks(c)])
    w_memkv = np.asarray(inputs["mem_w_kv"], f32)
    base = []
    for core in cores:
        b, c = divmod(core, 4)
        d = dict(com)
        d["x"] = np.ascontiguousarray(x[b, rows[core]])
        d["pos"] = np.ascontiguousarray(pos[b, rows[core]].reshape(16, 128).T)
        d["mem"] = mem[b]
        d["w_memkv"] = w_memkv
        d["mk"] = _masks(c)
        d["w_aout"] = np.asarray(inputs["a_w_out"][0], f32)
        d["w_kv"] = np.asarray(inputs["w_kv_shared"], f32)
        d["w_bin"] = np.asarray(inputs["b_w_in"][0], f32)
        d["w_bout"] = np.asarray(inputs["b_w_out"][0], f32)
        base.append(d)

    def run(phases, extra):
        nc, ins_names, outs_names = _get(phases, False)
        maps = []
        for core in cores:
            m = {}
            for nme in ins_names:
                if nme in extra[core]:
                    m["d_" + nme] = extra[core][nme]
                else:
                    m["d_" + nme] = base[core][nme]
            maps.append(m)
        res = run_bass_kernel_spmd(nc, maps, core_ids=cores)
        return [{k[2:]: v for k, v in r.items()} for r in res.results]

    r1 = run([1], [dict() for _ in cores])
    ex = []
    for core in cores:
        b = core // 4
        e = {"kall0": _gather([r["kown0"] for r in r1], b), "vall0": _gather([r["vown0"] for r in r1], b)}
        for nm in ("QT", "gateT", "qmT", "kmpad", "vmpad"):
            e["st_" + nm] = r1[core]["so_" + nm]
        ex.append(e)
    r2 = run([2], ex)
    ex3 = []
    for core in cores:
        b = core // 4
        e = {"kall1": _gather([r["kown1"] for r in r2], b), "vall1": _gather([r["vown1"] for r in r2], b),
             "x1": r2[core]["x1"]}
        for nm in ("QT", "gateT", "qmT"):
            e["st_" + nm] = r2[core]["so_" + nm]
        for nm in ("kmpad", "vmpad"):
            e["st_" + nm] = r1[core]["so_" + nm]
        ex3.append(e)
    r3 = run([3], ex3)
    out = np.zeros((2, 8192, 1024), f32)
    for core in cores:
        b = core // 4
        out[b, rows[core]] = r3[core]["out"]
    return out
```
